# Optimizing a Trainium2 kernel written in Bass

```python
import math
import jax, jax.numpy as jnp
from jax import lax
import numpy as np

D_MODEL = 1024
BATCH = 2
SEQ = 16384
DEPTH = 1
DEC_BATCH = 32
DEC_SEQ = 2048
PAST_LEN = 128

MIX_WIDTH = D_MODEL
MLA_HEADS = 8
QK_NOPE_DIM = 64
QK_ROPE_DIM = 32
V_HEAD_DIM = 64
Q_LORA_RANK = 256
KV_LORA_RANK = 128
KV_A_DIM = KV_LORA_RANK + QK_ROPE_DIM
QK_HEAD_DIM = QK_NOPE_DIM + QK_ROPE_DIM
MLA_WIDTH = MLA_HEADS * V_HEAD_DIM
ROPE_THETA = 10000.0
Q_BLOCK = 128
SSD_HEADS = 8
SSD_HEAD_DIM = 64
SSD_WIDTH = SSD_HEADS * SSD_HEAD_DIM
SSD_GROUPS = 2
HEADS_PER_GROUP = SSD_HEADS // SSD_GROUPS
D_STATE = 128
D_CONV = 5
CHUNK = 128
CONV_DIM = SSD_WIDTH + 2 * SSD_GROUPS * D_STATE
D_IN_PROJ = Q_LORA_RANK + KV_A_DIM + SSD_WIDTH + CONV_DIM + 2 * SSD_HEADS
IN_SPLITS = [Q_LORA_RANK,
             Q_LORA_RANK + KV_A_DIM,
             Q_LORA_RANK + KV_A_DIM + SSD_WIDTH,
             Q_LORA_RANK + KV_A_DIM + SSD_WIDTH + CONV_DIM]
D_FF = -(-8 * D_MODEL // (3 * 256)) * 256
EPS = 1e-6

kernel_name = 'hymba_mla_ssd_bidir_encoder'


def rms_norm(x, g):
    xf = x.astype(jnp.float32)
    y = xf * lax.rsqrt(jnp.mean(xf * xf, axis=-1, keepdims=True) + EPS)
    return (y * g.astype(jnp.float32)).astype(x.dtype)


def rope_tables(seq):
    pos = jnp.arange(seq, dtype=jnp.float32)
    inv = ROPE_THETA ** (-jnp.arange(0, QK_ROPE_DIM, 2, dtype=jnp.float32) / QK_ROPE_DIM)
    ang = pos[:, None] * inv[None, :]
    return jnp.cos(ang), jnp.sin(ang)


def apply_rope(x, cos, sin):
    x1, x2 = jnp.split(x.astype(jnp.float32), 2, axis=-1)
    c = cos[None, :, None, :]
    s = sin[None, :, None, :]
    return jnp.concatenate([x1 * c - x2 * s, x1 * s + x2 * c], axis=-1).astype(x.dtype)


def mla_mixer(q_lat, kv_lat, q_a_norm_g, w_q_b, kv_a_norm_g, w_kv_b):
    b, s, _ = q_lat.shape
    q = (rms_norm(q_lat, q_a_norm_g) @ w_q_b).reshape(b, s, MLA_HEADS, QK_HEAD_DIM)
    q_nope, q_rope = jnp.split(q, [QK_NOPE_DIM], axis=-1)
    c_kv, k_rope = jnp.split(kv_lat, [KV_LORA_RANK], axis=-1)
    kv = (rms_norm(c_kv, kv_a_norm_g) @ w_kv_b).reshape(b, s, MLA_HEADS, QK_NOPE_DIM + V_HEAD_DIM)
    k_nope, v = jnp.split(kv, [QK_NOPE_DIM], axis=-1)
    cos, sin = rope_tables(s)
    q_rope = apply_rope(q_rope, cos, sin)
    k_rope = apply_rope(k_rope[:, :, None, :], cos, sin)[:, :, 0, :]
    scale = QK_HEAD_DIM ** -0.5
    nb = s // Q_BLOCK

    def to_blocks(t):
        return t.reshape(b, nb, Q_BLOCK, *t.shape[2:]).swapaxes(0, 1)

    def attend(blk):
        qn, qr = blk
        logits = (jnp.einsum('bqhd,bkhd->bhqk', qn, k_nope, preferred_element_type=jnp.float32)
                  + jnp.einsum('bqhr,bkr->bhqk', qr, k_rope, preferred_element_type=jnp.float32)) * scale
        p = jax.nn.softmax(logits, axis=-1).astype(v.dtype)
        return jnp.einsum('bhqk,bkhd->bqhd', p, v)

    o = lax.map(attend, (to_blocks(q_nope), to_blocks(q_rope)))
    return o.swapaxes(0, 1).reshape(b, s, MLA_WIDTH)


def decay_matrix(a_cs):
    L = a_cs.shape[-1]
    diff = a_cs[..., :, None] - a_cs[..., None, :]
    mask = jnp.tril(jnp.ones((L, L), dtype=bool))
    return jnp.exp(jnp.where(mask, diff, -jnp.inf))


def ssd_scan(x, dt, a, bmat, cmat):
    bsz, l, h, p = x.shape
    n = bmat.shape[-1]
    c = l // CHUNK
    xdt = (x * dt[..., None]).reshape(bsz, c, CHUNK, h, p)
    a_cs = jnp.cumsum((dt * a).reshape(bsz, c, CHUNK, h).transpose(0, 1, 3, 2), axis=-1)
    bc = bmat.reshape(bsz, c, CHUNK, h, n)
    cc = cmat.reshape(bsz, c, CHUNK, h, n)
    scores = jnp.einsum('bclhn,bcshn->bchls', cc, bc) * decay_matrix(a_cs)
    y_diag = jnp.einsum('bchls,bcshp->bclhp', scores, xdt)
    decay_to_end = jnp.exp(a_cs[..., -1:] - a_cs).transpose(0, 1, 3, 2)[..., None]
    chunk_states = jnp.einsum('bclhn,bclhp->bchpn', bc * decay_to_end, xdt)
    chunk_decay = jnp.exp(a_cs[..., -1])

    def step(state, inp):
        st, dec = inp
        return state * dec[..., None, None] + st, state

    init = jnp.zeros((bsz, h, p, n), jnp.float32)
    _, states_in = lax.scan(step, init, (chunk_states.swapaxes(0, 1), chunk_decay.swapaxes(0, 1)))
    states_in = states_in.swapaxes(0, 1)
    decay_from_start = jnp.exp(a_cs).transpose(0, 1, 3, 2)[..., None]
    y_off = jnp.einsum('bclhn,bchpn->bclhp', cc * decay_from_start, states_in)
    return (y_diag + y_off).reshape(bsz, l, h, p)


def ssd_mixer(z, xbc, dt_raw, conv_w, conv_b, dt_bias, a_log, d_skip, ssd_norm_g):
    b, s, _ = xbc.shape
    pad = D_CONV // 2
    xbc = lax.conv_general_dilated(xbc, conv_w[:, None, :], window_strides=(1,),
                                   padding=[(pad, pad)], dimension_numbers=('NWC', 'WIO', 'NWC'),
                                   feature_group_count=CONV_DIM) + conv_b
    xbc = jax.nn.silu(xbc.astype(jnp.float32))
    xs, bm, cm = jnp.split(xbc, [SSD_WIDTH, SSD_WIDTH + SSD_GROUPS * D_STATE], axis=-1)
    xs = xs.reshape(b, s, SSD_HEADS, SSD_HEAD_DIM)
    bm = jnp.repeat(bm.reshape(b, s, SSD_GROUPS, D_STATE), HEADS_PER_GROUP, axis=2)
    cm = jnp.repeat(cm.reshape(b, s, SSD_GROUPS, D_STATE), HEADS_PER_GROUP, axis=2)
    dt = jax.nn.softplus(dt_raw.astype(jnp.float32).reshape(b, s, 2, SSD_HEADS)
                         + dt_bias.astype(jnp.float32))
    a = -jnp.exp(a_log.astype(jnp.float32))
    y_fwd = ssd_scan(xs, dt[:, :, 0], a[0], bm, cm)
    flip = lambda t: t[:, ::-1]
    y_bwd = flip(ssd_scan(flip(xs), flip(dt[:, :, 1]), a[1], flip(bm), flip(cm)))
    y = y_fwd + y_bwd + xs * d_skip.astype(jnp.float32)[:, None]
    y = y.reshape(b, s, SSD_WIDTH) * jax.nn.silu(z.astype(jnp.float32))
    return rms_norm(y, ssd_norm_g).astype(z.dtype)


def encoder_layer(x, attn_norm_g, w_in, q_a_norm_g, w_q_b, kv_a_norm_g, w_kv_b,
                  conv_w, conv_b, dt_bias, a_log, d_skip, ssd_norm_g, attn_out_norm_g,
                  w_out, ffn_norm_g, w_gate, w_up, w_down):
    h = rms_norm(x, attn_norm_g)
    q_lat, kv_lat, z, xbc, dt_raw = jnp.split(h @ w_in, IN_SPLITS, axis=-1)
    mla_out = rms_norm(mla_mixer(q_lat, kv_lat, q_a_norm_g, w_q_b, kv_a_norm_g, w_kv_b),
                       attn_out_norm_g)
    ssd_out = ssd_mixer(z, xbc, dt_raw, conv_w, conv_b, dt_bias, a_log, d_skip, ssd_norm_g)
    x = x + jnp.concatenate([mla_out, ssd_out.astype(mla_out.dtype)], axis=-1) @ w_out
    h = rms_norm(x, ffn_norm_g)
    return x + (jax.nn.silu(h @ w_gate) * (h @ w_up)) @ w_down


def encoder_trunk(x, layer_params, final_norm_g):
    for l in range(DEPTH):
        x = encoder_layer(x, *[p[l] for p in layer_params])
    return rms_norm(x, final_norm_g)


def setup_inputs(seed: int = 0) -> dict:
    key = jax.random.key(seed)
    ks = jax.random.split(key, 24)
    f32 = jnp.float32

    def dense(k, fan_in, fan_out):
        return jax.random.normal(k, (DEPTH, fan_in, fan_out), f32) * fan_in ** -0.5

    def gain(k, n):
        return 1.0 + 0.01 * jax.random.normal(k, (DEPTH, n), f32)

    dt0 = jnp.exp(jax.random.uniform(ks[10], (DEPTH, 2, SSD_HEADS), f32,
                                     math.log(1e-3), math.log(1e-1)))
    return {
        'x_prompt': jax.random.normal(ks[0], (BATCH, SEQ, D_MODEL), f32),
        'x_sample': jax.random.normal(ks[1], (DEC_BATCH, DEC_SEQ, D_MODEL), f32),
        'attn_norm_g': gain(ks[2], D_MODEL),
        'w_in': dense(ks[3], D_MODEL, D_IN_PROJ),
        'q_a_norm_g': gain(ks[4], Q_LORA_RANK),
        'w_q_b': dense(ks[5], Q_LORA_RANK, MLA_HEADS * QK_HEAD_DIM),
        'kv_a_norm_g': gain(ks[6], KV_LORA_RANK),
        'w_kv_b': dense(ks[7], KV_LORA_RANK, MLA_HEADS * (QK_NOPE_DIM + V_HEAD_DIM)),
        'conv_w': jax.random.normal(ks[8], (DEPTH, D_CONV, CONV_DIM), f32) * D_CONV ** -0.5,
        'conv_b': 0.01 * jax.random.normal(ks[9], (DEPTH, CONV_DIM), f32),
        'dt_bias': dt0 + jnp.log(-jnp.expm1(-dt0)),
        'a_log': jnp.log(jax.random.uniform(ks[11], (DEPTH, 2, SSD_HEADS), f32, 1.0, 16.0)),
        'd_skip': gain(ks[12], SSD_HEADS),
        'ssd_norm_g': gain(ks[13], SSD_WIDTH),
        'attn_out_norm_g': gain(ks[14], MLA_WIDTH),
        'w_out': dense(ks[15], MIX_WIDTH, D_MODEL),
        'ffn_norm_g': gain(ks[16], D_MODEL),
        'w_gate': dense(ks[17], D_MODEL, D_FF),
        'w_up': dense(ks[18], D_MODEL, D_FF),
        'w_down': dense(ks[19], D_FF, D_MODEL),
        'final_norm_g': 1.0 + 0.01 * jax.random.normal(ks[20], (D_MODEL,), f32),
    }


def reference(x_prompt, x_sample, attn_norm_g, w_in, q_a_norm_g, w_q_b, kv_a_norm_g, w_kv_b,
              conv_w, conv_b, dt_bias, a_log, d_skip, ssd_norm_g, attn_out_norm_g, w_out,
              ffn_norm_g, w_gate, w_up, w_down, final_norm_g):
    layer_params = (attn_norm_g, w_in, q_a_norm_g, w_q_b, kv_a_norm_g, w_kv_b,
                    conv_w, conv_b, dt_bias, a_log, d_skip, ssd_norm_g, attn_out_norm_g,
                    w_out, ffn_norm_g, w_gate, w_up, w_down)
    y_prompt = encoder_trunk(x_prompt, layer_params, final_norm_g)
    y_sample = encoder_trunk(x_sample, layer_params, final_norm_g)
    return (y_prompt, y_sample)
```

```python
import numpy as np
import concourse.bass as bass
import concourse.mybir as mybir

F32 = mybir.dt.float32
BF16 = mybir.dt.bfloat16
AF = mybir.ActivationFunctionType
ALU = mybir.AluOpType
AX = mybir.AxisListType


class Stream:
    def __init__(self, sem):
        self.sem = sem
        self.val = 0


class Prog:
    ENG = ("pe", "act", "dve", "pool")

    def __init__(self, nc, es):
        self.nc = nc
        self.es = es
        self.eng = {"pe": nc.tensor, "act": nc.scalar, "dve": nc.vector,
                    "pool": nc.gpsimd, "sp": nc.sync}
        self.sem = {e: es.enter_context(nc.semaphore("c_" + e)) for e in self.ENG}
        self.cnt = {e: 0 for e in self.ENG}
        self.waited = {e: {} for e in ("pe", "act", "dve", "pool", "sp")}
        self.semobj = {"c_" + e: self.sem[e] for e in self.ENG}
        self.tab = {}
        self.streams = []
        self._snames = {}
        self.nops = 0

    NDS = 24

    def stream(self, name):
        return None

    def _dma_sem(self):
        if not self.streams:
            for i in range(self.NDS):
                st = Stream(self.es.enter_context(self.nc.semaphore("d_%d" % i)))
                st.name = "d_%d" % i
                self.semobj[st.name] = st.sem
                self.streams.append(st)
            self._dsi = 0
        st = self.streams[self._dsi % self.NDS]
        self._dsi += 1
        return st

    def _entries(self, buf, key):
        t = self.tab.setdefault(buf, {})
        if key is None:
            return list(t.values())
        out = []
        if key in t:
            out.append(t[key])
        if None in t:
            out.append(t[None])
        return out

    def _deps(self, eng, reads, writes):
        deps = []
        for ap, key in reads:
            for ent in self._entries(ap.tensor.name, key):
                if ent[0] is not None:
                    deps.append((ent[0], True))
        for ap, key in writes:
            for ent in self._entries(ap.tensor.name, key):
                if ent[0] is not None:
                    deps.append((ent[0], False))
                for tok in ent[1].values():
                    deps.append((tok, False))
        return deps

    def _record(self, eng, tok, reads, writes):
        for ap, key in reads:
            t = self.tab.setdefault(ap.tensor.name, {})
            ent = t.setdefault(key, [None, {}])
            ent[1][eng] = tok
        for ap, key in writes:
            t = self.tab.setdefault(ap.tensor.name, {})
            if key is None:
                t.clear()
                t[None] = [tok, {}]
            else:
                t[key] = [tok, {}]

    def _norm(self, lst):
        out = []
        for x in lst:
            if isinstance(x, tuple):
                out.append(x)
            else:
                out.append((x, None))
        return out

    def _do_waits(self, eng, deps):
        need = {}
        for (semname, val, peng), raw in deps:
            if peng == eng:
                if eng not in ("act", "dve", "pool"):
                    continue
            if need.get(semname, 0) < val:
                need[semname] = val
        w = self.waited[eng]
        e = self.eng[eng]
        for semname, val in need.items():
            if w.get(semname, 0) >= val:
                continue
            w[semname] = val
            e.wait_ge(self.semobj[semname], val)

    def op(self, eng, fn, reads=(), writes=()):
        reads = self._norm(reads)
        writes = self._norm(writes)
        deps = self._deps(eng, reads, writes)
        self._do_waits(eng, deps)
        ins = fn(self.eng[eng])
        self.cnt[eng] += 1
        ins.then_inc(self.sem[eng], 1)
        tok = ("c_" + eng, self.cnt[eng], eng)
        self._record(eng, tok, reads, writes)
        self.nops += 1
        return tok

    def dma(self, out, in_, stream=None, rk=None, wk=None, q="sp", **kw):
        reads = [(in_, rk)]
        writes = [(out, wk)]
        st = self._dma_sem()
        deps = self._deps(q, reads, writes)
        if st.val:
            deps.append(((st.name, st.val, "dma"), False))
        self._do_waits(q, deps)
        ins = self.eng[q].dma_start(out=out, in_=in_, **kw)
        st.val += 16
        ins.then_inc(st.sem, 16)
        tok = (st.name, st.val, "dma")
        self._record("dma:" + st.name, tok, reads, writes)
        self.nops += 1
        return tok

    def finish(self):
        sp = self.eng["sp"]
        for s in self.streams:
            if s.val:
                sp.wait_ge(s.sem, s.val)
        for e in self.ENG:
            if self.cnt[e]:
                sp.wait_ge(self.sem[e], self.cnt[e])

    def barrier(self):
        for e in ("pe", "act", "dve", "pool", "sp"):
            h = self.eng[e]
            w = self.waited[e]
            for p in self.ENG:
                if p == e or self.cnt[p] == 0:
                    continue
                nm = "c_" + p
                if w.get(nm, 0) < self.cnt[p]:
                    w[nm] = self.cnt[p]
                    h.wait_ge(self.sem[p], self.cnt[p])
            for s in self.streams:
                if s.val and w.get(s.name, 0) < s.val:
                    w[s.name] = s.val
                    h.wait_ge(s.sem, s.val)
        self.tab.clear()

    def mm(self, out, lhsT, rhs, start=True, stop=True, ko=None, kl=None, kr=None, **kw):
        return self.op("pe", lambda e: e.matmul(out, lhsT, rhs, start=start, stop=stop, **kw),
                       reads=[(lhsT, kl), (rhs, kr)], writes=[(out, ko)])

    def tr(self, out, in_, ident, ko=None, ki=None):
        return self.op("pe", lambda e: e.transpose(out, in_, ident),
                       reads=[(in_, ki), (ident, None)], writes=[(out, ko)])

    def act(self, out, in_, func, bias=None, scale=None, accum_out=None, ko=None, ki=None,
            eng="act", extra_reads=()):
        kw = {}
        reads = [(in_, ki)] + list(extra_reads)
        if bias is not None:
            kw["bias"] = bias
            if not isinstance(bias, (int, float)):
                reads.append((bias, None))
        if scale is not None:
            kw["scale"] = scale
            if not isinstance(scale, (int, float)):
                reads.append((scale, None))
        writes = [(out, ko)]
        if accum_out is not None:
            kw["accum_out"] = accum_out
            writes.append((accum_out, None))
        return self.op("act", lambda e: e.activation(out, in_, func, **kw), reads=reads, writes=writes)

    def tt(self, out, in0, in1, op, eng="dve", ko=None, k0=None, k1=None):
        return self.op(eng, lambda e: e.tensor_tensor(out, in0, in1, op),
                       reads=[(in0, k0), (in1, k1)], writes=[(out, ko)])

    def ts(self, out, in0, s1, s2=None, op0=ALU.mult, op1=None, eng="dve", ko=None, k0=None,
           accum_out=None):
        reads = [(in0, k0)]
        if not isinstance(s1, (int, float)):
            reads.append((s1, None))
        if s2 is not None and not isinstance(s2, (int, float)):
            reads.append((s2, None))
        kw = {}
        if op1 is not None:
            kw["op1"] = op1
        writes = [(out, ko)]
        if accum_out is not None:
            kw["accum_out"] = accum_out
            writes.append((accum_out, None))
        return self.op(eng, lambda e: e.tensor_scalar(out, in0, s1, s2, op0, **kw),
                       reads=reads, writes=writes)

    def stt(self, out, in0, scalar, in1, op0, op1, eng="dve", ko=None, k0=None, k1=None):
        reads = [(in0, k0), (in1, k1)]
        if not isinstance(scalar, (int, float)):
            reads.append((scalar, None))
        return self.op(eng, lambda e: e.scalar_tensor_tensor(out, in0, scalar, in1, op0, op1),
                       reads=reads, writes=[(out, ko)])

    def copy(self, out, in_, eng="dve", ko=None, ki=None):
        if eng == "act":
            return self.act(out, in_, AF.Copy, ko=ko, ki=ki)
        return self.op(eng, lambda e: e.tensor_copy(out, in_), reads=[(in_, ki)], writes=[(out, ko)])

    def memset(self, ap, val, eng="dve", k=None):
        return self.op(eng, lambda e: e.memset(ap, val), writes=[(ap, k)])

    def recip(self, out, in_, ko=None, ki=None):
        return self.op("dve", lambda e: e.reciprocal(out, in_), reads=[(in_, ki)], writes=[(out, ko)])
from concourse.bass_utils import run_bass_kernel_spmd
from contextlib import ExitStack

EPS = 1e-6
D = 1024
KD = 8
TT = 512
HALO = 2
TW = TT + 2 * HALO
QSCALE = 96 ** -0.5


class Cfg:
    def __init__(self, lp_own=4096, nnon=24, ls=2048, ns=4, dff=2816):
        self.lp_own, self.nnon, self.ls, self.ns, self.dff = lp_own, nnon, ls, ns, dff
        self.nfb = dff // 128
        self.ng = dff // 256
        self.lmax = max(lp_own, ls)


def build(cfg):
    nc = bass.Bass("TRN2", target_bir_lowering=False)
    c = cfg
    NFB, NG = c.nfb, c.ng

    def din(name, shape, dt=F32):
        return nc.dram_tensor(name, list(shape), dt, kind="ExternalInput").ap()

    xo = din("xo", [D, max(c.lp_own, TT) + 4])
    xn = din("xn", [max(c.nnon, 1), D, TW])
    mk = din("mk", [128, max(c.nnon, 1) * 2])
    rpo = din("rpo", [2, 32, max(c.lp_own, TT)])
    rpn = din("rpn", [max(c.nnon, 1), 2, 32, TT])
    rs = din("rs", [2, 32, c.ls])
    xs = din("xs", [max(c.ns, 1), D, c.ls + 4])
    w_in = din("w_in", [D, 1968])
    w_q_b = din("w_q_b", [256, 768])
    w_kv_b = din("w_kv_b", [128, 1024])
    w_out = din("w_out", [D, D])
    w_gate = din("w_gate", [D, c.dff])
    w_up = din("w_up", [D, c.dff])
    w_down = din("w_down", [c.dff, D])
    gv = din("gv", [128, 88])
    rowv = din("rowv", [1, 40])
    cm = din("cm", [128, 6, 128])
    yo = nc.dram_tensor("yo", [D, max(c.lp_own, TT)], F32, kind="ExternalOutput").ap()
    ys = nc.dram_tensor("ys", [max(c.ns, 1), D, c.ls], F32, kind="ExternalOutput").ap()
    wg_s = nc.dram_tensor("wg_s", [NG, 128, KD, 256], BF16, kind="ExternalOutput").ap()
    wu_s = nc.dram_tensor("wu_s", [NG, 128, KD, 256], BF16, kind="ExternalOutput").ap()
    wd_s = nc.dram_tensor("wd_s", [8, 128, NFB, 128], BF16, kind="ExternalOutput").ap()

    with ExitStack() as es:
        P = Prog(nc, es)

        uid = {"n": 0}

        def sbuf(stack, name, shape, dt=F32):
            uid["n"] += 1
            return stack.enter_context(nc.sbuf_tensor("%s_%d" % (name, uid["n"]), list(shape), dt))

        pb = [es.enter_context(nc.psum_tensor(f"pb{i}", [128, 512], F32)) for i in range(8)]
        psrr = {"i": 0}

        def nps(lo=0, hi=8):
            k = "%d_%d" % (lo, hi)
            i = psrr.get(k, lo)
            psrr[k] = lo + ((i - lo + 1) % (hi - lo))
            return pb[i]

        G = {}
        gvt = sbuf(es, "gvt", [128, 88])
        rowt = sbuf(es, "rowt", [128, 40])
        cmt = sbuf(es, "cmt", [128, 6, 128])
        mneg = sbuf(es, "mneg", [128, 2, 128], BF16)
        identf = sbuf(es, "identf", [128, 128])
        identb = sbuf(es, "identb", [128, 128], BF16)
        onesb = sbuf(es, "onesb", [128, 128], BF16)
        onesf = sbuf(es, "onesf", [128, 128])
        abt = sbuf(es, "abt", [128, 16])
        sq2 = [sbuf(es, f"sq{i}", [128, TT], BF16) for i in range(3)]
        lnb = sbuf(es, "lnb", [128, TT])
        rr = {"sq": 0}

        st_c = P.stream("const")
        P.dma(gvt[:], gv[:, :], st_c)
        P.dma(rowt[:], rowv[0:1, :].partition_broadcast(128), st_c)
        P.dma(cmt[:], cm[:, :, :], st_c)
        P.copy(mneg[:], cmt[:, 4:6, :], eng="dve")
        P.memset(identf[:], 1.0, eng="pool")
        P.op("pool", lambda e: e.affine_select(identf[:], identf[:], pattern=[[-1, 128]],
                                               compare_op=ALU.is_equal, fill=0.0, base=0,
                                               channel_multiplier=1),
             reads=[identf[:]], writes=[identf[:]])
        P.copy(identb[:], identf[:], eng="dve")
        P.memset(onesb[:], 1.0, eng="dve")
        P.memset(onesf[:], 1.0, eng="dve")
        P.act(abt[:], rowt[:, 16:32], AF.Exp)
        P.ts(abt[:], abt[:], -1.0, None, op0=ALU.mult)
        G_ATTN, G_FFN, G_FIN, G_QA, G_KVA, G_AO, G_SSD, G_CB, G_CW = 0, 8, 16, 24, 26, 27, 31, 35, 43
        dtb = rowt[:, 0:16]
        dsk = rowt[:, 32:40]

        def rstd_fm(srcs, Dn, N, out_ap):
            ps = nps(0, 6)
            for i, s in enumerate(srcs):
                sq = sq2[rr["sq"] % 3]
                rr["sq"] += 1
                P.act(sq[:, :N], s, AF.Square)
                P.mm(ps[:, :N], onesb[:], sq[:, :N], start=(i == 0), stop=(i == len(srcs) - 1))
            P.act(lnb[:, :N], ps[:, :N], AF.Ln, bias=EPS, scale=1.0 / Dn)
            P.act(out_ap, lnb[:, :N], AF.Exp, scale=-0.5)

        stg = [sbuf(es, f"stg{i}", [128, KD, 128]) for i in range(2)]
        st_w = [P.stream("w0"), P.stream("w1")]
        wrr = {"i": 0}

        def prep_w(dst_fn, src, k_chunks, c0, ncols, gcol, scale=1.0):
            for a in range(0, ncols, 128):
                b = min(a + 128, ncols)
                i = wrr["i"] % 2
                wrr["i"] += 1
                P.dma(stg[i][:, 0:k_chunks, 0:b - a],
                      src[0:k_chunks * 128, c0 + a:c0 + b].rearrange("(k p) n -> p k n", p=128), st_w[i])
                for kc in range(k_chunks):
                    o = dst_fn(kc, a, b)
                    P.ts(o, stg[i][:, kc, 0:b - a], gvt[:, gcol + kc:gcol + kc + 1], None, op0=ALU.mult)

        with ExitStack() as ph:
            wtmp = [sbuf(ph, f"wtmp{i}", [128, KD, 256], BF16) for i in range(2)]
            wdt = [sbuf(ph, f"wdt{i}", [128, NFB, 128], BF16) for i in range(2)]
            wdf = [sbuf(ph, f"wdf{i}", [128, NFB, 128]) for i in range(2)]
            st_o = [P.stream("wo0"), P.stream("wo1")]
            n = 0
            for (src, dst) in ((w_gate, wg_s), (w_up, wu_s)):
                for g in range(NG):
                    t = wtmp[n % 2]
                    prep_w(lambda kc, a, b, t=t: t[:, kc, a:b], src, KD, g * 256, 256, G_FFN)
                    P.dma(dst[g, :, :, :], t[:], st_o[n % 2])
                    n += 1
            for ob in range(8):
                i = ob % 2
                P.dma(wdf[i][:], w_down[:, ob * 128:(ob + 1) * 128].rearrange("(k p) n -> p k n", p=128),
                      st_w[i])
                P.copy(wdt[i][:], wdf[i][:], eng="act" if ob % 2 else "dve")
                P.dma(wd_s[ob, :, :, :], wdt[i][:], st_o[i])
            P.barrier()

        def load_x(dst, src_cols, stream):
            P.dma(dst, src_cols.rearrange("(k p) n -> p k n", p=128), stream)

        def norm_tile(xt, hn, rst, N):
            pieces = [(0, min(N, TT))] + ([(TT, N)] if N > TT else [])
            for (a, b) in pieces:
                rstd_fm([xt[:, kc, a:b] for kc in range(KD)], D, b - a, rst[:, a:b])
            for kc in range(KD):
                P.tt(hn[:, kc, 0:N], xt[:, kc, 0:N], rst[:, 0:N], ALU.mult,
                     eng="pool" if kc % 2 else "dve")

        def mla_phase(own_src, lown, non_src, nnon, rope_own, rope_non):
            nto = lown // TT
            ltot = lown + nnon * TT
            ntt = ltot // TT
            with ExitStack() as ph:
                wqb = sbuf(ph, "wqb", [128, 2, 768], BF16)
                wqr = sbuf(ph, "wqr", [128, 2, 8, 96], BF16)
                wkvb = sbuf(ph, "wkvb", [128, 1024], BF16)
                ckvn = sbuf(ph, "ckvn", [128, ltot], BF16)
                KT = sbuf(ph, "KT", [96, ltot], BF16)
                qlatn = sbuf(ph, "qlatn", [128, 2, lown], BF16)
                rst2 = sbuf(ph, "mrst2", [128, TT])
                rtab = [sbuf(ph, f"rtab{i}", [96, 2, TT]) for i in range(2)]
                t1 = sbuf(ph, "mt1", [96, TT])
                t2 = sbuf(ph, "mt2", [96, TT])
                ph1 = ExitStack()
                w_mla = sbuf(ph1, "w_mla", [128, KD, 576], BF16)
                xt2 = [sbuf(ph1, f"mxt{i}", [128, KD, TT]) for i in range(2)]
                hn = sbuf(ph1, "mhn", [128, KD, TT], BF16)
                rst = sbuf(ph1, "mrst", [128, TT])
                st_x = [P.stream("mx0"), P.stream("mx1")]
                st_r = [P.stream("mr0"), P.stream("mr1")]

                prep_w(lambda kc, a, b: w_mla[:, kc, a:b], w_in, KD, 0, 384, G_ATTN)
                P.memset(w_mla[:, :, 384:576], 0.0, eng="pool")
                prep_w(lambda kc, a, b: w_mla[:, kc, 448 + a:448 + b], w_in, KD, 384, 32, G_ATTN)
                for kc in range(KD):
                    P.ts(w_mla[:, kc, 544:560], w_mla[:, kc, 464:480], -1.0, None, op0=ALU.mult)
                    P.copy(w_mla[:, kc, 560:576], w_mla[:, kc, 448:464], eng="pool")
                prep_w(lambda kc, a, b: wqb[:, kc, a:b], w_q_b, 2, 0, 768, G_QA)
                P.memset(wqr[:], 0.0, eng="pool")
                for kc in range(2):
                    for h in range(8):
                        P.ts(wqr[:, kc, h, 64:80], wqb[:, kc, h * 96 + 80:h * 96 + 96], -1.0, None,
                             op0=ALU.mult)
                        P.copy(wqr[:, kc, h, 80:96], wqb[:, kc, h * 96 + 64:h * 96 + 80], eng="pool")
                prep_w(lambda kc, a, b: wkvb[:, a:b], w_kv_b, 1, 0, 1024, G_KVA)

                def tile_src(t):
                    if t < nto:
                        return own_src[:, HALO + t * TT:HALO + (t + 1) * TT], \
                            rope_own[:, :, t * TT:(t + 1) * TT]
                    s = t - nto
                    return non_src[s, :, HALO:HALO + TT], rope_non[s, :, :, :]

                def issue_load(t):
                    xs_, rp_ = tile_src(t)
                    load_x(xt2[t % 2][:], xs_, st_x[t % 2])
                    P.dma(rtab[t % 2][64:96, :, :], rp_.rearrange("a p n -> p a n"), st_r[t % 2])

                issue_load(0)
                for t in range(ntt):
                    if t + 1 < ntt:
                        issue_load(t + 1)
                    xt = xt2[t % 2]
                    rt = rtab[t % 2]
                    norm_tile(xt, hn, rst, TT)
                    cols = slice(t * TT, (t + 1) * TT)
                    pc = nps(0, 6)
                    for kc in range(KD):
                        P.mm(pc[:], w_mla[:, kc, 256:384], hn[:, kc, :], start=(kc == 0), stop=(kc == KD - 1))
                    rstd_fm([pc[:]], 128, TT, rst2[:])
                    P.tt(ckvn[:, cols], pc[:], rst2[:], ALU.mult, ko=t)
                    pa = nps(0, 6)
                    pr = nps(0, 6)
                    for kc in range(KD):
                        P.mm(pa[0:96, :], w_mla[:, kc, 384:480], hn[:, kc, :], start=(kc == 0), stop=(kc == KD - 1))
                    for kc in range(KD):
                        P.mm(pr[0:96, :], w_mla[:, kc, 480:576], hn[:, kc, :], start=(kc == 0), stop=(kc == KD - 1))
                    P.tt(t1[64:96, :], pa[64:96, :], rt[64:96, 0, :], ALU.mult)
                    P.tt(t2[64:96, :], pr[64:96, :], rt[64:96, 1, :], ALU.mult)
                    P.tt(KT[64:96, cols], t1[64:96, :], t2[64:96, :], ALU.add, ko=("r", t))
                    if t < nto:
                        pq = [nps(0, 6), nps(0, 6)]
                        for cq in range(2):
                            for kc in range(KD):
                                P.mm(pq[cq][:], w_mla[:, kc, cq * 128:(cq + 1) * 128], hn[:, kc, :],
                                     start=(kc == 0), stop=(kc == KD - 1))
                        rstd_fm([pq[0][:], pq[1][:]], 256, TT, rst2[:])
                        for cq in range(2):
                            P.tt(qlatn[:, cq, cols], pq[cq][:], rst2[:], ALU.mult, ko=t)

                P.barrier()
                ph1.close()
                VH = sbuf(ph, "VH", [128, ltot // 128, 65], BF16)
                QH = sbuf(ph, "QH", [96, lown], BF16)
                PT = [sbuf(ph, f"PT{i}", [128, TT], BF16) for i in range(4)]
                osb = sbuf(ph, "osb", [65, TT])
                rc = sbuf(ph, "rc", [65, TT])
                P.memset(VH[:, :, 64:65], 1.0, eng="pool")
                nkb = ltot // 128
                for h in range(8):
                    for t in range(ntt):
                        cols = slice(t * TT, (t + 1) * TT)
                        pk = nps(0, 6)
                        P.mm(pk[0:64, :], wkvb[:, h * 128:h * 128 + 64], ckvn[:, cols], kr=t)
                        P.copy(KT[0:64, cols], pk[0:64, :], eng="act" if t % 2 else "dve", ko=("n", t))
                    for g8 in range(0, nkb, 8):
                        pv = nps(0, 6)
                        for j in range(8):
                            kb = g8 + j
                            P.mm(pv[:, j * 64:(j + 1) * 64], ckvn[:, kb * 128:(kb + 1) * 128],
                                 wkvb[:, h * 128 + 64:h * 128 + 128], kl=kb // 4)
                        P.copy(VH[:, g8:g8 + 8, 0:64], pv[:, :].rearrange("p (j d) -> p j d", j=8),
                               eng="dve" if (g8 // 8) % 2 else "act", ko=g8 // 8)
                    for t in range(nto):
                        cols = slice(t * TT, (t + 1) * TT)
                        rt = rtab[t % 2]
                        P.dma(rt[64:96, :, :], rope_own[:, :, cols].rearrange("a p n -> p a n"), st_r[t % 2])
                        pa = nps(0, 6)
                        pr = nps(0, 6)
                        for kc in range(2):
                            P.mm(pa[0:96, :], wqb[:, kc, h * 96:(h + 1) * 96], qlatn[:, kc, cols],
                                 start=(kc == 0), stop=(kc == 1), kr=t)
                        for kc in range(2):
                            P.mm(pr[0:96, :], wqr[:, kc, h, :], qlatn[:, kc, cols],
                                 start=(kc == 0), stop=(kc == 1), kr=t)
                        P.copy(QH[0:64, cols], pa[0:64, :], eng="act", ko=("n", t))
                        P.tt(t1[64:96, :], pa[64:96, :], rt[64:96, 0, :], ALU.mult)
                        P.tt(t2[64:96, :], pr[64:96, :], rt[64:96, 1, :], ALU.mult)
                        P.tt(QH[64:96, cols], t1[64:96, :], t2[64:96, :], ALU.add, ko=("r", t))
                    for t in range(nto):
                        cols = slice(t * TT, (t + 1) * TT)
                        po = pb[6 + (t % 2)]
                        for kb in range(nkb):
                            ps_ = nps(0, 6)
                            tk = kb // 4
                            P.op("pe", lambda e, ps_=ps_, kb=kb, cols=cols: e.matmul(
                                ps_[:], KT[0:96, kb * 128:(kb + 1) * 128], QH[0:96, cols], start=True, stop=True),
                                reads=[(KT[:], ("n", tk)), (KT[:], ("r", tk)), (QH[:], ("n", t)), (QH[:], ("r", t))],
                                writes=[(ps_[:], None)])
                            pt = PT[kb % 4]
                            P.act(pt[:], ps_[:], AF.Exp, scale=QSCALE)
                            P.mm(po[0:65, :], VH[:, kb, :], pt[:], start=(kb == 0), stop=(kb == nkb - 1),
                                 kl=kb // 8)
                        P.copy(osb[:], po[0:65, :], eng="dve")
                        P.recip(rc[64:65, :], osb[64:65, :])
                        pbc = nps(0, 6)
                        P.mm(pbc[0:64, :], onesf[64:65, 0:64], rc[64:65, :])
                        P.tt(G["mlaT"][(h % 2) * 64:(h % 2) * 64 + 64, h // 2, cols], osb[0:64, :], pbc[0:64, :],
                             ALU.mult, ko=(h, t))
                for t in range(nto):
                    cols = slice(t * TT, (t + 1) * TT)
                    rstd_fm([G["mlaT"][:, cc, cols] for cc in range(4)], 512, TT, rst2[:])
                    for cc in range(4):
                        P.tt(G["mlaT"][:, cc, cols], G["mlaT"][:, cc, cols], rst2[:], ALU.mult,
                             eng="pool" if cc % 2 else "dve")
                P.barrier()

        def ssd_phase(own_src, own_t0, nto, slots, out_t0):
            nch = nto * 4
            with ExitStack() as ph:
                w_ssd = sbuf(ph, "w_ssd", [128, KD, 1552], BF16)
                Sst = [sbuf(ph, f"Sst{d}", [128, 512]) for d in range(2)]
                SBW = sbuf(ph, "SBW", [128, nch, 512], BF16)
                Sfb = sbuf(ph, "Sfb", [128, 512], BF16)
                xt = sbuf(ph, "sxt", [128, KD, TW])
                hn = sbuf(ph, "shn", [128, KD, TW], BF16)
                rst = sbuf(ph, "srst", [128, TW])
                pre = [sbuf(ph, f"pre{i}", [128, TW]) for i in range(2)]
                cacc = [sbuf(ph, f"cacc{i}", [128, TT]) for i in range(2)]
                xact = sbuf(ph, "xact", [128, 8, TT], BF16)
                x_tok = sbuf(ph, "x_tok", [128, 4, 512], BF16)
                B_tok = sbuf(ph, "B_tok", [128, 4, 256], BF16)
                z_tok = sbuf(ph, "z_tok", [128, 4, 512], BF16)
                dt_tok = sbuf(ph, "dt_tok", [128, 4, 16])
                dtm = sbuf(ph, "dtm", [128, 4, 16])
                sm = [sbuf(ph, f"sm{i}", [128, 16]) for i in range(8)]
                xw = sbuf(ph, "xw", [128, 512], BF16)
                xdt = [sbuf(ph, f"xdt{d}", [128, 512], BF16) for d in range(2)]
                GT = sbuf(ph, "GT", [128, 2, 128], BF16)
                Dm = [sbuf(ph, f"Dm{i}", [128, 128], BF16) for i in range(2)]
                Mm = [sbuf(ph, f"Mm{i}", [128, 128], BF16) for i in range(2)]
                ya = sbuf(ph, "ya", [128, 512])
                yb = sbuf(ph, "yb", [128, 512])
                yg = sbuf(ph, "yg", [128, 512])
                junk = sbuf(ph, "junk", [128, 512], BF16)
                stok = sbuf(ph, "stok", [128, 512], BF16)
                st_x = P.stream("sx")
                smr = {"i": 0}

                def smn():
                    smr["i"] += 1
                    return sm[smr["i"] % 8]

                prep_w(lambda kc, a, b: w_ssd[:, kc, a:b], w_in, KD, 416, 1552, G_ATTN)
                P.memset(Sst[0][:], 0.0)
                P.memset(Sst[1][:], 0.0)

                def tile_front(src, full):
                    load_x(xt[:], src, st_x)
                    import os
                    dbg = os.environ.get("SSD_DBG", "z")
                    if dbg == "0":
                        return
                    norm_tile(xt, hn, rst, TW)
                    if dbg == "a":
                        return
                    ncc = 8 if full else 6
                    for cc in range(ncc):
                        pm = nps(0, 6)
                        ph_ = nps(0, 6)
                        col = 512 + cc * 128
                        for kc in range(KD):
                            P.mm(pm[:], w_ssd[:, kc, col:col + 128], hn[:, kc, 0:TT],
                                 start=(kc == 0), stop=(kc == KD - 1))
                        for kc in range(KD):
                            P.mm(ph_[:, 0:4], w_ssd[:, kc, col:col + 128], hn[:, kc, TT:TW],
                                 start=(kc == 0), stop=(kc == KD - 1))
                        pr_ = pre[cc % 2]
                        P.copy(pr_[:, 0:TT], pm[:], eng="act")
                        P.copy(pr_[:, TT:TW], ph_[:, 0:4], eng="act")
                        ca = cacc[cc % 2]
                        P.ts(ca[:], pr_[:, 0:TT], gvt[:, G_CW + cc * 5:G_CW + cc * 5 + 1],
                             gvt[:, G_CB + cc:G_CB + cc + 1], op0=ALU.mult, op1=ALU.add)
                        for k in range(1, 5):
                            P.stt(ca[:], pr_[:, k:k + TT], gvt[:, G_CW + cc * 5 + k:G_CW + cc * 5 + k + 1],
                                  ca[:], ALU.mult, ALU.add)
                        P.act(xact[:, cc, :], ca[:], AF.Silu)
                    if dbg == "b":
                        return
                    for tb in range(4):
                        pd = nps(0, 6)
                        for kc in range(KD):
                            P.mm(pd[:, 0:16], hn[:, kc, HALO + tb * 128:HALO + (tb + 1) * 128],
                                 w_ssd[:, kc, 1536:1552], start=(kc == 0), stop=(kc == KD - 1))
                        v = smn()
                        P.tt(v[:], pd[:, 0:16], dtb, ALU.add)
                        a_ = smn()
                        P.act(a_[:], v[:], AF.Abs)
                        e_ = smn()
                        P.act(e_[:], a_[:], AF.Exp, scale=-1.0)
                        l_ = smn()
                        P.act(l_[:], e_[:], AF.Ln, bias=1.0)
                        P.ts(v[:], v[:], 0.0, None, op0=ALU.max)
                        P.tt(dt_tok[:, tb, :], v[:], l_[:], ALU.add, ko=tb)
                    if full:
                        for tb in range(4):
                            pz = nps(0, 6)
                            for kc in range(KD):
                                P.mm(pz[:], hn[:, kc, HALO + tb * 128:HALO + (tb + 1) * 128],
                                     w_ssd[:, kc, 0:512], start=(kc == 0), stop=(kc == KD - 1))
                            P.act(z_tok[:, tb, :], pz[:], AF.Silu, ko=tb)
                    if dbg == "c":
                        return
                    for tb in range(4):
                        pt_ = nps(0, 6)
                        pt2 = nps(0, 6)
                        for cc in range(4):
                            P.mm(pt_[:, cc * 128:(cc + 1) * 128], xact[:, cc, tb * 128:(tb + 1) * 128], identb[:])
                        for cc in range(2):
                            P.mm(pt2[:, cc * 128:(cc + 1) * 128], xact[:, 4 + cc, tb * 128:(tb + 1) * 128],
                                 identb[:])
                        P.copy(x_tok[:, tb, :], pt_[:, 0:512], eng="dve", ko=tb)
                        P.copy(B_tok[:, tb, :], pt2[:, 0:256], eng="act", ko=tb)

                def bc8(ap8):
                    return ap8.unsqueeze(2).to_broadcast([128, 8, 64])

                def v8(ap512):
                    return ap512.rearrange("p (h d) -> p h d", h=8)

                def state_update(d, tb, dt8, save=None):
                    dtA = smn()
                    P.tt(dtA[:, 0:8], dt8, abt[:, d * 8:(d + 1) * 8], ALU.mult)
                    pp = nps(0, 6)
                    P.mm(pp[:, 0:8], cmt[:, 2 + d, :], dtA[:, 0:8])
                    P.mm(pp[:, 8:16], onesf[:], dtA[:, 0:8])
                    ex = smn()
                    P.act(ex[:], pp[:, 0:16], AF.Exp)
                    w_ = smn()
                    P.tt(w_[:, 0:8], dt8, ex[:, 0:8], ALU.mult)
                    P.tt(v8(xw[:]), v8(x_tok[:, tb, :]), bc8(w_[:, 0:8]), ALU.mult, k0=tb)
                    pc = nps(0, 6)
                    for g in range(2):
                        P.mm(pc[:, g * 256:(g + 1) * 256], B_tok[:, tb, g * 128:(g + 1) * 128],
                             xw[:, g * 256:(g + 1) * 256], kl=tb)
                    S = Sst[d]
                    if save is not None:
                        P.copy(save, S[:], eng="act")
                    P.tt(v8(S[:]), v8(S[:]), bc8(ex[:, 8:16]), ALU.mult)
                    P.tt(S[:], S[:], pc[:], ALU.add)

                def full_chunk(tb, ci, out_cols):
                    pg = nps(0, 6)
                    for g in range(2):
                        P.mm(pg[:, g * 128:(g + 1) * 128], xact[:, 4 + g, tb * 128:(tb + 1) * 128],
                             xact[:, 6 + g, tb * 128:(tb + 1) * 128])
                    P.copy(GT[:], pg[:, 0:256].rearrange("p (g l) -> p g l", g=2), eng="act")
                    P.copy(Sfb[:], Sst[0][:], eng="act")
                    for d in range(2):
                        dt8 = dt_tok[:, tb, d * 8:(d + 1) * 8]
                        dtA = smn()
                        P.tt(dtA[:, 0:8], dt8, abt[:, d * 8:(d + 1) * 8], ALU.mult, k0=tb)
                        pcs = nps(0, 6)
                        P.mm(pcs[:, 0:8], cmt[:, d, :], dtA[:, 0:8])
                        et = smn()
                        P.act(et[:, 0:8], pcs[:, 0:8], AF.Exp)
                        ncs = smn()
                        P.ts(ncs[:, 0:8], pcs[:, 0:8], -1.0, None, op0=ALU.mult)
                        P.tt(v8(xdt[d][:]), v8(x_tok[:, tb, :]), bc8(dt8), ALU.mult, k0=tb, k1=tb)
                        poff = nps(0, 6)
                        Sin = Sfb if d == 0 else None
                        for g in range(2):
                            rhs = Sfb[:, g * 256:(g + 1) * 256] if d == 0 else SBW[:, ci, g * 256:(g + 1) * 256]
                            P.mm(poff[:, g * 256:(g + 1) * 256], xact[:, 6 + g, tb * 128:(tb + 1) * 128], rhs,
                                 kr=(None if d == 0 else ci))
                        tgt = ya if d == 0 else yb
                        P.tt(v8(tgt[:]), v8(poff[:]), bc8(et[:, 0:8]), ALU.mult)
                        pdg = nps(6, 8)
                        for h in range(8):
                            g = h // 4
                            pcb = nps(0, 6)
                            P.mm(pcb[:, 0:128], dtA[:, h:h + 1].to_broadcast([128, 128]), cmt[:, d, :],
                                 start=True, stop=False)
                            P.mm(pcb[:, 0:128], identb[:], mneg[:, d, :], start=False, stop=True)
                            Dd = Dm[h % 2]
                            P.act(Dd[:], pcb[:, 0:128], AF.Exp, bias=ncs[:, h:h + 1])
                            Md = Mm[h % 2]
                            P.tt(Md[:], Dd[:], GT[:, g, :], ALU.mult)
                            P.mm(pdg[:, h * 64:(h + 1) * 64], Md[:], xdt[d][:, h * 64:(h + 1) * 64])
                        P.tt(tgt[:], tgt[:], pdg[:], ALU.add)
                    P.tt(ya[:], ya[:], yb[:], ALU.add)
                    P.tt(v8(yb[:]), v8(x_tok[:, tb, :]), bc8(dsk), ALU.mult, k0=tb)
                    P.tt(ya[:], ya[:], yb[:], ALU.add)
                    P.tt(yg[:], ya[:], z_tok[:, tb, :], ALU.mult, k1=tb)
                    ss = smn()
                    P.memset(ss[:, 0:1], 0.0)
                    P.act(junk[:], yg[:], AF.Square, accum_out=ss[:, 0:1])
                    l2 = smn()
                    P.act(l2[:, 0:1], ss[:, 0:1], AF.Ln, bias=EPS, scale=1.0 / 512)
                    r2 = smn()
                    P.act(r2[:, 0:1], l2[:, 0:1], AF.Exp, scale=-0.5)
                    P.ts(stok[:], yg[:], r2[:, 0:1], None, op0=ALU.mult)
                    pt_ = nps(0, 6)
                    for cc in range(4):
                        P.mm(pt_[:, cc * 128:(cc + 1) * 128], stok[:, cc * 128:(cc + 1) * 128], identb[:])
                    P.copy(G["ssdT"][:, :, out_cols], pt_[:, 0:512].rearrange("p (c n) -> p c n", c=4), eng="dve",
                           ko=ci)

                sstop = getattr(c, "stop", 0)
                for (src, midx, fon, bon) in slots:
                    tile_front(src, False)
                    if midx is not None:
                        for tb in range(4):
                            P.tt(dtm[:, tb, :].rearrange("p (d h) -> p d h", d=2),
                                 dt_tok[:, tb, :].rearrange("p (d h) -> p d h", d=2),
                                 mkt[:, midx * 2:midx * 2 + 2].unsqueeze(2).to_broadcast([128, 2, 8]),
                                 ALU.mult, k0=tb)
                        dsrc = dtm
                    else:
                        dsrc = dt_tok
                    if fon:
                        for tb in range(4):
                            state_update(0, tb, dsrc[:, tb, 0:8])
                    if bon:
                        for tb in (3, 2, 1, 0):
                            state_update(1, tb, dsrc[:, tb, 8:16])
                for t in range(nto - 1, -1, -1):
                    g0 = (own_t0 + t) * TT
                    if sstop == 31:
                        break
                    tile_front(own_src[:, g0:g0 + TW], False)
                    if sstop == 32:
                        break
                    for tb in (3, 2, 1, 0):
                        state_update(1, tb, dt_tok[:, tb, 8:16], save=SBW[:, t * 4 + tb, :])
                for t in range(nto):
                    g0 = (own_t0 + t) * TT
                    if sstop in (31, 32, 33):
                        break
                    tile_front(own_src[:, g0:g0 + TW], True)
                    if sstop == 34:
                        break
                    for tb in range(4):
                        oc = (out_t0 + t) * TT + tb * 128
                        full_chunk(tb, t * 4 + tb, slice(oc, oc + 128))
                        state_update(0, tb, dt_tok[:, tb, 0:8])
                P.barrier()

        def ffn_phase(own_src, lown, out_dst):
            nto = lown // TT
            with ExitStack() as ph:
                wout = sbuf(ph, "wout", [128, KD, D], BF16)
                xt2 = [sbuf(ph, f"fxt{i}", [128, KD, TT]) for i in range(2)]
                h2 = sbuf(ph, "h2", [128, KD, TT], BF16)
                rst = sbuf(ph, "frst", [128, TT])
                actT = sbuf(ph, "actT", [128, NFB, TT], BF16)
                sg = [sbuf(ph, f"sg{i}", [128, TT]) for i in range(2)]
                wg = [sbuf(ph, f"wg{i}", [128, KD, 256], BF16) for i in range(2)]
                wu = [sbuf(ph, f"wu{i}", [128, KD, 256], BF16) for i in range(2)]
                wd = [sbuf(ph, f"wd{i}", [128, NFB, 128], BF16) for i in range(2)]
                st_x = [P.stream("fx0"), P.stream("fx1")]
                st_g = [P.stream("fg0"), P.stream("fg1")]
                st_d = [P.stream("fd0"), P.stream("fd1")]
                st_y = [P.stream("fy0"), P.stream("fy1")]
                prep_w(lambda kc, a, b: wout[:, kc, a:b], w_out, 4, 0, D, G_AO)
                prep_w(lambda kc, a, b: wout[:, 4 + kc, a:b], w_out[512:1024, :], 4, 0, D, G_SSD)
                load_x(xt2[0][:], own_src[:, HALO:HALO + TT], st_x[0])
                for t in range(nto):
                    if t + 1 < nto:
                        load_x(xt2[(t + 1) % 2][:], own_src[:, HALO + (t + 1) * TT:HALO + (t + 2) * TT],
                               st_x[(t + 1) % 2])
                    xt = xt2[t % 2]
                    cols = slice(t * TT, (t + 1) * TT)
                    for ob in range(8):
                        p_ = nps(0, 8)
                        for kc in range(8):
                            rhs = G["mlaT"][:, kc, cols] if kc < 4 else G["ssdT"][:, kc - 4, cols]
                            P.mm(p_[:], wout[:, kc, ob * 128:(ob + 1) * 128], rhs, start=(kc == 0), stop=(kc == 7))
                        P.tt(xt[:, ob, :], p_[:], xt[:, ob, :], ALU.add)
                    norm_tile(xt, h2, rst, TT)
                    for g in range(NG):
                        i = g % 2
                        P.dma(wg[i][:], wg_s[g, :, :, :], st_g[i])
                        P.dma(wu[i][:], wu_s[g, :, :, :], st_g[i])
                        for j in range(2):
                            fb = g * 2 + j
                            pgt = nps(0, 8)
                            put = nps(0, 8)
                            for kc in range(KD):
                                P.mm(pgt[:], wg[i][:, kc, j * 128:(j + 1) * 128], h2[:, kc, :],
                                     start=(kc == 0), stop=(kc == KD - 1))
                            for kc in range(KD):
                                P.mm(put[:], wu[i][:, kc, j * 128:(j + 1) * 128], h2[:, kc, :],
                                     start=(kc == 0), stop=(kc == KD - 1))
                            s_ = sg[fb % 2]
                            P.act(s_[:], pgt[:], AF.Silu)
                            P.tt(actT[:, fb, :], s_[:], put[:], ALU.mult, ko=fb)
                    for ob in range(8):
                        i = ob % 2
                        P.dma(wd[i][:], wd_s[ob, :, :, :], st_d[i])
                        p_ = nps(0, 8)
                        for kc in range(NFB):
                            P.mm(p_[:], wd[i][:, kc, :], actT[:, kc, :], start=(kc == 0), stop=(kc == NFB - 1),
                                 kr=kc)
                        P.tt(xt[:, ob, :], p_[:], xt[:, ob, :], ALU.add)
                    rstd_fm([xt[:, kc, :] for kc in range(KD)], D, TT, rst[:])
                    for ob in range(8):
                        P.stt(xt[:, ob, :], xt[:, ob, :], gvt[:, G_FIN + ob:G_FIN + ob + 1], rst[:],
                              ALU.mult, ALU.mult)
                    P.dma(out_dst[:, cols].rearrange("(k p) n -> p k n", p=128), xt[:], st_y[t % 2])
                P.barrier()

        mkt = sbuf(es, "mkt", [128, max(c.nnon, 1) * 2])
        P.dma(mkt[:], mk[:, :], st_c)
        jobs = []
        if c.lp_own:
            jobs.append(dict(src=xo, lown=c.lp_own, non=xn, nnon=c.nnon, rope=rpo, rnon=rpn, out=yo))
        for s in range(c.ns):
            jobs.append(dict(src=xs[s], lown=c.ls, non=None, nnon=0, rope=rs, rnon=None, out=ys[s]))
        stop = getattr(c, "stop", 0)
        for jb in jobs:
          with ExitStack() as js:
            if stop == 1:
                break
            G["mlaT"] = sbuf(js, "mlaT", [128, 4, jb["lown"]], BF16)
            mla_phase(jb["src"], jb["lown"], jb["non"], jb["nnon"], jb["rope"], jb["rnon"])
            if stop == 2:
                break
            G["ssdT"] = sbuf(js, "ssdT", [128, 4, jb["lown"]], BF16)
            nto = jb["lown"] // TT
            xslots = [(jb["non"][s, :, :], s, True, True) for s in range(jb["nnon"])]
            if nto <= 4:
                ssd_phase(jb["src"], 0, nto, xslots, 0)
            else:
                hh = nto // 2
                sl = xslots + [(jb["src"][:, (hh + t) * TT:(hh + t) * TT + TW], None, False, True)
                               for t in range(nto - hh - 1, -1, -1)]
                ssd_phase(jb["src"], 0, hh, sl, 0)
                sl = xslots + [(jb["src"][:, t * TT:t * TT + TW], None, True, False) for t in range(hh)]
                ssd_phase(jb["src"], hh, nto - hh, sl, hh)
            if stop >= 3:
                break
            ffn_phase(jb["src"], jb["lown"], jb["out"])
        P.finish()
    return nc
def _rope_tab(pos):
    inv = (10000.0 ** (-np.arange(0, 32, 2, dtype=np.float32) / 32)).astype(np.float32)
    ang = pos.astype(np.float32)[:, None] * inv[None, :]
    co = np.cos(ang).astype(np.float32).T
    si = np.sin(ang).astype(np.float32).T
    return np.stack([np.concatenate([co, co], 0), np.concatenate([si, si], 0)], 0)


def _consts():
    j = np.arange(128)[:, None]
    l = np.arange(128)[None, :]
    cm = np.zeros((128, 6, 128), np.float32)
    cm[:, 0] = (j <= l)
    cm[:, 1] = (j >= l)
    cm[:, 2] = (j > l)
    cm[:, 3] = (j < l)
    cm[:, 4] = np.where(j <= l, 0.0, -30000.0)
    cm[:, 5] = np.where(j >= l, 0.0, -30000.0)
    return cm


def make_in_maps(inp, cfg, ncores, cores_per_seq):
    f = lambda k: np.asarray(inp[k], np.float32)
    xp = f("x_prompt")
    xsm = f("x_sample")
    gv = np.zeros((128, 88), np.float32)
    def put(col, vec):
        v = vec.reshape(-1, 128).T
        gv[:, col:col + v.shape[1]] = v
    put(0, f("attn_norm_g")[0]); put(8, f("ffn_norm_g")[0]); put(16, f("final_norm_g"))
    put(24, f("q_a_norm_g")[0]); put(26, f("kv_a_norm_g")[0]); put(27, f("attn_out_norm_g")[0])
    put(31, f("ssd_norm_g")[0]); put(35, f("conv_b")[0])
    cw = f("conv_w")[0]
    for cc in range(8):
        for k in range(5):
            gv[:, 43 + cc * 5 + k] = cw[k, cc * 128:(cc + 1) * 128]
    rowv = np.concatenate([f("dt_bias")[0].reshape(-1), f("a_log")[0].reshape(-1), f("d_skip")[0].reshape(-1)])[None, :]
    cm = _consts()
    LP = xp.shape[1]
    own = cfg.lp_own
    maps = []
    rs = _rope_tab(np.arange(cfg.ls))
    for c in range(ncores):
        b, q = divmod(c, cores_per_seq)
        xT = np.zeros((D, LP + 4), np.float32)
        xT[:, 2:LP + 2] = xp[b].T
        o0 = q * own
        xo = np.ascontiguousarray(xT[:, o0:o0 + own + 4])
        ntl = LP // TT
        ot0, ot1 = o0 // TT, (o0 + own) // TT
        order = list(range(0, ot0)) + list(range(ntl - 1, ot1 - 1, -1))
        nn = max(len(order), 1)
        xn = np.zeros((nn, D, TW), np.float32)
        mk = np.zeros((128, nn * 2), np.float32)
        rpn = np.zeros((nn, 2, 32, TT), np.float32)
        for s, t in enumerate(order):
            xn[s] = xT[:, t * TT:t * TT + TW]
            mk[:, 2 * s] = 1.0 if t < ot0 else 0.0
            mk[:, 2 * s + 1] = 1.0 if t >= ot1 else 0.0
            rpn[s] = _rope_tab(np.arange(t * TT, (t + 1) * TT))
        rpo = _rope_tab(np.arange(o0, o0 + own))
        xsp = np.zeros((cfg.ns, D, cfg.ls + 4), np.float32)
        for i in range(cfg.ns):
            xsp[i, :, 2:cfg.ls + 2] = xsm[c * cfg.ns + i].T
        maps.append(dict(xo=xo, xn=xn, mk=mk, rpo=rpo, rpn=rpn, rs=rs, xs=xsp,
                         w_in=f("w_in")[0], w_q_b=f("w_q_b")[0], w_kv_b=f("w_kv_b")[0], w_out=f("w_out")[0],
                         w_gate=f("w_gate")[0], w_up=f("w_up")[0], w_down=f("w_down")[0],
                         gv=gv, rowv=rowv.astype(np.float32), cm=cm))
    return maps


def run(inp, cfg, ncores, cores_per_seq):
    nc = build(cfg)
    maps = make_in_maps(inp, cfg, ncores, cores_per_seq)
    res = run_bass_kernel_spmd(nc, maps, core_ids=list(range(ncores)))
    xp = np.asarray(inp["x_prompt"]); xsm = np.asarray(inp["x_sample"])
    yp = np.zeros(xp.shape, np.float32)
    ysm = np.zeros(xsm.shape, np.float32)
    for c in range(ncores):
        r = res.results[c]
        b, q = divmod(c, cores_per_seq)
        yp[b, q * cfg.lp_own:(q + 1) * cfg.lp_own, :] = r["yo"].T
        for i in range(cfg.ns):
            ysm[c * cfg.ns + i] = r["ys"][i].T
    return yp, ysm


def kernel(**inputs):
    cfg = Cfg()
    return run(inputs, cfg, 8, 4)
```

```python
import numpy as np
import concourse.bass as bass
import concourse.mybir as mybir

F32 = mybir.dt.float32
BF16 = mybir.dt.bfloat16
AF = mybir.ActivationFunctionType
ALU = mybir.AluOpType
AX = mybir.AxisListType


class Stream:
    def __init__(self, sem):
        self.sem = sem
        self.val = 0


class Prog:
    ENG = ("pe", "act", "dve", "pool")

    def __init__(self, nc, es):
        self.nc = nc
        self.es = es
        self.eng = {"pe": nc.tensor, "act": nc.scalar, "dve": nc.vector,
                    "pool": nc.gpsimd, "sp": nc.sync}
        self.sem = {e: es.enter_context(nc.semaphore("c_" + e)) for e in self.ENG}
        self.cnt = {e: 0 for e in self.ENG}
        self.waited = {e: {} for e in ("pe", "act", "dve", "pool", "sp")}
        self.semobj = {"c_" + e: self.sem[e] for e in self.ENG}
        self.tab = {}
        self.streams = []
        self._snames = {}
        self.nops = 0

    NDS = 24

    def stream(self, name):
        return None

    def _dma_sem(self):
        if not self.streams:
            for i in range(self.NDS):
                st = Stream(self.es.enter_context(self.nc.semaphore("d_%d" % i)))
                st.name = "d_%d" % i
                self.semobj[st.name] = st.sem
                self.streams.append(st)
            self._dsi = 0
        st = self.streams[self._dsi % self.NDS]
        self._dsi += 1
        return st

    def _entries(self, buf, key):
        t = self.tab.setdefault(buf, {})
        if key is None:
            return list(t.values())
        out = []
        if key in t:
            out.append(t[key])
        if None in t:
            out.append(t[None])
        return out

    def _deps(self, eng, reads, writes):
        deps = []
        for ap, key in reads:
            for ent in self._entries(ap.tensor.name, key):
                if ent[0] is not None:
                    deps.append((ent[0], True))
        for ap, key in writes:
            for ent in self._entries(ap.tensor.name, key):
                if ent[0] is not None:
                    deps.append((ent[0], False))
                for tok in ent[1].values():
                    deps.append((tok, False))
        return deps

    def _record(self, eng, tok, reads, writes):
        for ap, key in reads:
            t = self.tab.setdefault(ap.tensor.name, {})
            ent = t.setdefault(key, [None, {}])
            ent[1][eng] = tok
        for ap, key in writes:
            t = self.tab.setdefault(ap.tensor.name, {})
            if key is None:
                t.clear()
                t[None] = [tok, {}]
            else:
                t[key] = [tok, {}]

    def _norm(self, lst):
        out = []
        for x in lst:
            if isinstance(x, tuple):
                out.append(x)
            else:
                out.append((x, None))
        return out

    def _do_waits(self, eng, deps):
        need = {}
        for (semname, val, peng), raw in deps:
            if peng == eng:
                if eng not in ("act", "dve", "pool"):
                    continue
            if need.get(semname, 0) < val:
                need[semname] = val
        w = self.waited[eng]
        e = self.eng[eng]
        for semname, val in need.items():
            if w.get(semname, 0) >= val:
                continue
            w[semname] = val
            e.wait_ge(self.semobj[semname], val)

    def op(self, eng, fn, reads=(), writes=()):
        reads = self._norm(reads)
        writes = self._norm(writes)
        deps = self._deps(eng, reads, writes)
        self._do_waits(eng, deps)
        ins = fn(self.eng[eng])
        self.cnt[eng] += 1
        ins.then_inc(self.sem[eng], 1)
        tok = ("c_" + eng, self.cnt[eng], eng)
        self._record(eng, tok, reads, writes)
        self.nops += 1
        return tok

    def dma(self, out, in_, stream=None, rk=None, wk=None, q="sp", **kw):
        reads = [(in_, rk)]
        writes = [(out, wk)]
        st = self._dma_sem()
        deps = self._deps(q, reads, writes)
        if st.val:
            deps.append(((st.name, st.val, "dma"), False))
        self._do_waits(q, deps)
        ins = self.eng[q].dma_start(out=out, in_=in_, **kw)
        st.val += 16
        ins.then_inc(st.sem, 16)
        tok = (st.name, st.val, "dma")
        self._record("dma:" + st.name, tok, reads, writes)
        self.nops += 1
        return tok

    def finish(self):
        sp = self.eng["sp"]
        for s in self.streams:
            if s.val:
                sp.wait_ge(s.sem, s.val)
        for e in self.ENG:
            if self.cnt[e]:
                sp.wait_ge(self.sem[e], self.cnt[e])

    def barrier(self):
        for e in ("pe", "act", "dve", "pool", "sp"):
            h = self.eng[e]
            w = self.waited[e]
            for p in self.ENG:
                if p == e or self.cnt[p] == 0:
                    continue
                nm = "c_" + p
                if w.get(nm, 0) < self.cnt[p]:
                    w[nm] = self.cnt[p]
                    h.wait_ge(self.sem[p], self.cnt[p])
            for s in self.streams:
                if s.val and w.get(s.name, 0) < s.val:
                    w[s.name] = s.val
                    h.wait_ge(s.sem, s.val)
        self.tab.clear()

    def mm(self, out, lhsT, rhs, start=True, stop=True, ko=None, kl=None, kr=None, **kw):
        return self.op("pe", lambda e: e.matmul(out, lhsT, rhs, start=start, stop=stop, **kw),
                       reads=[(lhsT, kl), (rhs, kr)], writes=[(out, ko)])

    def tr(self, out, in_, ident, ko=None, ki=None):
        return self.op("pe", lambda e: e.transpose(out, in_, ident),
                       reads=[(in_, ki), (ident, None)], writes=[(out, ko)])

    def act(self, out, in_, func, bias=None, scale=None, accum_out=None, ko=None, ki=None,
            eng="act", extra_reads=()):
        kw = {}
        reads = [(in_, ki)] + list(extra_reads)
        if bias is not None:
            kw["bias"] = bias
            if not isinstance(bias, (int, float)):
                reads.append((bias, None))
        if scale is not None:
            kw["scale"] = scale
            if not isinstance(scale, (int, float)):
                reads.append((scale, None))
        writes = [(out, ko)]
        if accum_out is not None:
            kw["accum_out"] = accum_out
            writes.append((accum_out, None))
        return self.op("act", lambda e: e.activation(out, in_, func, **kw), reads=reads, writes=writes)

    def tt(self, out, in0, in1, op, eng="dve", ko=None, k0=None, k1=None):
        return self.op(eng, lambda e: e.tensor_tensor(out, in0, in1, op),
                       reads=[(in0, k0), (in1, k1)], writes=[(out, ko)])

    def ts(self, out, in0, s1, s2=None, op0=ALU.mult, op1=None, eng="dve", ko=None, k0=None,
           accum_out=None):
        reads = [(in0, k0)]
        if not isinstance(s1, (int, float)):
            reads.append((s1, None))
        if s2 is not None and not isinstance(s2, (int, float)):
            reads.append((s2, None))
        kw = {}
        if op1 is not None:
            kw["op1"] = op1
        writes = [(out, ko)]
        if accum_out is not None:
            kw["accum_out"] = accum_out
            writes.append((accum_out, None))
        return self.op(eng, lambda e: e.tensor_scalar(out, in0, s1, s2, op0, **kw),
                       reads=reads, writes=writes)

    def stt(self, out, in0, scalar, in1, op0, op1, eng="dve", ko=None, k0=None, k1=None):
        reads = [(in0, k0), (in1, k1)]
        if not isinstance(scalar, (int, float)):
            reads.append((scalar, None))
        return self.op(eng, lambda e: e.scalar_tensor_tensor(out, in0, scalar, in1, op0, op1),
                       reads=reads, writes=[(out, ko)])

    def copy(self, out, in_, eng="dve", ko=None, ki=None):
        if eng == "act":
            return self.act(out, in_, AF.Copy, ko=ko, ki=ki)
        return self.op(eng, lambda e: e.tensor_copy(out, in_), reads=[(in_, ki)], writes=[(out, ko)])

    def memset(self, ap, val, eng="dve", k=None):
        return self.op(eng, lambda e: e.memset(ap, val), writes=[(ap, k)])

    def recip(self, out, in_, ko=None, ki=None):
        return self.op("dve", lambda e: e.reciprocal(out, in_), reads=[(in_, ki)], writes=[(out, ko)])
from concourse.bass_utils import run_bass_kernel_spmd
from contextlib import ExitStack

EPS = 1e-6
D = 1024
KD = 8
TT = 512
HALO = 2
TW = TT + 2 * HALO
QSCALE = 96 ** -0.5


class Cfg:
    def __init__(self, lp_own=4096, nnon=24, ls=2048, ns=4, dff=2816):
        self.lp_own, self.nnon, self.ls, self.ns, self.dff = lp_own, nnon, ls, ns, dff
        self.nfb = dff // 128
        self.ng = dff // 256
        self.lmax = max(lp_own, ls)


def build(cfg):
    nc = bass.Bass("TRN2", target_bir_lowering=False)
    c = cfg
    NFB, NG = c.nfb, c.ng

    def din(name, shape, dt=F32):
        return nc.dram_tensor(name, list(shape), dt, kind="ExternalInput").ap()

    xo = din("xo", [D, max(c.lp_own, TT) + 4])
    xn = din("xn", [max(c.nnon, 1), D, TW])
    mk = din("mk", [128, max(c.nnon, 1) * 2])
    rpo = din("rpo", [2, 32, max(c.lp_own, TT)])
    rpn = din("rpn", [max(c.nnon, 1), 2, 32, TT])
    rs = din("rs", [2, 32, c.ls])
    xs = din("xs", [max(c.ns, 1), D, c.ls + 4])
    w_in = din("w_in", [D, 1968])
    w_q_b = din("w_q_b", [256, 768])
    w_kv_b = din("w_kv_b", [128, 1024])
    w_out = din("w_out", [D, D])
    w_gate = din("w_gate", [D, c.dff])
    w_up = din("w_up", [D, c.dff])
    w_down = din("w_down", [c.dff, D])
    gv = din("gv", [128, 88])
    rowv = din("rowv", [1, 40])
    cm = din("cm", [128, 6, 128])
    yo = nc.dram_tensor("yo", [D, max(c.lp_own, TT)], F32, kind="ExternalOutput").ap()
    ys = nc.dram_tensor("ys", [max(c.ns, 1), D, c.ls], F32, kind="ExternalOutput").ap()
    wg_s = nc.dram_tensor("wg_s", [NG, 128, KD, 256], BF16, kind="ExternalOutput").ap()
    wu_s = nc.dram_tensor("wu_s", [NG, 128, KD, 256], BF16, kind="ExternalOutput").ap()
    wd_s = nc.dram_tensor("wd_s", [8, 128, NFB, 128], BF16, kind="ExternalOutput").ap()

    with ExitStack() as es:
        P = Prog(nc, es)

        uid = {"n": 0}

        def sbuf(stack, name, shape, dt=F32):
            uid["n"] += 1
            return stack.enter_context(nc.sbuf_tensor("%s_%d" % (name, uid["n"]), list(shape), dt))

        pb = [es.enter_context(nc.psum_tensor(f"pb{i}", [128, 512], F32)) for i in range(8)]
        psrr = {"i": 0}

        def nps(lo=0, hi=8):
            k = "%d_%d" % (lo, hi)
            i = psrr.get(k, lo)
            psrr[k] = lo + ((i - lo + 1) % (hi - lo))
            return pb[i]

        G = {}
        gvt = sbuf(es, "gvt", [128, 88])
        rowt = sbuf(es, "rowt", [128, 40])
        cmt = sbuf(es, "cmt", [128, 6, 128])
        mneg = sbuf(es, "mneg", [128, 2, 128], BF16)
        identf = sbuf(es, "identf", [128, 128])
        identb = sbuf(es, "identb", [128, 128], BF16)
        onesb = sbuf(es, "onesb", [128, 128], BF16)
        onesf = sbuf(es, "onesf", [128, 128])
        abt = sbuf(es, "abt", [128, 16])
        sq2 = [sbuf(es, f"sq{i}", [128, TT], BF16) for i in range(3)]
        lnb = sbuf(es, "lnb", [128, TT])
        rr = {"sq": 0}

        st_c = P.stream("const")
        P.dma(gvt[:], gv[:, :], st_c)
        P.dma(rowt[:], rowv[0:1, :].partition_broadcast(128), st_c)
        P.dma(cmt[:], cm[:, :, :], st_c)
        P.copy(mneg[:], cmt[:, 4:6, :], eng="dve")
        P.memset(identf[:], 1.0, eng="pool")
        P.op("pool", lambda e: e.affine_select(identf[:], identf[:], pattern=[[-1, 128]],
                                               compare_op=ALU.is_equal, fill=0.0, base=0,
                                               channel_multiplier=1),
             reads=[identf[:]], writes=[identf[:]])
        P.copy(identb[:], identf[:], eng="dve")
        P.memset(onesb[:], 1.0, eng="dve")
        P.memset(onesf[:], 1.0, eng="dve")
        P.act(abt[:], rowt[:, 16:32], AF.Exp)
        P.ts(abt[:], abt[:], -1.0, None, op0=ALU.mult)
        G_ATTN, G_FFN, G_FIN, G_QA, G_KVA, G_AO, G_SSD, G_CB, G_CW = 0, 8, 16, 24, 26, 27, 31, 35, 43
        dtb = rowt[:, 0:16]
        dsk = rowt[:, 32:40]

        def rstd_fm(srcs, Dn, N, out_ap):
            ps = nps(0, 6)
            for i, s in enumerate(srcs):
                sq = sq2[rr["sq"] % 3]
                rr["sq"] += 1
                P.act(sq[:, :N], s, AF.Square)
                P.mm(ps[:, :N], onesb[:], sq[:, :N], start=(i == 0), stop=(i == len(srcs) - 1))
            P.act(lnb[:, :N], ps[:, :N], AF.Ln, bias=EPS, scale=1.0 / Dn)
            P.act(out_ap, lnb[:, :N], AF.Exp, scale=-0.5)

        stg = [sbuf(es, f"stg{i}", [128, KD, 128]) for i in range(2)]
        st_w = [P.stream("w0"), P.stream("w1")]
        wrr = {"i": 0}

        def prep_w(dst_fn, src, k_chunks, c0, ncols, gcol, scale=1.0):
            for a in range(0, ncols, 128):
                b = min(a + 128, ncols)
                i = wrr["i"] % 2
                wrr["i"] += 1
                P.dma(stg[i][:, 0:k_chunks, 0:b - a],
                      src[0:k_chunks * 128, c0 + a:c0 + b].rearrange("(k p) n -> p k n", p=128), st_w[i])
                for kc in range(k_chunks):
                    o = dst_fn(kc, a, b)
                    P.ts(o, stg[i][:, kc, 0:b - a], gvt[:, gcol + kc:gcol + kc + 1], None, op0=ALU.mult)

        with ExitStack() as ph:
            wtmp = [sbuf(ph, f"wtmp{i}", [128, KD, 256], BF16) for i in range(2)]
            wdt = [sbuf(ph, f"wdt{i}", [128, NFB, 128], BF16) for i in range(2)]
            wdf = [sbuf(ph, f"wdf{i}", [128, NFB, 128]) for i in range(2)]
            st_o = [P.stream("wo0"), P.stream("wo1")]
            n = 0
            for (src, dst) in ((w_gate, wg_s), (w_up, wu_s)):
                for g in range(NG):
                    t = wtmp[n % 2]
                    prep_w(lambda kc, a, b, t=t: t[:, kc, a:b], src, KD, g * 256, 256, G_FFN)
                    P.dma(dst[g, :, :, :], t[:], st_o[n % 2])
                    n += 1
            for ob in range(8):
                i = ob % 2
                P.dma(wdf[i][:], w_down[:, ob * 128:(ob + 1) * 128].rearrange("(k p) n -> p k n", p=128),
                      st_w[i])
                P.copy(wdt[i][:], wdf[i][:], eng="act" if ob % 2 else "dve")
                P.dma(wd_s[ob, :, :, :], wdt[i][:], st_o[i])
            P.barrier()

        def load_x(dst, src_cols, stream):
            P.dma(dst, src_cols.rearrange("(k p) n -> p k n", p=128), stream)

        def norm_tile(xt, hn, rst, N):
            pieces = [(0, min(N, TT))] + ([(TT, N)] if N > TT else [])
            for (a, b) in pieces:
                rstd_fm([xt[:, kc, a:b] for kc in range(KD)], D, b - a, rst[:, a:b])
            for kc in range(KD):
                P.tt(hn[:, kc, 0:N], xt[:, kc, 0:N], rst[:, 0:N], ALU.mult,
                     eng="pool" if kc % 2 else "dve")

        def mla_phase(own_src, lown, non_src, nnon, rope_own, rope_non):
            nto = lown // TT
            ltot = lown + nnon * TT
            ntt = ltot // TT
            with ExitStack() as ph:
                wqb = sbuf(ph, "wqb", [128, 2, 768], BF16)
                wqr = sbuf(ph, "wqr", [128, 2, 8, 96], BF16)
                wkvb = sbuf(ph, "wkvb", [128, 1024], BF16)
                ckvn = sbuf(ph, "ckvn", [128, ltot], BF16)
                KT = sbuf(ph, "KT", [96, ltot], BF16)
                qlatn = sbuf(ph, "qlatn", [128, 2, lown], BF16)
                rst2 = sbuf(ph, "mrst2", [128, TT])
                rtab = [sbuf(ph, f"rtab{i}", [96, 2, TT]) for i in range(2)]
                t1 = sbuf(ph, "mt1", [96, TT])
                t2 = sbuf(ph, "mt2", [96, TT])
                ph1 = ExitStack()
                w_mla = sbuf(ph1, "w_mla", [128, KD, 576], BF16)
                xt2 = [sbuf(ph1, f"mxt{i}", [128, KD, TT]) for i in range(2)]
                hn = sbuf(ph1, "mhn", [128, KD, TT], BF16)
                rst = sbuf(ph1, "mrst", [128, TT])
                st_x = [P.stream("mx0"), P.stream("mx1")]
                st_r = [P.stream("mr0"), P.stream("mr1")]

                prep_w(lambda kc, a, b: w_mla[:, kc, a:b], w_in, KD, 0, 384, G_ATTN)
                P.memset(w_mla[:, :, 384:576], 0.0, eng="pool")
                prep_w(lambda kc, a, b: w_mla[:, kc, 448 + a:448 + b], w_in, KD, 384, 32, G_ATTN)
                for kc in range(KD):
                    P.ts(w_mla[:, kc, 544:560], w_mla[:, kc, 464:480], -1.0, None, op0=ALU.mult)
                    P.copy(w_mla[:, kc, 560:576], w_mla[:, kc, 448:464], eng="pool")
                prep_w(lambda kc, a, b: wqb[:, kc, a:b], w_q_b, 2, 0, 768, G_QA)
                P.memset(wqr[:], 0.0, eng="pool")
                for kc in range(2):
                    for h in range(8):
                        P.ts(wqr[:, kc, h, 64:80], wqb[:, kc, h * 96 + 80:h * 96 + 96], -1.0, None,
                             op0=ALU.mult)
                        P.copy(wqr[:, kc, h, 80:96], wqb[:, kc, h * 96 + 64:h * 96 + 80], eng="pool")
                prep_w(lambda kc, a, b: wkvb[:, a:b], w_kv_b, 1, 0, 1024, G_KVA)

                def tile_src(t):
                    if t < nto:
                        return own_src[:, HALO + t * TT:HALO + (t + 1) * TT], \
                            rope_own[:, :, t * TT:(t + 1) * TT]
                    s = t - nto
                    return non_src[s, :, HALO:HALO + TT], rope_non[s, :, :, :]

                def issue_load(t):
                    xs_, rp_ = tile_src(t)
                    load_x(xt2[t % 2][:], xs_, st_x[t % 2])
                    P.dma(rtab[t % 2][64:96, :, :], rp_.rearrange("a p n -> p a n"), st_r[t % 2])

                issue_load(0)
                for t in range(ntt):
                    if t + 1 < ntt:
                        issue_load(t + 1)
                    xt = xt2[t % 2]
                    rt = rtab[t % 2]
                    norm_tile(xt, hn, rst, TT)
                    cols = slice(t * TT, (t + 1) * TT)
                    pc = nps(0, 6)
                    for kc in range(KD):
                        P.mm(pc[:], w_mla[:, kc, 256:384], hn[:, kc, :], start=(kc == 0), stop=(kc == KD - 1))
                    rstd_fm([pc[:]], 128, TT, rst2[:])
                    P.tt(ckvn[:, cols], pc[:], rst2[:], ALU.mult, ko=t)
                    pa = nps(0, 6)
                    pr = nps(0, 6)
                    for kc in range(KD):
                        P.mm(pa[0:96, :], w_mla[:, kc, 384:480], hn[:, kc, :], start=(kc == 0), stop=(kc == KD - 1))
                    for kc in range(KD):
                        P.mm(pr[0:96, :], w_mla[:, kc, 480:576], hn[:, kc, :], start=(kc == 0), stop=(kc == KD - 1))
                    P.tt(t1[64:96, :], pa[64:96, :], rt[64:96, 0, :], ALU.mult)
                    P.tt(t2[64:96, :], pr[64:96, :], rt[64:96, 1, :], ALU.mult)
                    P.tt(KT[64:96, cols], t1[64:96, :], t2[64:96, :], ALU.add, ko=("r", t))
                    if t < nto:
                        pq = [nps(0, 6), nps(0, 6)]
                        for cq in range(2):
                            for kc in range(KD):
                                P.mm(pq[cq][:], w_mla[:, kc, cq * 128:(cq + 1) * 128], hn[:, kc, :],
                                     start=(kc == 0), stop=(kc == KD - 1))
                        rstd_fm([pq[0][:], pq[1][:]], 256, TT, rst2[:])
                        for cq in range(2):
                            P.tt(qlatn[:, cq, cols], pq[cq][:], rst2[:], ALU.mult, ko=t)

                P.barrier()
                ph1.close()
                VH = sbuf(ph, "VH", [128, ltot // 128, 65], BF16)
                QH = sbuf(ph, "QH", [96, lown], BF16)
                PT = [sbuf(ph, f"PT{i}", [128, TT], BF16) for i in range(6)]
                osb = sbuf(ph, "osb", [65, TT])
                rc = sbuf(ph, "rc", [65, TT])
                P.memset(VH[:, :, 64:65], 1.0, eng="pool")
                nkb = ltot // 128
                for h in range(8):
                    for t in range(ntt):
                        cols = slice(t * TT, (t + 1) * TT)
                        pk = nps(0, 6)
                        P.mm(pk[0:64, :], wkvb[:, h * 128:h * 128 + 64], ckvn[:, cols], kr=t)
                        P.copy(KT[0:64, cols], pk[0:64, :], eng="act" if t % 2 else "dve", ko=("n", t))
                    for g8 in range(0, nkb, 8):
                        pv = nps(0, 6)
                        for j in range(8):
                            kb = g8 + j
                            P.mm(pv[:, j * 64:(j + 1) * 64], ckvn[:, kb * 128:(kb + 1) * 128],
                                 wkvb[:, h * 128 + 64:h * 128 + 128], kl=kb // 4)
                        P.copy(VH[:, g8:g8 + 8, 0:64], pv[:, :].rearrange("p (j d) -> p j d", j=8),
                               eng="dve" if (g8 // 8) % 2 else "act", ko=g8 // 8)
                    for t in range(nto):
                        cols = slice(t * TT, (t + 1) * TT)
                        rt = rtab[t % 2]
                        P.dma(rt[64:96, :, :], rope_own[:, :, cols].rearrange("a p n -> p a n"), st_r[t % 2])
                        pa = nps(0, 6)
                        pr = nps(0, 6)
                        for kc in range(2):
                            P.mm(pa[0:96, :], wqb[:, kc, h * 96:(h + 1) * 96], qlatn[:, kc, cols],
                                 start=(kc == 0), stop=(kc == 1), kr=t)
                        for kc in range(2):
                            P.mm(pr[0:96, :], wqr[:, kc, h, :], qlatn[:, kc, cols],
                                 start=(kc == 0), stop=(kc == 1), kr=t)
                        P.copy(QH[0:64, cols], pa[0:64, :], eng="act", ko=("n", t))
                        P.tt(t1[64:96, :], pa[64:96, :], rt[64:96, 0, :], ALU.mult)
                        P.tt(t2[64:96, :], pr[64:96, :], rt[64:96, 1, :], ALU.mult)
                        P.tt(QH[64:96, cols], t1[64:96, :], t2[64:96, :], ALU.add, ko=("r", t))
                    for t in range(nto):
                        cols = slice(t * TT, (t + 1) * TT)
                        po = pb[6 + (t % 2)]
                        LOOK = 3
                        pend = []
                        for idx in range(nkb + LOOK):
                            if idx < nkb:
                                kb = idx
                                ps_ = nps(0, 6)
                                tk = kb // 4
                                P.op("pe", lambda e, ps_=ps_, kb=kb, cols=cols: e.matmul(
                                    ps_[:], KT[0:96, kb * 128:(kb + 1) * 128], QH[0:96, cols], start=True, stop=True),
                                    reads=[(KT[:], ("n", tk)), (KT[:], ("r", tk)), (QH[:], ("n", t)), (QH[:], ("r", t))],
                                    writes=[(ps_[:], None)])
                                pt = PT[kb % 6]
                                P.act(pt[:], ps_[:], AF.Exp, scale=QSCALE)
                            if idx >= LOOK:
                                kb = idx - LOOK
                                P.mm(po[0:65, :], VH[:, kb, :], PT[kb % 6][:], start=(kb == 0), stop=(kb == nkb - 1),
                                     kl=kb // 8)
                        P.copy(osb[:], po[0:65, :], eng="dve")
                        P.recip(rc[64:65, :], osb[64:65, :])
                        pbc = nps(0, 6)
                        P.mm(pbc[0:64, :], onesf[64:65, 0:64], rc[64:65, :])
                        P.tt(G["mlaT"][(h % 2) * 64:(h % 2) * 64 + 64, h // 2, cols], osb[0:64, :], pbc[0:64, :],
                             ALU.mult, ko=(h, t))
                for t in range(nto):
                    cols = slice(t * TT, (t + 1) * TT)
                    rstd_fm([G["mlaT"][:, cc, cols] for cc in range(4)], 512, TT, rst2[:])
                    for cc in range(4):
                        P.tt(G["mlaT"][:, cc, cols], G["mlaT"][:, cc, cols], rst2[:], ALU.mult,
                             eng="pool" if cc % 2 else "dve")
                P.barrier()

        def ssd_phase(own_src, own_t0, nto, slots, out_t0):
            nch = nto * 4
            with ExitStack() as ph:
                w_ssd = sbuf(ph, "w_ssd", [128, KD, 1552], BF16)
                Sst = [sbuf(ph, f"Sst{d}", [128, 512]) for d in range(2)]
                SBW = sbuf(ph, "SBW", [128, nch, 512], BF16)
                Sfb = sbuf(ph, "Sfb", [128, 512], BF16)
                xtb = [sbuf(ph, f"sxt{i}", [128, KD, TW]) for i in range(1)]
                dgw = sbuf(ph, "dgw", [128, 8, 5, 128], BF16)
                preb = [sbuf(ph, f"preb{i}", [128, TW], BF16) for i in range(2)]
                hn = sbuf(ph, "shn", [128, KD, TW], BF16)
                rst = sbuf(ph, "srst", [128, TW])
                xact = sbuf(ph, "xact", [128, 8, TT], BF16)
                x_tok = sbuf(ph, "x_tok", [128, 4, 512], BF16)
                B_tok = sbuf(ph, "B_tok", [128, 4, 256], BF16)
                z_tok = sbuf(ph, "z_tok", [128, 4, 512], BF16)
                dt_tok = sbuf(ph, "dt_tok", [128, 4, 16])
                dtm = sbuf(ph, "dtm", [128, 4, 16])
                sm = [sbuf(ph, f"sm{i}", [128, 16]) for i in range(16)]
                xw = sbuf(ph, "xw", [128, 512], BF16)
                xdt = [sbuf(ph, f"xdt{d}", [128, 512], BF16) for d in range(2)]
                GT = sbuf(ph, "GT", [128, 2, 128], BF16)
                Dm = [sbuf(ph, f"Dm{i}", [128, 512], BF16) for i in range(2)]
                Mm = [sbuf(ph, f"Mm{i}", [128, 512], BF16) for i in range(2)]
                ya = sbuf(ph, "ya", [128, 512])
                yb = sbuf(ph, "yb", [128, 512])
                yg = sbuf(ph, "yg", [128, 512])
                junk = sbuf(ph, "junk", [128, 512], BF16)
                stok = sbuf(ph, "stok", [128, 512], BF16)
                st_x = P.stream("sx")
                smr = {"i": 0}

                def smn():
                    smr["i"] += 1
                    return sm[smr["i"] % 16]

                prep_w(lambda kc, a, b: w_ssd[:, kc, a:b], w_in, KD, 416, 1552, G_ATTN)
                P.memset(Sst[0][:], 0.0)
                P.memset(Sst[1][:], 0.0)
                for cc in range(8):
                    for k in range(5):
                        P.ts(dgw[:, cc, k, :], identf[:], gvt[:, G_CW + cc * 5 + k:G_CW + cc * 5 + k + 1], None,
                             op0=ALU.mult)
                tl = {"i": 0, "q": []}

                def prefetch(src):
                    i = tl["i"]
                    tl["i"] += 1
                    load_x(xtb[0][:], src, st_x)
                    tl["q"].append(xtb[0])

                def tile_front(src, full):
                    xt = tl["q"].pop(0)
                    import os
                    dbg = os.environ.get("SSD_DBG", "z")
                    if dbg == "0":
                        return
                    norm_tile(xt, hn, rst, TW)
                    if tl["next"] is not None:
                        prefetch(tl["next"])
                        tl["next"] = None
                    if dbg == "a":
                        return
                    ncc = 8 if full else 6
                    for cc in range(ncc):
                        pm = nps(0, 6)
                        ph_ = nps(0, 6)
                        col = 512 + cc * 128
                        for kc in range(KD):
                            P.mm(pm[:], w_ssd[:, kc, col:col + 128], hn[:, kc, 0:TT],
                                 start=(kc == 0), stop=(kc == KD - 1))
                        for kc in range(KD):
                            P.mm(ph_[:, 0:4], w_ssd[:, kc, col:col + 128], hn[:, kc, TT:TW],
                                 start=(kc == 0), stop=(kc == KD - 1))
                        pr_ = preb[cc % 2]
                        P.copy(pr_[:, 0:TT], pm[:], eng="act")
                        P.copy(pr_[:, TT:TW], ph_[:, 0:4], eng="dve")
                        pcv = nps(0, 6)
                        for k in range(5):
                            P.mm(pcv[:], dgw[:, cc, k, :], pr_[:, k:k + TT], start=(k == 0), stop=(k == 4))
                        P.act(xact[:, cc, :], pcv[:], AF.Silu, bias=gvt[:, G_CB + cc:G_CB + cc + 1])
                    if dbg == "b":
                        return
                    for tb in range(4):
                        pd = nps(0, 6)
                        for kc in range(KD):
                            P.mm(pd[:, 0:16], hn[:, kc, HALO + tb * 128:HALO + (tb + 1) * 128],
                                 w_ssd[:, kc, 1536:1552], start=(kc == 0), stop=(kc == KD - 1))
                        v = smn()
                        P.tt(v[:], pd[:, 0:16], dtb, ALU.add)
                        a_ = smn()
                        P.act(a_[:], v[:], AF.Abs)
                        e_ = smn()
                        P.act(e_[:], a_[:], AF.Exp, scale=-1.0)
                        l_ = smn()
                        P.act(l_[:], e_[:], AF.Ln, bias=1.0)
                        P.ts(v[:], v[:], 0.0, None, op0=ALU.max)
                        P.tt(dt_tok[:, tb, :], v[:], l_[:], ALU.add, ko=tb)
                    if full:
                        for tb in range(4):
                            pz = nps(0, 6)
                            for kc in range(KD):
                                P.mm(pz[:], hn[:, kc, HALO + tb * 128:HALO + (tb + 1) * 128],
                                     w_ssd[:, kc, 0:512], start=(kc == 0), stop=(kc == KD - 1))
                            P.act(z_tok[:, tb, :], pz[:], AF.Silu, ko=tb)
                    if dbg == "c":
                        return
                    for tb in range(4):
                        pt_ = nps(0, 6)
                        pt2 = nps(0, 6)
                        for cc in range(4):
                            P.mm(pt_[:, cc * 128:(cc + 1) * 128], xact[:, cc, tb * 128:(tb + 1) * 128], identb[:])
                        for cc in range(2):
                            P.mm(pt2[:, cc * 128:(cc + 1) * 128], xact[:, 4 + cc, tb * 128:(tb + 1) * 128],
                                 identb[:])
                        P.copy(x_tok[:, tb, :], pt_[:, 0:512], eng="dve", ko=tb)
                        P.copy(B_tok[:, tb, :], pt2[:, 0:256], eng="act", ko=tb)

                def bc8(ap8):
                    return ap8.unsqueeze(2).to_broadcast([128, 8, 64])

                def v8(ap512):
                    return ap512.rearrange("p (h d) -> p h d", h=8)

                def state_update(d, tb, dt8, save=None):
                    dtA = smn()
                    P.tt(dtA[:, 0:8], dt8, abt[:, d * 8:(d + 1) * 8], ALU.mult)
                    pp = nps(0, 6)
                    P.mm(pp[:, 0:8], cmt[:, 2 + d, :], dtA[:, 0:8])
                    P.mm(pp[:, 8:16], onesf[:], dtA[:, 0:8])
                    ex = smn()
                    P.act(ex[:], pp[:, 0:16], AF.Exp)
                    w_ = smn()
                    P.tt(w_[:, 0:8], dt8, ex[:, 0:8], ALU.mult)
                    P.tt(v8(xw[:]), v8(x_tok[:, tb, :]), bc8(w_[:, 0:8]), ALU.mult, k0=tb)
                    pc = nps(0, 6)
                    for g in range(2):
                        P.mm(pc[:, g * 256:(g + 1) * 256], B_tok[:, tb, g * 128:(g + 1) * 128],
                             xw[:, g * 256:(g + 1) * 256], kl=tb)
                    S = Sst[d]
                    if save is not None:
                        P.copy(save, S[:], eng="act")
                    P.tt(v8(S[:]), v8(S[:]), bc8(ex[:, 8:16]), ALU.mult)
                    P.tt(S[:], S[:], pc[:], ALU.add)

                def full_chunk(tb, ci, out_cols):
                    pg = nps(0, 6)
                    for g in range(2):
                        P.mm(pg[:, g * 128:(g + 1) * 128], xact[:, 4 + g, tb * 128:(tb + 1) * 128],
                             xact[:, 6 + g, tb * 128:(tb + 1) * 128])
                    P.copy(GT[:], pg[:, 0:256].rearrange("p (g l) -> p g l", g=2), eng="act")
                    P.copy(Sfb[:], Sst[0][:], eng="act")
                    prel = []
                    for d in range(2):
                        dt8 = dt_tok[:, tb, d * 8:(d + 1) * 8]
                        dtA = smn()
                        P.tt(dtA[:, 0:8], dt8, abt[:, d * 8:(d + 1) * 8], ALU.mult, k0=tb)
                        pcs = nps(0, 6)
                        P.mm(pcs[:, 0:8], cmt[:, d, :], dtA[:, 0:8])
                        et = smn()
                        P.act(et[:, 0:8], pcs[:, 0:8], AF.Exp)
                        ncs = smn()
                        P.ts(ncs[:, 0:8], pcs[:, 0:8], -1.0, None, op0=ALU.mult)
                        P.tt(v8(xdt[d][:]), v8(x_tok[:, tb, :]), bc8(dt8), ALU.mult, k0=tb, k1=tb)
                        poff = nps(0, 6)
                        for g in range(2):
                            rhs = Sfb[:, g * 256:(g + 1) * 256] if d == 0 else SBW[:, ci, g * 256:(g + 1) * 256]
                            P.mm(poff[:, g * 256:(g + 1) * 256], xact[:, 6 + g, tb * 128:(tb + 1) * 128], rhs,
                                 kr=(None if d == 0 else ci))
                        tgt = ya if d == 0 else yb
                        P.tt(v8(tgt[:]), v8(poff[:]), bc8(et[:, 0:8]), ALU.mult)
                        prel.append((dtA, ncs))
                    pdgs = [pb[6], pb[7]]
                    groups = [(d, g) for d in range(2) for g in range(2)]

                    def stageA(k):
                        d, g = groups[k]
                        dtA, ncs = prel[d]
                        pcb = nps(0, 6)
                        for j in range(4):
                            h = g * 4 + j
                            P.mm(pcb[:, j * 128:(j + 1) * 128], dtA[:, h:h + 1].to_broadcast([128, 128]),
                                 cmt[:, d, :], start=True, stop=False)
                            P.mm(pcb[:, j * 128:(j + 1) * 128], identb[:], mneg[:, d, :], start=False, stop=True)
                        Dq = Dm[k % 2]
                        for j in range(4):
                            h = g * 4 + j
                            P.act(Dq[:, j * 128:(j + 1) * 128], pcb[:, j * 128:(j + 1) * 128], AF.Exp,
                                  bias=ncs[:, h:h + 1])
                        P.tt(Mm[k % 2][:, :].rearrange("p (j l) -> p j l", j=4),
                             Dq[:, :].rearrange("p (j l) -> p j l", j=4),
                             GT[:, g, :].unsqueeze(1).to_broadcast([128, 4, 128]), ALU.mult)

                    def stageB(k):
                        d, g = groups[k]
                        for j in range(4):
                            h = g * 4 + j
                            P.mm(pdgs[d][:, h * 64:(h + 1) * 64], Mm[k % 2][:, j * 128:(j + 1) * 128],
                                 xdt[d][:, h * 64:(h + 1) * 64])

                    stageA(0)
                    stageA(1)
                    stageB(0)
                    stageA(2)
                    stageB(1)
                    stageA(3)
                    stageB(2)
                    stageB(3)
                    P.tt(ya[:], ya[:], pdgs[0][:], ALU.add)
                    P.tt(yb[:], yb[:], pdgs[1][:], ALU.add)
                    P.tt(ya[:], ya[:], yb[:], ALU.add)
                    P.tt(v8(yb[:]), v8(x_tok[:, tb, :]), bc8(dsk), ALU.mult, k0=tb)
                    P.tt(ya[:], ya[:], yb[:], ALU.add)
                    P.tt(yg[:], ya[:], z_tok[:, tb, :], ALU.mult, k1=tb)
                    ss = smn()
                    P.memset(ss[:, 0:1], 0.0)
                    P.act(junk[:], yg[:], AF.Square, accum_out=ss[:, 0:1])
                    l2 = smn()
                    P.act(l2[:, 0:1], ss[:, 0:1], AF.Ln, bias=EPS, scale=1.0 / 512)
                    r2 = smn()
                    P.act(r2[:, 0:1], l2[:, 0:1], AF.Exp, scale=-0.5)
                    P.ts(stok[:], yg[:], r2[:, 0:1], None, op0=ALU.mult)
                    pt_ = nps(0, 6)
                    for cc in range(4):
                        P.mm(pt_[:, cc * 128:(cc + 1) * 128], stok[:, cc * 128:(cc + 1) * 128], identb[:])
                    P.copy(G["ssdT"][:, :, out_cols], pt_[:, 0:512].rearrange("p (c n) -> p c n", c=4), eng="dve",
                           ko=ci)

                sstop = getattr(c, "stop", 0)
                visits = []
                for (src, midx, fon, bon) in slots:
                    visits.append(("slot", src, midx, fon, bon))
                for t in range(nto - 1, -1, -1):
                    g0 = (own_t0 + t) * TT
                    visits.append(("bwd", own_src[:, g0:g0 + TW], t))
                for t in range(nto):
                    g0 = (own_t0 + t) * TT
                    visits.append(("full", own_src[:, g0:g0 + TW], t))
                prefetch(visits[0][1])
                for vi, v in enumerate(visits):
                    tl["next"] = visits[vi + 1][1] if vi + 1 < len(visits) else None
                    if v[0] == "slot":
                        _, src, midx, fon, bon = v
                        tile_front(src, False)
                        if midx is not None:
                            for tb in range(4):
                                P.tt(dtm[:, tb, :].rearrange("p (d h) -> p d h", d=2),
                                     dt_tok[:, tb, :].rearrange("p (d h) -> p d h", d=2),
                                     mkt[:, midx * 2:midx * 2 + 2].unsqueeze(2).to_broadcast([128, 2, 8]),
                                     ALU.mult, k0=tb)
                            dsrc = dtm
                        else:
                            dsrc = dt_tok
                        if fon:
                            for tb in range(4):
                                state_update(0, tb, dsrc[:, tb, 0:8])
                        if bon:
                            for tb in (3, 2, 1, 0):
                                state_update(1, tb, dsrc[:, tb, 8:16])
                    elif v[0] == "bwd":
                        t = v[2]
                        tile_front(v[1], False)
                        for tb in (3, 2, 1, 0):
                            state_update(1, tb, dt_tok[:, tb, 8:16], save=SBW[:, t * 4 + tb, :])
                    else:
                        t = v[2]
                        tile_front(v[1], True)
                        for tb in range(4):
                            oc = (out_t0 + t) * TT + tb * 128
                            full_chunk(tb, t * 4 + tb, slice(oc, oc + 128))
                            state_update(0, tb, dt_tok[:, tb, 0:8])
                P.barrier()

        def ffn_phase(own_src, lown, out_dst):
            nto = lown // TT
            with ExitStack() as ph:
                wout = sbuf(ph, "wout", [128, KD, D], BF16)
                xt2 = [sbuf(ph, f"fxt{i}", [128, KD, TT]) for i in range(2)]
                h2 = sbuf(ph, "h2", [128, KD, TT], BF16)
                rst = sbuf(ph, "frst", [128, TT])
                actT = sbuf(ph, "actT", [128, NFB, TT], BF16)
                sg = [sbuf(ph, f"sg{i}", [128, TT]) for i in range(2)]
                wg = [sbuf(ph, f"wg{i}", [128, KD, 256], BF16) for i in range(2)]
                wu = [sbuf(ph, f"wu{i}", [128, KD, 256], BF16) for i in range(2)]
                wd = [sbuf(ph, f"wd{i}", [128, NFB, 128], BF16) for i in range(2)]
                st_x = [P.stream("fx0"), P.stream("fx1")]
                st_g = [P.stream("fg0"), P.stream("fg1")]
                st_d = [P.stream("fd0"), P.stream("fd1")]
                st_y = [P.stream("fy0"), P.stream("fy1")]
                prep_w(lambda kc, a, b: wout[:, kc, a:b], w_out, 4, 0, D, G_AO)
                prep_w(lambda kc, a, b: wout[:, 4 + kc, a:b], w_out[512:1024, :], 4, 0, D, G_SSD)
                load_x(xt2[0][:], own_src[:, HALO:HALO + TT], st_x[0])
                for t in range(nto):
                    if t + 1 < nto:
                        load_x(xt2[(t + 1) % 2][:], own_src[:, HALO + (t + 1) * TT:HALO + (t + 2) * TT],
                               st_x[(t + 1) % 2])
                    xt = xt2[t % 2]
                    cols = slice(t * TT, (t + 1) * TT)
                    for ob in range(8):
                        p_ = nps(0, 8)
                        for kc in range(8):
                            rhs = G["mlaT"][:, kc, cols] if kc < 4 else G["ssdT"][:, kc - 4, cols]
                            P.mm(p_[:], wout[:, kc, ob * 128:(ob + 1) * 128], rhs, start=(kc == 0), stop=(kc == 7))
                        P.tt(xt[:, ob, :], p_[:], xt[:, ob, :], ALU.add)
                    norm_tile(xt, h2, rst, TT)
                    for g in range(NG):
                        i = g % 2
                        P.dma(wg[i][:], wg_s[g, :, :, :], st_g[i])
                        P.dma(wu[i][:], wu_s[g, :, :, :], st_g[i])
                        for j in range(2):
                            fb = g * 2 + j
                            pgt = nps(0, 8)
                            put = nps(0, 8)
                            for kc in range(KD):
                                P.mm(pgt[:], wg[i][:, kc, j * 128:(j + 1) * 128], h2[:, kc, :],
                                     start=(kc == 0), stop=(kc == KD - 1))
                            for kc in range(KD):
                                P.mm(put[:], wu[i][:, kc, j * 128:(j + 1) * 128], h2[:, kc, :],
                                     start=(kc == 0), stop=(kc == KD - 1))
                            s_ = sg[fb % 2]
                            P.act(s_[:], pgt[:], AF.Silu)
                            P.tt(actT[:, fb, :], s_[:], put[:], ALU.mult, ko=fb)
                    for ob in range(8):
                        i = ob % 2
                        P.dma(wd[i][:], wd_s[ob, :, :, :], st_d[i])
                        p_ = nps(0, 8)
                        for kc in range(NFB):
                            P.mm(p_[:], wd[i][:, kc, :], actT[:, kc, :], start=(kc == 0), stop=(kc == NFB - 1),
                                 kr=kc)
                        P.tt(xt[:, ob, :], p_[:], xt[:, ob, :], ALU.add)
                    rstd_fm([xt[:, kc, :] for kc in range(KD)], D, TT, rst[:])
                    for ob in range(8):
                        P.stt(xt[:, ob, :], xt[:, ob, :], gvt[:, G_FIN + ob:G_FIN + ob + 1], rst[:],
                              ALU.mult, ALU.mult)
                    P.dma(out_dst[:, cols].rearrange("(k p) n -> p k n", p=128), xt[:], st_y[t % 2])
                P.barrier()

        mkt = sbuf(es, "mkt", [128, max(c.nnon, 1) * 2])
        P.dma(mkt[:], mk[:, :], st_c)
        jobs = []
        if c.lp_own:
            jobs.append(dict(src=xo, lown=c.lp_own, non=xn, nnon=c.nnon, rope=rpo, rnon=rpn, out=yo))
        for s in range(c.ns):
            jobs.append(dict(src=xs[s], lown=c.ls, non=None, nnon=0, rope=rs, rnon=None, out=ys[s]))
        stop = getattr(c, "stop", 0)
        for jb in jobs:
          with ExitStack() as js:
            if stop == 1:
                break
            G["mlaT"] = sbuf(js, "mlaT", [128, 4, jb["lown"]], BF16)
            mla_phase(jb["src"], jb["lown"], jb["non"], jb["nnon"], jb["rope"], jb["rnon"])
            if stop == 2:
                break
            G["ssdT"] = sbuf(js, "ssdT", [128, 4, jb["lown"]], BF16)
            nto = jb["lown"] // TT
            xslots = [(jb["non"][s, :, :], s, True, True) for s in range(jb["nnon"])]
            if nto <= 4:
                ssd_phase(jb["src"], 0, nto, xslots, 0)
            else:
                hh = nto // 2
                sl = xslots + [(jb["src"][:, (hh + t) * TT:(hh + t) * TT + TW], None, False, True)
                               for t in range(nto - hh - 1, -1, -1)]
                ssd_phase(jb["src"], 0, hh, sl, 0)
                sl = xslots + [(jb["src"][:, t * TT:t * TT + TW], None, True, False) for t in range(hh)]
                ssd_phase(jb["src"], hh, nto - hh, sl, hh)
            if stop >= 3:
                break
            ffn_phase(jb["src"], jb["lown"], jb["out"])
        P.finish()
    return nc
def _rope_tab(pos):
    inv = (10000.0 ** (-np.arange(0, 32, 2, dtype=np.float32) / 32)).astype(np.float32)
    ang = pos.astype(np.float32)[:, None] * inv[None, :]
    co = np.cos(ang).astype(np.float32).T
    si = np.sin(ang).astype(np.float32).T
    return np.stack([np.concatenate([co, co], 0), np.concatenate([si, si], 0)], 0)


def _consts():
    j = np.arange(128)[:, None]
    l = np.arange(128)[None, :]
    cm = np.zeros((128, 6, 128), np.float32)
    cm[:, 0] = (j <= l)
    cm[:, 1] = (j >= l)
    cm[:, 2] = (j > l)
    cm[:, 3] = (j < l)
    cm[:, 4] = np.where(j <= l, 0.0, -30000.0)
    cm[:, 5] = np.where(j >= l, 0.0, -30000.0)
    return cm


def make_in_maps(inp, cfg, ncores, cores_per_seq):
    f = lambda k: np.asarray(inp[k], np.float32)
    xp = f("x_prompt")
    xsm = f("x_sample")
    gv = np.zeros((128, 88), np.float32)
    def put(col, vec):
        v = vec.reshape(-1, 128).T
        gv[:, col:col + v.shape[1]] = v
    put(0, f("attn_norm_g")[0]); put(8, f("ffn_norm_g")[0]); put(16, f("final_norm_g"))
    put(24, f("q_a_norm_g")[0]); put(26, f("kv_a_norm_g")[0]); put(27, f("attn_out_norm_g")[0])
    put(31, f("ssd_norm_g")[0]); put(35, f("conv_b")[0])
    cw = f("conv_w")[0]
    for cc in range(8):
        for k in range(5):
            gv[:, 43 + cc * 5 + k] = cw[k, cc * 128:(cc + 1) * 128]
    rowv = np.concatenate([f("dt_bias")[0].reshape(-1), f("a_log")[0].reshape(-1), f("d_skip")[0].reshape(-1)])[None, :]
    cm = _consts()
    LP = xp.shape[1]
    own = cfg.lp_own
    maps = []
    rs = _rope_tab(np.arange(cfg.ls))
    for c in range(ncores):
        b, q = divmod(c, cores_per_seq)
        xT = np.zeros((D, LP + 4), np.float32)
        xT[:, 2:LP + 2] = xp[b].T
        o0 = q * own
        xo = np.ascontiguousarray(xT[:, o0:o0 + own + 4])
        ntl = LP // TT
        ot0, ot1 = o0 // TT, (o0 + own) // TT
        order = list(range(0, ot0)) + list(range(ntl - 1, ot1 - 1, -1))
        nn = max(len(order), 1)
        xn = np.zeros((nn, D, TW), np.float32)
        mk = np.zeros((128, nn * 2), np.float32)
        rpn = np.zeros((nn, 2, 32, TT), np.float32)
        for s, t in enumerate(order):
            xn[s] = xT[:, t * TT:t * TT + TW]
            mk[:, 2 * s] = 1.0 if t < ot0 else 0.0
            mk[:, 2 * s + 1] = 1.0 if t >= ot1 else 0.0
            rpn[s] = _rope_tab(np.arange(t * TT, (t + 1) * TT))
        rpo = _rope_tab(np.arange(o0, o0 + own))
        xsp = np.zeros((cfg.ns, D, cfg.ls + 4), np.float32)
        for i in range(cfg.ns):
            xsp[i, :, 2:cfg.ls + 2] = xsm[c * cfg.ns + i].T
        maps.append(dict(xo=xo, xn=xn, mk=mk, rpo=rpo, rpn=rpn, rs=rs, xs=xsp,
                         w_in=f("w_in")[0], w_q_b=f("w_q_b")[0], w_kv_b=f("w_kv_b")[0], w_out=f("w_out")[0],
                         w_gate=f("w_gate")[0], w_up=f("w_up")[0], w_down=f("w_down")[0],
                         gv=gv, rowv=rowv.astype(np.float32), cm=cm))
    return maps


def run(inp, cfg, ncores, cores_per_seq):
    nc = build(cfg)
    maps = make_in_maps(inp, cfg, ncores, cores_per_seq)
    res = run_bass_kernel_spmd(nc, maps, core_ids=list(range(ncores)))
    xp = np.asarray(inp["x_prompt"]); xsm = np.asarray(inp["x_sample"])
    yp = np.zeros(xp.shape, np.float32)
    ysm = np.zeros(xsm.shape, np.float32)
    for c in range(ncores):
        r = res.results[c]
        b, q = divmod(c, cores_per_seq)
        yp[b, q * cfg.lp_own:(q + 1) * cfg.lp_own, :] = r["yo"].T
        for i in range(cfg.ns):
            ysm[c * cfg.ns + i] = r["ys"][i].T
    return yp, ysm


def kernel(**inputs):
    cfg = Cfg()
    return run(inputs, cfg, 8, 4)
```

```python
import numpy as np
import concourse.bass as bass
import concourse.mybir as mybir

F32 = mybir.dt.float32
BF16 = mybir.dt.bfloat16
AF = mybir.ActivationFunctionType
ALU = mybir.AluOpType
AX = mybir.AxisListType


class Stream:
    def __init__(self, sem):
        self.sem = sem
        self.val = 0


class Prog:
    ENG = ("pe", "act", "dve", "pool")

    def __init__(self, nc, es):
        self.nc = nc
        self.es = es
        self.eng = {"pe": nc.tensor, "act": nc.scalar, "dve": nc.vector,
                    "pool": nc.gpsimd, "sp": nc.sync}
        self.sem = {e: es.enter_context(nc.semaphore("c_" + e)) for e in self.ENG}
        self.cnt = {e: 0 for e in self.ENG}
        self.waited = {e: {} for e in ("pe", "act", "dve", "pool", "sp")}
        self.semobj = {"c_" + e: self.sem[e] for e in self.ENG}
        self.tab = {}
        self.streams = []
        self._snames = {}
        self.nops = 0

    NDS = 24

    def stream(self, name):
        return None

    def _dma_sem(self):
        if not self.streams:
            for i in range(self.NDS):
                st = Stream(self.es.enter_context(self.nc.semaphore("d_%d" % i)))
                st.name = "d_%d" % i
                self.semobj[st.name] = st.sem
                self.streams.append(st)
            self._dsi = 0
        st = self.streams[self._dsi % self.NDS]
        self._dsi += 1
        return st

    def _entries(self, buf, key):
        t = self.tab.setdefault(buf, {})
        if key is None:
            return list(t.values())
        out = []
        if key in t:
            out.append(t[key])
        if None in t:
            out.append(t[None])
        return out

    def _deps(self, eng, reads, writes):
        deps = []
        for ap, key in reads:
            for ent in self._entries(ap.tensor.name, key):
                if ent[0] is not None:
                    deps.append((ent[0], True))
        for ap, key in writes:
            for ent in self._entries(ap.tensor.name, key):
                if ent[0] is not None:
                    deps.append((ent[0], False))
                for tok in ent[1].values():
                    deps.append((tok, False))
        return deps

    def _record(self, eng, tok, reads, writes):
        for ap, key in reads:
            t = self.tab.setdefault(ap.tensor.name, {})
            ent = t.setdefault(key, [None, {}])
            ent[1][eng] = tok
        for ap, key in writes:
            t = self.tab.setdefault(ap.tensor.name, {})
            if key is None:
                t.clear()
                t[None] = [tok, {}]
            else:
                t[key] = [tok, {}]

    @staticmethod
    def _autokey(ap):
        if not ap.tensor.name.startswith("pbd"):
            return None
        a = ap.ap
        c0 = ap.offset % a[0][0]
        c1 = c0 + sum((cnt - 1) * st for st, cnt in a[1:]) + 1
        if c1 <= 512:
            return 0
        if c0 >= 512:
            return 1
        return None

    def _norm(self, lst):
        out = []
        for x in lst:
            if not isinstance(x, tuple):
                x = (x, None)
            if x[1] is None:
                x = (x[0], self._autokey(x[0]))
            out.append(x)
        return out

    def _do_waits(self, eng, deps):
        need = {}
        for (semname, val, peng), raw in deps:
            if peng == eng:
                if eng not in ("act", "dve", "pool"):
                    continue
            if need.get(semname, 0) < val:
                need[semname] = val
        w = self.waited[eng]
        e = self.eng[eng]
        for semname, val in need.items():
            if w.get(semname, 0) >= val:
                continue
            w[semname] = val
            e.wait_ge(self.semobj[semname], val)

    def op(self, eng, fn, reads=(), writes=()):
        reads = self._norm(reads)
        writes = self._norm(writes)
        deps = self._deps(eng, reads, writes)
        self._do_waits(eng, deps)
        ins = fn(self.eng[eng])
        self.cnt[eng] += 1
        ins.then_inc(self.sem[eng], 1)
        tok = ("c_" + eng, self.cnt[eng], eng)
        self._record(eng, tok, reads, writes)
        self.nops += 1
        return tok

    def dma(self, out, in_, stream=None, rk=None, wk=None, q="sp", **kw):
        reads = [(in_, rk)]
        writes = [(out, wk)]
        st = self._dma_sem()
        deps = self._deps(q, reads, writes)
        if st.val:
            deps.append(((st.name, st.val, "dma"), False))
        self._do_waits(q, deps)
        ins = self.eng[q].dma_start(out=out, in_=in_, **kw)
        st.val += 16
        ins.then_inc(st.sem, 16)
        tok = (st.name, st.val, "dma")
        self._record("dma:" + st.name, tok, reads, writes)
        self.nops += 1
        return tok

    def finish(self):
        sp = self.eng["sp"]
        for s in self.streams:
            if s.val:
                sp.wait_ge(s.sem, s.val)
        for e in self.ENG:
            if self.cnt[e]:
                sp.wait_ge(self.sem[e], self.cnt[e])

    def barrier(self):
        for e in ("pe", "act", "dve", "pool", "sp"):
            h = self.eng[e]
            w = self.waited[e]
            for p in self.ENG:
                if p == e or self.cnt[p] == 0:
                    continue
                nm = "c_" + p
                if w.get(nm, 0) < self.cnt[p]:
                    w[nm] = self.cnt[p]
                    h.wait_ge(self.sem[p], self.cnt[p])
            for s in self.streams:
                if s.val and w.get(s.name, 0) < s.val:
                    w[s.name] = s.val
                    h.wait_ge(s.sem, s.val)
        self.tab.clear()

    def mm(self, out, lhsT, rhs, start=True, stop=True, ko=None, kl=None, kr=None, **kw):
        return self.op("pe", lambda e: e.matmul(out, lhsT, rhs, start=start, stop=stop, **kw),
                       reads=[(lhsT, kl), (rhs, kr)], writes=[(out, ko)])

    def tr(self, out, in_, ident, ko=None, ki=None):
        return self.op("pe", lambda e: e.transpose(out, in_, ident),
                       reads=[(in_, ki), (ident, None)], writes=[(out, ko)])

    def act(self, out, in_, func, bias=None, scale=None, accum_out=None, ko=None, ki=None,
            eng="act", extra_reads=()):
        kw = {}
        reads = [(in_, ki)] + list(extra_reads)
        if bias is not None:
            kw["bias"] = bias
            if not isinstance(bias, (int, float)):
                reads.append((bias, None))
        if scale is not None:
            kw["scale"] = scale
            if not isinstance(scale, (int, float)):
                reads.append((scale, None))
        writes = [(out, ko)]
        if accum_out is not None:
            kw["accum_out"] = accum_out
            writes.append((accum_out, None))
        return self.op("act", lambda e: e.activation(out, in_, func, **kw), reads=reads, writes=writes)

    def tt(self, out, in0, in1, op, eng="dve", ko=None, k0=None, k1=None):
        return self.op(eng, lambda e: e.tensor_tensor(out, in0, in1, op),
                       reads=[(in0, k0), (in1, k1)], writes=[(out, ko)])

    def ts(self, out, in0, s1, s2=None, op0=ALU.mult, op1=None, eng="dve", ko=None, k0=None,
           accum_out=None):
        reads = [(in0, k0)]
        if not isinstance(s1, (int, float)):
            reads.append((s1, None))
        if s2 is not None and not isinstance(s2, (int, float)):
            reads.append((s2, None))
        kw = {}
        if op1 is not None:
            kw["op1"] = op1
        writes = [(out, ko)]
        if accum_out is not None:
            kw["accum_out"] = accum_out
            writes.append((accum_out, None))
        return self.op(eng, lambda e: e.tensor_scalar(out, in0, s1, s2, op0, **kw),
                       reads=reads, writes=writes)

    def stt(self, out, in0, scalar, in1, op0, op1, eng="dve", ko=None, k0=None, k1=None):
        reads = [(in0, k0), (in1, k1)]
        if not isinstance(scalar, (int, float)):
            reads.append((scalar, None))
        return self.op(eng, lambda e: e.scalar_tensor_tensor(out, in0, scalar, in1, op0, op1),
                       reads=reads, writes=[(out, ko)])

    def copy(self, out, in_, eng="dve", ko=None, ki=None):
        if eng == "act":
            return self.act(out, in_, AF.Copy, ko=ko, ki=ki)
        return self.op(eng, lambda e: e.tensor_copy(out, in_), reads=[(in_, ki)], writes=[(out, ko)])

    def memset(self, ap, val, eng="dve", k=None):
        return self.op(eng, lambda e: e.memset(ap, val), writes=[(ap, k)])

    def recip(self, out, in_, ko=None, ki=None):
        return self.op("dve", lambda e: e.reciprocal(out, in_), reads=[(in_, ki)], writes=[(out, ko)])
from concourse.bass_utils import run_bass_kernel_spmd
from contextlib import ExitStack

EPS = 1e-6
D = 1024
KD = 8
TT = 512
HALO = 2
TW = TT + 2 * HALO
QSCALE = 96 ** -0.5


class Cfg:
    def __init__(self, lp_own=4096, nnon=24, ls=2048, ns=4, dff=2816):
        self.lp_own, self.nnon, self.ls, self.ns, self.dff = lp_own, nnon, ls, ns, dff
        self.nfb = dff // 128
        self.ng = dff // 256
        self.lmax = max(lp_own, ls)


def build(cfg):
    nc = bass.Bass("TRN2", target_bir_lowering=False)
    c = cfg
    NFB, NG = c.nfb, c.ng

    def din(name, shape, dt=F32):
        return nc.dram_tensor(name, list(shape), dt, kind="ExternalInput").ap()

    xo = din("xo", [D, max(c.lp_own, TT) + 4])
    xn = din("xn", [max(c.nnon, 1), D, TW])
    mk = din("mk", [128, max(c.nnon, 1) * 2])
    rpo = din("rpo", [2, 32, max(c.lp_own, TT)])
    rpn = din("rpn", [max(c.nnon, 1), 2, 32, TT])
    rs = din("rs", [2, 32, c.ls])
    xs = din("xs", [max(c.ns, 1), D, c.ls + 4])
    w_in = din("w_in", [D, 1968])
    w_q_b = din("w_q_b", [256, 768])
    w_kv_b = din("w_kv_b", [128, 1024])
    w_out = din("w_out", [D, D])
    w_gate = din("w_gate", [D, c.dff])
    w_up = din("w_up", [D, c.dff])
    w_down = din("w_down", [c.dff, D])
    gv = din("gv", [128, 88])
    rowv = din("rowv", [1, 40])
    cm = din("cm", [128, 6, 128])
    yo = nc.dram_tensor("yo", [D, max(c.lp_own, TT)], F32, kind="ExternalOutput").ap()
    ys = nc.dram_tensor("ys", [max(c.ns, 1), D, c.ls], F32, kind="ExternalOutput").ap()
    wg_s = nc.dram_tensor("wg_s", [NG, 128, KD, 256], BF16, kind="ExternalOutput").ap()
    wu_s = nc.dram_tensor("wu_s", [NG, 128, KD, 256], BF16, kind="ExternalOutput").ap()
    wd_s = nc.dram_tensor("wd_s", [8, 128, NFB, 128], BF16, kind="ExternalOutput").ap()

    with ExitStack() as es:
        P = Prog(nc, es)

        uid = {"n": 0}

        def sbuf(stack, name, shape, dt=F32):
            uid["n"] += 1
            return stack.enter_context(nc.sbuf_tensor("%s_%d" % (name, uid["n"]), list(shape), dt))

        pbd = [es.enter_context(nc.psum_tensor(f"pbd{i}", [128, 1024], F32)) for i in range(4)]

        class Bank:
            def __init__(self, t, half):
                self.t, self.h = t, half

            def __getitem__(self, idx):
                if not isinstance(idx, tuple):
                    idx = (idx,)
                cs = idx[1] if len(idx) > 1 else slice(None)
                a = (cs.start or 0) + self.h * 512
                b = (cs.stop if cs.stop is not None else 512) + self.h * 512
                return self.t[idx[0], a:b]

        pb = [Bank(pbd[i // 2], i % 2) for i in range(8)]
        psrr = {"i": 0}

        def nps(lo=0, hi=8):
            k = "%d_%d" % (lo, hi)
            i = psrr.get(k, lo)
            psrr[k] = lo + ((i - lo + 1) % (hi - lo))
            return pb[i]

        G = {}
        gvt = sbuf(es, "gvt", [128, 88])
        rowt = sbuf(es, "rowt", [128, 40])
        cmt = sbuf(es, "cmt", [128, 6, 128])
        mneg = sbuf(es, "mneg", [128, 2, 128], BF16)
        identf = sbuf(es, "identf", [128, 128])
        identb = sbuf(es, "identb", [128, 128], BF16)
        onesb = sbuf(es, "onesb", [128, 128], BF16)
        onesf = sbuf(es, "onesf", [128, 128])
        abt = sbuf(es, "abt", [128, 16])
        sq2 = [sbuf(es, f"sq{i}", [128, TT], BF16) for i in range(3)]
        lnb = sbuf(es, "lnb", [128, TT])
        rr = {"sq": 0}

        st_c = P.stream("const")
        P.dma(gvt[:], gv[:, :], st_c)
        P.dma(rowt[:], rowv[0:1, :].partition_broadcast(128), st_c)
        P.dma(cmt[:], cm[:, :, :], st_c)
        P.copy(mneg[:], cmt[:, 4:6, :], eng="dve")
        P.memset(identf[:], 1.0, eng="pool")
        P.op("pool", lambda e: e.affine_select(identf[:], identf[:], pattern=[[-1, 128]],
                                               compare_op=ALU.is_equal, fill=0.0, base=0,
                                               channel_multiplier=1),
             reads=[identf[:]], writes=[identf[:]])
        P.copy(identb[:], identf[:], eng="dve")
        P.memset(onesb[:], 1.0, eng="dve")
        P.memset(onesf[:], 1.0, eng="dve")
        P.act(abt[:], rowt[:, 16:32], AF.Exp)
        P.ts(abt[:], abt[:], -1.0, None, op0=ALU.mult)
        G_ATTN, G_FFN, G_FIN, G_QA, G_KVA, G_AO, G_SSD, G_CB, G_CW = 0, 8, 16, 24, 26, 27, 31, 35, 43
        dtb = rowt[:, 0:16]
        dsk = rowt[:, 32:40]

        def rstd_fm(srcs, Dn, N, out_ap):
            ps = nps(0, 6)
            for i, s in enumerate(srcs):
                sq = sq2[rr["sq"] % 3]
                rr["sq"] += 1
                P.act(sq[:, :N], s, AF.Square)
                P.mm(ps[:, :N], onesb[:], sq[:, :N], start=(i == 0), stop=(i == len(srcs) - 1))
            P.act(lnb[:, :N], ps[:, :N], AF.Ln, bias=EPS, scale=1.0 / Dn)
            P.act(out_ap, lnb[:, :N], AF.Exp, scale=-0.5)

        stg = [sbuf(es, f"stg{i}", [128, KD, 128]) for i in range(2)]
        st_w = [P.stream("w0"), P.stream("w1")]
        wrr = {"i": 0}

        def prep_w(dst_fn, src, k_chunks, c0, ncols, gcol, scale=1.0):
            for a in range(0, ncols, 128):
                b = min(a + 128, ncols)
                i = wrr["i"] % 2
                wrr["i"] += 1
                P.dma(stg[i][:, 0:k_chunks, 0:b - a],
                      src[0:k_chunks * 128, c0 + a:c0 + b].rearrange("(k p) n -> p k n", p=128), st_w[i])
                for kc in range(k_chunks):
                    o = dst_fn(kc, a, b)
                    P.ts(o, stg[i][:, kc, 0:b - a], gvt[:, gcol + kc:gcol + kc + 1], None, op0=ALU.mult)

        with ExitStack() as ph:
            wtmp = [sbuf(ph, f"wtmp{i}", [128, KD, 256], BF16) for i in range(2)]
            wdt = [sbuf(ph, f"wdt{i}", [128, NFB, 128], BF16) for i in range(2)]
            wdf = [sbuf(ph, f"wdf{i}", [128, NFB, 128]) for i in range(2)]
            st_o = [P.stream("wo0"), P.stream("wo1")]
            n = 0
            for (src, dst) in ((w_gate, wg_s), (w_up, wu_s)):
                for g in range(NG):
                    t = wtmp[n % 2]
                    prep_w(lambda kc, a, b, t=t: t[:, kc, a:b], src, KD, g * 256, 256, G_FFN)
                    P.dma(dst[g, :, :, :], t[:], st_o[n % 2])
                    n += 1
            for ob in range(8):
                i = ob % 2
                P.dma(wdf[i][:], w_down[:, ob * 128:(ob + 1) * 128].rearrange("(k p) n -> p k n", p=128),
                      st_w[i])
                P.copy(wdt[i][:], wdf[i][:], eng="act" if ob % 2 else "dve")
                P.dma(wd_s[ob, :, :, :], wdt[i][:], st_o[i])
            P.barrier()

        def load_x(dst, src_cols, stream):
            P.dma(dst, src_cols.rearrange("(k p) n -> p k n", p=128), stream)

        def norm_tile(xt, hn, rst, N):
            pieces = [(0, min(N, TT))] + ([(TT, N)] if N > TT else [])
            for (a, b) in pieces:
                rstd_fm([xt[:, kc, a:b] for kc in range(KD)], D, b - a, rst[:, a:b])
            for kc in range(KD):
                P.tt(hn[:, kc, 0:N], xt[:, kc, 0:N], rst[:, 0:N], ALU.mult,
                     eng="pool" if kc % 2 else "dve")

        def mla_phase(own_src, lown, non_src, nnon, rope_own, rope_non):
            nto = lown // TT
            ltot = lown + nnon * TT
            ntt = ltot // TT
            with ExitStack() as ph:
                wqb = sbuf(ph, "wqb", [128, 2, 768], BF16)
                wqr = sbuf(ph, "wqr", [128, 2, 8, 96], BF16)
                wkvb = sbuf(ph, "wkvb", [128, 1024], BF16)
                ckvn = sbuf(ph, "ckvn", [128, ltot], BF16)
                KT = sbuf(ph, "KT", [96, ltot], BF16)
                qlatn = sbuf(ph, "qlatn", [128, 2, lown], BF16)
                rst2 = sbuf(ph, "mrst2", [128, TT])
                rtab = [sbuf(ph, f"rtab{i}", [96, 2, TT]) for i in range(2)]
                t1 = sbuf(ph, "mt1", [96, TT])
                t2 = sbuf(ph, "mt2", [96, TT])
                ph1 = ExitStack()
                w_mla = sbuf(ph1, "w_mla", [128, KD, 576], BF16)
                xt2 = [sbuf(ph1, f"mxt{i}", [128, KD, TT]) for i in range(2)]
                hn = sbuf(ph1, "mhn", [128, KD, TT], BF16)
                rst = sbuf(ph1, "mrst", [128, TT])
                st_x = [P.stream("mx0"), P.stream("mx1")]
                st_r = [P.stream("mr0"), P.stream("mr1")]

                prep_w(lambda kc, a, b: w_mla[:, kc, a:b], w_in, KD, 0, 384, G_ATTN)
                P.memset(w_mla[:, :, 384:576], 0.0, eng="pool")
                prep_w(lambda kc, a, b: w_mla[:, kc, 448 + a:448 + b], w_in, KD, 384, 32, G_ATTN)
                for kc in range(KD):
                    P.ts(w_mla[:, kc, 544:560], w_mla[:, kc, 464:480], -1.0, None, op0=ALU.mult)
                    P.copy(w_mla[:, kc, 560:576], w_mla[:, kc, 448:464], eng="pool")
                prep_w(lambda kc, a, b: wqb[:, kc, a:b], w_q_b, 2, 0, 768, G_QA)
                P.memset(wqr[:], 0.0, eng="pool")
                for kc in range(2):
                    for h in range(8):
                        P.ts(wqr[:, kc, h, 64:80], wqb[:, kc, h * 96 + 80:h * 96 + 96], -1.0, None,
                             op0=ALU.mult)
                        P.copy(wqr[:, kc, h, 80:96], wqb[:, kc, h * 96 + 64:h * 96 + 80], eng="pool")
                prep_w(lambda kc, a, b: wkvb[:, a:b], w_kv_b, 1, 0, 1024, G_KVA)

                def tile_src(t):
                    if t < nto:
                        return own_src[:, HALO + t * TT:HALO + (t + 1) * TT], \
                            rope_own[:, :, t * TT:(t + 1) * TT]
                    s = t - nto
                    return non_src[s, :, HALO:HALO + TT], rope_non[s, :, :, :]

                def issue_load(t):
                    xs_, rp_ = tile_src(t)
                    load_x(xt2[t % 2][:], xs_, st_x[t % 2])
                    P.dma(rtab[t % 2][64:96, :, :], rp_.rearrange("a p n -> p a n"), st_r[t % 2])

                issue_load(0)
                for t in range(ntt):
                    if t + 1 < ntt:
                        issue_load(t + 1)
                    xt = xt2[t % 2]
                    rt = rtab[t % 2]
                    norm_tile(xt, hn, rst, TT)
                    cols = slice(t * TT, (t + 1) * TT)
                    pc = nps(0, 6)
                    for kc in range(KD):
                        P.mm(pc[:], w_mla[:, kc, 256:384], hn[:, kc, :], start=(kc == 0), stop=(kc == KD - 1))
                    rstd_fm([pc[:]], 128, TT, rst2[:])
                    P.tt(ckvn[:, cols], pc[:], rst2[:], ALU.mult, ko=t)
                    pa = nps(0, 6)
                    pr = nps(0, 6)
                    for kc in range(KD):
                        P.mm(pa[0:96, :], w_mla[:, kc, 384:480], hn[:, kc, :], start=(kc == 0), stop=(kc == KD - 1))
                    for kc in range(KD):
                        P.mm(pr[0:96, :], w_mla[:, kc, 480:576], hn[:, kc, :], start=(kc == 0), stop=(kc == KD - 1))
                    P.tt(t1[64:96, :], pa[64:96, :], rt[64:96, 0, :], ALU.mult)
                    P.tt(t2[64:96, :], pr[64:96, :], rt[64:96, 1, :], ALU.mult)
                    P.tt(KT[64:96, cols], t1[64:96, :], t2[64:96, :], ALU.add, ko=("r", t))
                    if t < nto:
                        pq = [nps(0, 6), nps(0, 6)]
                        for cq in range(2):
                            for kc in range(KD):
                                P.mm(pq[cq][:], w_mla[:, kc, cq * 128:(cq + 1) * 128], hn[:, kc, :],
                                     start=(kc == 0), stop=(kc == KD - 1))
                        rstd_fm([pq[0][:], pq[1][:]], 256, TT, rst2[:])
                        for cq in range(2):
                            P.tt(qlatn[:, cq, cols], pq[cq][:], rst2[:], ALU.mult, ko=t)

                P.barrier()
                ph1.close()
                VH = sbuf(ph, "VH", [128, ltot // 128, 65], BF16)
                QH = sbuf(ph, "QH", [96, lown], BF16)
                PT = [sbuf(ph, f"PT{i}", [128, 2 * TT], BF16) for i in range(4)]
                osb2 = [sbuf(ph, f"osb{i}", [64, TT]) for i in range(2)]
                rc2 = [sbuf(ph, f"rc{i}", [65, TT]) for i in range(2)]
                fin = {"f": None, "n": 0}
                pbfin = pb[5]
                P.memset(VH[:, :, 64:65], 1.0, eng="pool")
                nkb = ltot // 128
                for h in range(8):
                    for t in range(ntt):
                        cols = slice(t * TT, (t + 1) * TT)
                        pk = nps(0, 6)
                        P.mm(pk[0:64, :], wkvb[:, h * 128:h * 128 + 64], ckvn[:, cols], kr=t)
                        P.copy(KT[0:64, cols], pk[0:64, :], eng="act" if t % 2 else "dve", ko=("n", t))
                    for g8 in range(0, nkb, 8):
                        pv = nps(0, 6)
                        for j in range(8):
                            kb = g8 + j
                            P.mm(pv[:, j * 64:(j + 1) * 64], ckvn[:, kb * 128:(kb + 1) * 128],
                                 wkvb[:, h * 128 + 64:h * 128 + 128], kl=kb // 4)
                        P.copy(VH[:, g8:g8 + 8, 0:64], pv[:, :].rearrange("p (j d) -> p j d", j=8),
                               eng="dve" if (g8 // 8) % 2 else "act", ko=g8 // 8)
                    for t in range(nto):
                        cols = slice(t * TT, (t + 1) * TT)
                        rt = rtab[t % 2]
                        P.dma(rt[64:96, :, :], rope_own[:, :, cols].rearrange("a p n -> p a n"), st_r[t % 2])
                        pa = nps(0, 6)
                        pr = nps(0, 6)
                        for kc in range(2):
                            P.mm(pa[0:96, :], wqb[:, kc, h * 96:(h + 1) * 96], qlatn[:, kc, cols],
                                 start=(kc == 0), stop=(kc == 1), kr=t)
                        for kc in range(2):
                            P.mm(pr[0:96, :], wqr[:, kc, h, :], qlatn[:, kc, cols],
                                 start=(kc == 0), stop=(kc == 1), kr=t)
                        P.copy(QH[0:64, cols], pa[0:64, :], eng="act", ko=("n", t))
                        P.tt(t1[64:96, :], pa[64:96, :], rt[64:96, 0, :], ALU.mult)
                        P.tt(t2[64:96, :], pr[64:96, :], rt[64:96, 1, :], ALU.mult)
                        P.tt(QH[64:96, cols], t1[64:96, :], t2[64:96, :], ALU.add, ko=("r", t))
                    for t in range(nto):
                        cols = slice(t * TT, (t + 1) * TT)
                        po = pb[6 + (t % 2)]
                        LOOK = 2
                        npair = nkb // 2
                        for idx in range(npair + LOOK):
                            if idx < npair:
                                pd_ = pbd[idx % 3]
                                for j in range(2):
                                    kb = idx * 2 + j
                                    tk = kb // 4
                                    P.op("pe", lambda e, pd_=pd_, kb=kb, j=j, cols=cols: e.matmul(
                                        pd_[:, j * 512:(j + 1) * 512], KT[0:96, kb * 128:(kb + 1) * 128],
                                        QH[0:96, cols], start=True, stop=True),
                                        reads=[(KT[:], ("n", tk)), (KT[:], ("r", tk)), (QH[:], ("n", t)),
                                               (QH[:], ("r", t))],
                                        writes=[(pd_[:, j * 512:(j + 1) * 512], None)])
                                P.act(PT[idx % 4][:], pd_[:, :], AF.Exp, scale=QSCALE)
                            if idx == LOOK and fin["f"] is not None:
                                fin["f"]()
                                fin["f"] = None
                            if idx >= LOOK:
                                pi = idx - LOOK
                                for j in range(2):
                                    kb = pi * 2 + j
                                    P.mm(po[0:65, :], VH[:, kb, :], PT[pi % 4][:, j * 512:(j + 1) * 512],
                                         start=(kb == 0), stop=(kb == nkb - 1), kl=kb // 8)
                        ob_ = osb2[fin["n"] % 2]
                        rc_ = rc2[fin["n"] % 2]
                        fin["n"] += 1
                        P.act(rc_[64:65, :], po[64:65, :], AF.Ln)
                        P.act(rc_[64:65, :], rc_[64:65, :], AF.Exp, scale=-1.0)
                        P.copy(ob_[0:64, :], po[0:64, :], eng="dve")

                        def finalize(ob_=ob_, rc_=rc_, h=h, cols=cols, t=t):
                            pbc = pbfin
                            P.mm(pbc[0:64, :], onesf[64:65, 0:64], rc_[64:65, :])
                            P.tt(G["mlaT"][(h % 2) * 64:(h % 2) * 64 + 64, h // 2, cols], ob_[0:64, :],
                                 pbc[0:64, :], ALU.mult, ko=(h, t))
                        fin["f"] = finalize
                if fin["f"] is not None:
                    fin["f"]()
                    fin["f"] = None
                for t in range(nto):
                    cols = slice(t * TT, (t + 1) * TT)
                    rstd_fm([G["mlaT"][:, cc, cols] for cc in range(4)], 512, TT, rst2[:])
                    for cc in range(4):
                        P.tt(G["mlaT"][:, cc, cols], G["mlaT"][:, cc, cols], rst2[:], ALU.mult,
                             eng="pool" if cc % 2 else "dve")
                P.barrier()

        def ssd_phase(own_src, own_t0, nto, slots, out_t0):
            nch = (nto if nto <= 4 else (nto + 1) // 2) * 4
            with ExitStack() as ph:
                w_ssd = sbuf(ph, "w_ssd", [128, KD, 1552], BF16)
                Sst = [sbuf(ph, f"Sst{d}", [128, 512]) for d in range(2)]
                SBW = sbuf(ph, "SBW", [128, nch, 512], BF16)
                xtb = [sbuf(ph, f"sxt{i}", [128, KD, TW]) for i in range(1)]
                dgw = sbuf(ph, "dgw", [128, 8, 5, 128], BF16)
                preb = [sbuf(ph, f"preb{i}", [128, TW], BF16) for i in range(2)]
                hn = sbuf(ph, "shn", [128, KD, TW], BF16)
                rst = sbuf(ph, "srst", [128, TW])
                xact = sbuf(ph, "xact", [128, 8, TT], BF16)
                x_tok = sbuf(ph, "x_tok", [128, 4, 512], BF16)
                B_tok = sbuf(ph, "B_tok", [128, 4, 256], BF16)
                z_tok = sbuf(ph, "z_tok", [128, 4, 512], BF16)
                dt_tok = sbuf(ph, "dt_tok", [128, 4, 16])
                dtm = sbuf(ph, "dtm", [128, 4, 16])
                sm = [sbuf(ph, f"sm{i}", [128, 16]) for i in range(16)]
                xw4 = sbuf(ph, "xw4", [128, 4, 512], BF16)
                dA4 = sbuf(ph, "dA4", [128, 4, 8])
                w4 = sbuf(ph, "w4", [128, 4, 8])
                ex4 = sbuf(ph, "ex4", [128, 64])
                Sfin = sbuf(ph, "Sfin", [128, 4, 512], BF16)
                Ssnap = sbuf(ph, "Ssnap", [128, 512])
                xdt = [sbuf(ph, f"xdt{d}", [128, 512], BF16) for d in range(2)]
                GT = sbuf(ph, "GT", [128, 2, 128], BF16)
                Dm = [sbuf(ph, f"Dm{i}", [128, 512], BF16) for i in range(2)]
                Mm = [sbuf(ph, f"Mm{i}", [128, 512], BF16) for i in range(2)]
                ya = sbuf(ph, "ya", [128, 512])
                yb = sbuf(ph, "yb", [128, 512])
                stok = sbuf(ph, "stok", [128, 512], BF16)
                st_x = P.stream("sx")
                smr = {"i": 0}

                def smn():
                    smr["i"] += 1
                    return sm[smr["i"] % 16]

                prep_w(lambda kc, a, b: w_ssd[:, kc, a:b], w_in, KD, 416, 1552, G_ATTN)
                P.memset(Sst[0][:], 0.0)
                P.memset(Sst[1][:], 0.0)
                for cc in range(8):
                    for k in range(5):
                        P.ts(dgw[:, cc, k, :], identf[:], gvt[:, G_CW + cc * 5 + k:G_CW + cc * 5 + k + 1], None,
                             op0=ALU.mult)
                tl = {"i": 0, "q": []}

                def prefetch(src):
                    i = tl["i"]
                    tl["i"] += 1
                    load_x(xtb[0][:], src, st_x)
                    tl["q"].append(xtb[0])

                def tile_front(src, full):
                    xt = tl["q"].pop(0)
                    import os
                    dbg = os.environ.get("SSD_DBG", "z")
                    if dbg == "0":
                        return
                    norm_tile(xt, hn, rst, TW)
                    if tl["next"] is not None:
                        prefetch(tl["next"])
                        tl["next"] = None
                    if dbg == "a":
                        return
                    ncc = 8 if full else 6
                    for cc in range(ncc):
                        pm = nps(0, 6)
                        ph_ = nps(0, 6)
                        col = 512 + cc * 128
                        for kc in range(KD):
                            P.mm(pm[:], w_ssd[:, kc, col:col + 128], hn[:, kc, 0:TT],
                                 start=(kc == 0), stop=(kc == KD - 1))
                        for kc in range(KD):
                            P.mm(ph_[:, 0:4], w_ssd[:, kc, col:col + 128], hn[:, kc, TT:TW],
                                 start=(kc == 0), stop=(kc == KD - 1))
                        pr_ = preb[cc % 2]
                        P.copy(pr_[:, 0:TT], pm[:], eng="act")
                        P.copy(pr_[:, TT:TW], ph_[:, 0:4], eng="dve")
                        pcv = nps(0, 6)
                        for k in range(5):
                            P.mm(pcv[:], dgw[:, cc, k, :], pr_[:, k:k + TT], start=(k == 0), stop=(k == 4))
                        P.act(xact[:, cc, :], pcv[:], AF.Silu, bias=gvt[:, G_CB + cc:G_CB + cc + 1])
                    if dbg == "b":
                        return
                    for tb in range(4):
                        pd = nps(0, 6)
                        for kc in range(KD):
                            P.mm(pd[:, 0:16], hn[:, kc, HALO + tb * 128:HALO + (tb + 1) * 128],
                                 w_ssd[:, kc, 1536:1552], start=(kc == 0), stop=(kc == KD - 1))
                        v = smn()
                        P.tt(v[:], pd[:, 0:16], dtb, ALU.add)
                        a_ = smn()
                        P.act(a_[:], v[:], AF.Abs)
                        e_ = smn()
                        P.act(e_[:], a_[:], AF.Exp, scale=-1.0)
                        l_ = smn()
                        P.act(l_[:], e_[:], AF.Ln, bias=1.0)
                        P.ts(v[:], v[:], 0.0, None, op0=ALU.max)
                        P.tt(dt_tok[:, tb, :], v[:], l_[:], ALU.add, ko=tb)
                    if full:
                        for tb in range(4):
                            pz = nps(0, 6)
                            for kc in range(KD):
                                P.mm(pz[:], hn[:, kc, HALO + tb * 128:HALO + (tb + 1) * 128],
                                     w_ssd[:, kc, 0:512], start=(kc == 0), stop=(kc == KD - 1))
                            P.act(z_tok[:, tb, :], pz[:], AF.Silu, ko=tb)
                    if dbg == "c":
                        return
                    for tb in range(4):
                        pt_ = nps(0, 6)
                        pt2 = nps(0, 6)
                        for cc in range(4):
                            P.mm(pt_[:, cc * 128:(cc + 1) * 128], xact[:, cc, tb * 128:(tb + 1) * 128], identb[:])
                        for cc in range(2):
                            P.mm(pt2[:, cc * 128:(cc + 1) * 128], xact[:, 4 + cc, tb * 128:(tb + 1) * 128],
                                 identb[:])
                        P.copy(x_tok[:, tb, :], pt_[:, 0:512], eng="dve", ko=tb)
                        P.copy(B_tok[:, tb, :], pt2[:, 0:256], eng="act", ko=tb)

                def bc8(ap8):
                    return ap8.unsqueeze(2).to_broadcast([128, 8, 64])

                def v8(ap512):
                    return ap512.rearrange("p (h d) -> p h d", h=8)

                def su_batch(d, dsrc, order, save=None):
                    dv = dsrc[:, :, d * 8:(d + 1) * 8]
                    P.tt(dA4[:], dv, abt[:, d * 8:(d + 1) * 8].unsqueeze(1).to_broadcast([128, 4, 8]), ALU.mult)
                    pp = nps(0, 6)
                    dflat = dA4[:, :, :].rearrange("p c h -> p (c h)")
                    P.mm(pp[:, 0:32], cmt[:, 2 + d, :], dflat)
                    P.mm(pp[:, 32:64], onesf[:], dflat)
                    P.act(ex4[:], pp[:, 0:64], AF.Exp)
                    P.tt(w4[:], dv, ex4[:, 0:32].rearrange("p (c h) -> p c h", c=4), ALU.mult)
                    P.tt(xw4[:, :, :].rearrange("p c (h e) -> p c h e", h=8),
                         x_tok[:, :, :].rearrange("p c (h e) -> p c h e", h=8),
                         w4[:, :, :].unsqueeze(3).to_broadcast([128, 4, 8, 64]), ALU.mult)
                    S = Sst[d]
                    for tb in order:
                        pc = nps(0, 6)
                        for g in range(2):
                            P.mm(pc[:, g * 256:(g + 1) * 256], B_tok[:, tb, g * 128:(g + 1) * 128],
                                 xw4[:, tb, g * 256:(g + 1) * 256])
                        if save is not None:
                            P.copy(save(tb), S[:], eng="act")
                        P.tt(v8(S[:]), v8(S[:]), bc8(ex4[:, 32 + tb * 8:32 + tb * 8 + 8]), ALU.mult)
                        P.tt(S[:], S[:], pc[:], ALU.add)

                def full_chunk(tb, ci, out_cols):
                    pg = nps(0, 6)
                    for g in range(2):
                        P.mm(pg[:, g * 128:(g + 1) * 128], xact[:, 4 + g, tb * 128:(tb + 1) * 128],
                             xact[:, 6 + g, tb * 128:(tb + 1) * 128])
                    P.copy(GT[:], pg[:, 0:256].rearrange("p (g l) -> p g l", g=2), eng="act")
                    prel = []
                    for d in range(2):
                        dt8 = dt_tok[:, tb, d * 8:(d + 1) * 8]
                        dtA = smn()
                        P.tt(dtA[:, 0:8], dt8, abt[:, d * 8:(d + 1) * 8], ALU.mult, k0=tb)
                        pcs = nps(0, 6)
                        P.mm(pcs[:, 0:8], cmt[:, d, :], dtA[:, 0:8])
                        et = smn()
                        P.act(et[:, 0:8], pcs[:, 0:8], AF.Exp)
                        ncs = smn()
                        P.ts(ncs[:, 0:8], pcs[:, 0:8], -1.0, None, op0=ALU.mult)
                        P.tt(v8(xdt[d][:]), v8(x_tok[:, tb, :]), bc8(dt8), ALU.mult, k0=tb, k1=tb)
                        poff = nps(0, 6)
                        for g in range(2):
                            rhs = Sfin[:, tb, g * 256:(g + 1) * 256] if d == 0 else SBW[:, ci, g * 256:(g + 1) * 256]
                            P.mm(poff[:, g * 256:(g + 1) * 256], xact[:, 6 + g, tb * 128:(tb + 1) * 128], rhs,
                                 kr=(None if d == 0 else ci))
                        tgt = ya if d == 0 else yb
                        P.tt(v8(tgt[:]), v8(poff[:]), bc8(et[:, 0:8]), ALU.mult)
                        prel.append((dtA, ncs))
                    pdgs = [pb[6], pb[7]]
                    groups = [(d, g) for d in range(2) for g in range(2)]

                    def stageA(k):
                        d, g = groups[k]
                        dtA, ncs = prel[d]
                        pcb = nps(0, 6)
                        for j in range(4):
                            h = g * 4 + j
                            P.mm(pcb[:, j * 128:(j + 1) * 128], dtA[:, h:h + 1].to_broadcast([128, 128]),
                                 cmt[:, d, :], start=True, stop=False)
                            P.mm(pcb[:, j * 128:(j + 1) * 128], identb[:], mneg[:, d, :], start=False, stop=True)
                        Dq = Dm[k % 2]
                        for j in range(4):
                            h = g * 4 + j
                            P.act(Dq[:, j * 128:(j + 1) * 128], pcb[:, j * 128:(j + 1) * 128], AF.Exp,
                                  bias=ncs[:, h:h + 1])
                        P.tt(Mm[k % 2][:, :].rearrange("p (j l) -> p j l", j=4),
                             Dq[:, :].rearrange("p (j l) -> p j l", j=4),
                             GT[:, g, :].unsqueeze(1).to_broadcast([128, 4, 128]), ALU.mult)

                    def stageB(k):
                        d, g = groups[k]
                        for j in range(4):
                            h = g * 4 + j
                            P.mm(pdgs[d][:, h * 64:(h + 1) * 64], Mm[k % 2][:, j * 128:(j + 1) * 128],
                                 xdt[d][:, h * 64:(h + 1) * 64])

                    stageA(0)
                    stageA(1)
                    stageB(0)
                    stageA(2)
                    stageB(1)
                    stageA(3)
                    stageB(2)
                    stageB(3)
                    P.tt(ya[:], ya[:], pdgs[0][:], ALU.add)
                    P.tt(yb[:], yb[:], pdgs[1][:], ALU.add)
                    P.tt(ya[:], ya[:], yb[:], ALU.add)
                    P.tt(v8(yb[:]), v8(x_tok[:, tb, :]), bc8(dsk), ALU.mult, k0=tb)
                    P.tt(ya[:], ya[:], yb[:], ALU.add)
                    yg = yb
                    P.tt(yg[:], ya[:], z_tok[:, tb, :], ALU.mult, k1=tb)
                    ss = smn()
                    P.memset(ss[:, 0:1], 0.0)
                    P.act(stok[:], yg[:], AF.Square, accum_out=ss[:, 0:1])
                    l2 = smn()
                    P.act(l2[:, 0:1], ss[:, 0:1], AF.Ln, bias=EPS, scale=1.0 / 512)
                    r2 = smn()
                    P.act(r2[:, 0:1], l2[:, 0:1], AF.Exp, scale=-0.5)
                    P.ts(stok[:], yg[:], r2[:, 0:1], None, op0=ALU.mult)
                    pt_ = nps(0, 6)
                    for cc in range(4):
                        P.mm(pt_[:, cc * 128:(cc + 1) * 128], stok[:, cc * 128:(cc + 1) * 128], identb[:])
                    P.copy(G["ssdT"][:, :, out_cols], pt_[:, 0:512].rearrange("p (c n) -> p c n", c=4), eng="dve",
                           ko=ci)

                sstop = getattr(c, "stop", 0)
                def own_v(kind, t):
                    g0 = (own_t0 + t) * TT
                    return (kind, own_src[:, g0:g0 + TW], t)
                visits = [("slot", src, midx) for (src, midx, fon, bon) in slots]
                if nto <= 4:
                    parts = [(0, nto)]
                else:
                    parts = [(0, nto // 2), (nto // 2, nto)]
                for pi_, (ta, tb_) in enumerate(parts):
                    if len(parts) == 2 and pi_ == 0:
                        visits.append(("snap", None, None))
                        for t in range(nto - 1, tb_ - 1, -1):
                            visits.append(own_v("bwdns", t))
                    if len(parts) == 2 and pi_ == 1:
                        visits.append(("restore", None, None))
                    for t in range(tb_ - 1, ta - 1, -1):
                        visits.append(own_v("bwd", t))
                    for t in range(ta, tb_):
                        visits.append(own_v("full", t))
                tiles = [v for v in visits if v[1] is not None]
                prefetch(tiles[0][1])
                ti = 0
                for v in visits:
                    if v[0] == "snap":
                        P.copy(Ssnap[:], Sst[1][:], eng="dve")
                        continue
                    if v[0] == "restore":
                        P.copy(Sst[1][:], Ssnap[:], eng="dve")
                        continue
                    ti += 1
                    tl["next"] = tiles[ti][1] if ti < len(tiles) else None
                    if v[0] == "slot":
                        midx = v[2]
                        tile_front(v[1], False)
                        P.tt(dtm[:, :, :].rearrange("p c (d h) -> p c d h", d=2),
                             dt_tok[:, :, :].rearrange("p c (d h) -> p c d h", d=2),
                             mkt[:, midx * 2:midx * 2 + 2].unsqueeze(1).unsqueeze(3).to_broadcast([128, 4, 2, 8]),
                             ALU.mult)
                        su_batch(0, dtm, (0, 1, 2, 3))
                        su_batch(1, dtm, (3, 2, 1, 0))
                    elif v[0] == "bwdns":
                        tile_front(v[1], False)
                        su_batch(1, dt_tok, (3, 2, 1, 0))
                    elif v[0] == "bwd":
                        t = v[2]
                        tile_front(v[1], False)
                        lt = t - (parts[-1][0] if t >= parts[-1][0] and len(parts) == 2 else 0)
                        su_batch(1, dt_tok, (3, 2, 1, 0), save=lambda tb, lt=lt: SBW[:, lt * 4 + tb, :])
                    else:
                        t = v[2]
                        tile_front(v[1], True)
                        lt = t - (parts[-1][0] if t >= parts[-1][0] and len(parts) == 2 else 0)
                        su_batch(0, dt_tok, (0, 1, 2, 3), save=lambda tb: Sfin[:, tb, :])
                        for tb in range(4):
                            oc = (out_t0 + t) * TT + tb * 128
                            full_chunk(tb, lt * 4 + tb, slice(oc, oc + 128))
                P.barrier()

        def ffn_phase(own_src, lown, out_dst):
            nto = lown // TT
            with ExitStack() as ph:
                wout = sbuf(ph, "wout", [128, KD, D], BF16)
                xt2 = [sbuf(ph, f"fxt{i}", [128, KD, TT]) for i in range(2)]
                h2 = sbuf(ph, "h2", [128, KD, TT], BF16)
                rst = sbuf(ph, "frst", [128, TT])
                actT = sbuf(ph, "actT", [128, NFB, TT], BF16)
                sg = [sbuf(ph, f"sg{i}", [128, TT]) for i in range(2)]
                wg = [sbuf(ph, f"wg{i}", [128, KD, 256], BF16) for i in range(2)]
                wu = [sbuf(ph, f"wu{i}", [128, KD, 256], BF16) for i in range(2)]
                wd = [sbuf(ph, f"wd{i}", [128, NFB, 128], BF16) for i in range(2)]
                st_x = [P.stream("fx0"), P.stream("fx1")]
                st_g = [P.stream("fg0"), P.stream("fg1")]
                st_d = [P.stream("fd0"), P.stream("fd1")]
                st_y = [P.stream("fy0"), P.stream("fy1")]
                prep_w(lambda kc, a, b: wout[:, kc, a:b], w_out, 4, 0, D, G_AO)
                prep_w(lambda kc, a, b: wout[:, 4 + kc, a:b], w_out[512:1024, :], 4, 0, D, G_SSD)
                load_x(xt2[0][:], own_src[:, HALO:HALO + TT], st_x[0])

                def issue_g(g):
                    P.dma(wg[g % 2][:], wg_s[g, :, :, :], st_g[g % 2])
                    P.dma(wu[g % 2][:], wu_s[g, :, :, :], st_g[g % 2])

                def issue_d(ob):
                    P.dma(wd[ob % 2][:], wd_s[ob, :, :, :], st_d[ob % 2])

                issue_g(0)
                for t in range(nto):
                    if t + 1 < nto:
                        load_x(xt2[(t + 1) % 2][:], own_src[:, HALO + (t + 1) * TT:HALO + (t + 2) * TT],
                               st_x[(t + 1) % 2])
                    xt = xt2[t % 2]
                    cols = slice(t * TT, (t + 1) * TT)
                    for ob in range(8):
                        p_ = nps(0, 8)
                        for kc in range(8):
                            rhs = G["mlaT"][:, kc, cols] if kc < 4 else G["ssdT"][:, kc - 4, cols]
                            P.mm(p_[:], wout[:, kc, ob * 128:(ob + 1) * 128], rhs, start=(kc == 0), stop=(kc == 7))
                        P.tt(xt[:, ob, :], p_[:], xt[:, ob, :], ALU.add)
                    norm_tile(xt, h2, rst, TT)
                    for g in range(NG):
                        i = g % 2
                        if g + 1 < NG:
                            issue_g(g + 1)
                        if g == NG - 1:
                            issue_d(0)
                        for j in range(2):
                            fb = g * 2 + j
                            pgt = nps(0, 8)
                            put = nps(0, 8)
                            for kc in range(KD):
                                P.mm(pgt[:], wg[i][:, kc, j * 128:(j + 1) * 128], h2[:, kc, :],
                                     start=(kc == 0), stop=(kc == KD - 1))
                            for kc in range(KD):
                                P.mm(put[:], wu[i][:, kc, j * 128:(j + 1) * 128], h2[:, kc, :],
                                     start=(kc == 0), stop=(kc == KD - 1))
                            s_ = sg[fb % 2]
                            P.act(s_[:], pgt[:], AF.Silu)
                            P.tt(actT[:, fb, :], s_[:], put[:], ALU.mult, ko=fb)
                    if t + 1 < nto:
                        issue_g(0)
                    for ob in range(8):
                        i = ob % 2
                        if ob + 1 < 8:
                            issue_d(ob + 1)
                        p_ = nps(0, 8)
                        for kc in range(NFB):
                            P.mm(p_[:], wd[i][:, kc, :], actT[:, kc, :], start=(kc == 0), stop=(kc == NFB - 1),
                                 kr=kc)
                        P.tt(xt[:, ob, :], p_[:], xt[:, ob, :], ALU.add)
                    rstd_fm([xt[:, kc, :] for kc in range(KD)], D, TT, rst[:])
                    for ob in range(8):
                        P.stt(xt[:, ob, :], xt[:, ob, :], gvt[:, G_FIN + ob:G_FIN + ob + 1], rst[:],
                              ALU.mult, ALU.mult)
                    P.dma(out_dst[:, cols].rearrange("(k p) n -> p k n", p=128), xt[:], st_y[t % 2])
                P.barrier()

        mkt = sbuf(es, "mkt", [128, max(c.nnon, 1) * 2])
        P.dma(mkt[:], mk[:, :], st_c)
        jobs = []
        if c.lp_own:
            jobs.append(dict(src=xo, lown=c.lp_own, non=xn, nnon=c.nnon, rope=rpo, rnon=rpn, out=yo))
        for s in range(c.ns):
            jobs.append(dict(src=xs[s], lown=c.ls, non=None, nnon=0, rope=rs, rnon=None, out=ys[s]))
        stop = getattr(c, "stop", 0)
        for jb in jobs:
          with ExitStack() as js:
            if stop == 1:
                break
            G["mlaT"] = sbuf(js, "mlaT", [128, 4, jb["lown"]], BF16)
            mla_phase(jb["src"], jb["lown"], jb["non"], jb["nnon"], jb["rope"], jb["rnon"])
            if stop == 2:
                break
            G["ssdT"] = sbuf(js, "ssdT", [128, 4, jb["lown"]], BF16)
            nto = jb["lown"] // TT
            xslots = [(jb["non"][s, :, :], s, True, True) for s in range(jb["nnon"])]
            ssd_phase(jb["src"], 0, nto, xslots, 0)
            if stop >= 3:
                break
            ffn_phase(jb["src"], jb["lown"], jb["out"])
        P.finish()
    return nc
def _rope_tab(pos):
    inv = (10000.0 ** (-np.arange(0, 32, 2, dtype=np.float32) / 32)).astype(np.float32)
    ang = pos.astype(np.float32)[:, None] * inv[None, :]
    co = np.cos(ang).astype(np.float32).T
    si = np.sin(ang).astype(np.float32).T
    return np.stack([np.concatenate([co, co], 0), np.concatenate([si, si], 0)], 0)


def _consts():
    j = np.arange(128)[:, None]
    l = np.arange(128)[None, :]
    cm = np.zeros((128, 6, 128), np.float32)
    cm[:, 0] = (j <= l)
    cm[:, 1] = (j >= l)
    cm[:, 2] = (j > l)
    cm[:, 3] = (j < l)
    cm[:, 4] = np.where(j <= l, 0.0, -30000.0)
    cm[:, 5] = np.where(j >= l, 0.0, -30000.0)
    return cm


def make_in_maps(inp, cfg, ncores, cores_per_seq):
    f = lambda k: np.asarray(inp[k], np.float32)
    xp = f("x_prompt")
    xsm = f("x_sample")
    gv = np.zeros((128, 88), np.float32)
    def put(col, vec):
        v = vec.reshape(-1, 128).T
        gv[:, col:col + v.shape[1]] = v
    put(0, f("attn_norm_g")[0]); put(8, f("ffn_norm_g")[0]); put(16, f("final_norm_g"))
    put(24, f("q_a_norm_g")[0]); put(26, f("kv_a_norm_g")[0]); put(27, f("attn_out_norm_g")[0])
    put(31, f("ssd_norm_g")[0]); put(35, f("conv_b")[0])
    cw = f("conv_w")[0]
    for cc in range(8):
        for k in range(5):
            gv[:, 43 + cc * 5 + k] = cw[k, cc * 128:(cc + 1) * 128]
    rowv = np.concatenate([f("dt_bias")[0].reshape(-1), f("a_log")[0].reshape(-1), f("d_skip")[0].reshape(-1)])[None, :]
    cm = _consts()
    LP = xp.shape[1]
    own = cfg.lp_own
    maps = []
    rs = _rope_tab(np.arange(cfg.ls))
    for c in range(ncores):
        b, q = divmod(c, cores_per_seq)
        xT = np.zeros((D, LP + 4), np.float32)
        xT[:, 2:LP + 2] = xp[b].T
        o0 = q * own
        xo = np.ascontiguousarray(xT[:, o0:o0 + own + 4])
        ntl = LP // TT
        ot0, ot1 = o0 // TT, (o0 + own) // TT
        order = list(range(0, ot0)) + list(range(ntl - 1, ot1 - 1, -1))
        nn = max(len(order), 1)
        xn = np.zeros((nn, D, TW), np.float32)
        mk = np.zeros((128, nn * 2), np.float32)
        rpn = np.zeros((nn, 2, 32, TT), np.float32)
        for s, t in enumerate(order):
            xn[s] = xT[:, t * TT:t * TT + TW]
            mk[:, 2 * s] = 1.0 if t < ot0 else 0.0
            mk[:, 2 * s + 1] = 1.0 if t >= ot1 else 0.0
            rpn[s] = _rope_tab(np.arange(t * TT, (t + 1) * TT))
        rpo = _rope_tab(np.arange(o0, o0 + own))
        xsp = np.zeros((cfg.ns, D, cfg.ls + 4), np.float32)
        for i in range(cfg.ns):
            xsp[i, :, 2:cfg.ls + 2] = xsm[c * cfg.ns + i].T
        maps.append(dict(xo=xo, xn=xn, mk=mk, rpo=rpo, rpn=rpn, rs=rs, xs=xsp,
                         w_in=f("w_in")[0], w_q_b=f("w_q_b")[0], w_kv_b=f("w_kv_b")[0], w_out=f("w_out")[0],
                         w_gate=f("w_gate")[0], w_up=f("w_up")[0], w_down=f("w_down")[0],
                         gv=gv, rowv=rowv.astype(np.float32), cm=cm))
    return maps


def run(inp, cfg, ncores, cores_per_seq):
    nc = build(cfg)
    maps = make_in_maps(inp, cfg, ncores, cores_per_seq)
    res = run_bass_kernel_spmd(nc, maps, core_ids=list(range(ncores)))
    xp = np.asarray(inp["x_prompt"]); xsm = np.asarray(inp["x_sample"])
    yp = np.zeros(xp.shape, np.float32)
    ysm = np.zeros(xsm.shape, np.float32)
    for c in range(ncores):
        r = res.results[c]
        b, q = divmod(c, cores_per_seq)
        yp[b, q * cfg.lp_own:(q + 1) * cfg.lp_own, :] = r["yo"].T
        for i in range(cfg.ns):
            ysm[c * cfg.ns + i] = r["ys"][i].T
    return yp, ysm


def kernel(**inputs):
    cfg = Cfg()
    return run(inputs, cfg, 8, 4)
```

```python
import numpy as np
import concourse.bass as bass
import concourse.mybir as mybir

F32 = mybir.dt.float32
BF16 = mybir.dt.bfloat16
AF = mybir.ActivationFunctionType
ALU = mybir.AluOpType
AX = mybir.AxisListType


class Stream:
    def __init__(self, sem):
        self.sem = sem
        self.val = 0


class Prog:
    ENG = ("pe", "act", "dve", "pool")

    def __init__(self, nc, es):
        self.nc = nc
        self.es = es
        self.eng = {"pe": nc.tensor, "act": nc.scalar, "dve": nc.vector,
                    "pool": nc.gpsimd, "sp": nc.sync}
        self.sem = {e: es.enter_context(nc.semaphore("c_" + e)) for e in self.ENG}
        self.cnt = {e: 0 for e in self.ENG}
        self.waited = {e: {} for e in ("pe", "act", "dve", "pool", "sp")}
        self.semobj = {"c_" + e: self.sem[e] for e in self.ENG}
        self.tab = {}
        self.streams = []
        self._snames = {}
        self.nops = 0

    NDS = 24

    def stream(self, name):
        return None

    def _dma_sem(self):
        if not self.streams:
            for i in range(self.NDS):
                st = Stream(self.es.enter_context(self.nc.semaphore("d_%d" % i)))
                st.name = "d_%d" % i
                self.semobj[st.name] = st.sem
                self.streams.append(st)
            self._dsi = 0
        st = self.streams[self._dsi % self.NDS]
        self._dsi += 1
        return st

    def _entries(self, buf, key):
        t = self.tab.setdefault(buf, {})
        if key is None:
            return list(t.values())
        out = []
        if key in t:
            out.append(t[key])
        if None in t:
            out.append(t[None])
        return out

    def _deps(self, eng, reads, writes):
        deps = []
        for ap, key in reads:
            for ent in self._entries(ap.tensor.name, key):
                if ent[0] is not None:
                    deps.append((ent[0], True))
        for ap, key in writes:
            for ent in self._entries(ap.tensor.name, key):
                if ent[0] is not None:
                    deps.append((ent[0], False))
                for tok in ent[1].values():
                    deps.append((tok, False))
        return deps

    def _record(self, eng, tok, reads, writes):
        for ap, key in reads:
            t = self.tab.setdefault(ap.tensor.name, {})
            ent = t.setdefault(key, [None, {}])
            ent[1][eng] = tok
        for ap, key in writes:
            t = self.tab.setdefault(ap.tensor.name, {})
            if key is None:
                t.clear()
                t[None] = [tok, {}]
            else:
                t[key] = [tok, {}]

    @staticmethod
    def _autokey(ap):
        if not ap.tensor.name.startswith("pbd"):
            return None
        a = ap.ap
        c0 = ap.offset % a[0][0]
        c1 = c0 + sum((cnt - 1) * st for st, cnt in a[1:]) + 1
        if c1 <= 512:
            return 0
        if c0 >= 512:
            return 1
        return None

    def _norm(self, lst):
        out = []
        for x in lst:
            if not isinstance(x, tuple):
                x = (x, None)
            if x[1] is None:
                x = (x[0], self._autokey(x[0]))
            out.append(x)
        return out

    def _do_waits(self, eng, deps):
        need = {}
        for (semname, val, peng), raw in deps:
            if peng == eng:
                if eng not in ("act", "dve", "pool"):
                    continue
            if need.get(semname, 0) < val:
                need[semname] = val
        w = self.waited[eng]
        e = self.eng[eng]
        for semname, val in need.items():
            if w.get(semname, 0) >= val:
                continue
            w[semname] = val
            e.wait_ge(self.semobj[semname], val)

    def op(self, eng, fn, reads=(), writes=()):
        reads = self._norm(reads)
        writes = self._norm(writes)
        deps = self._deps(eng, reads, writes)
        self._do_waits(eng, deps)
        ins = fn(self.eng[eng])
        self.cnt[eng] += 1
        ins.then_inc(self.sem[eng], 1)
        tok = ("c_" + eng, self.cnt[eng], eng)
        self._record(eng, tok, reads, writes)
        self.nops += 1
        return tok

    def dma(self, out, in_, stream=None, rk=None, wk=None, q="sp", **kw):
        reads = [(in_, rk)]
        writes = [(out, wk)]
        st = self._dma_sem()
        deps = self._deps(q, reads, writes)
        if st.val:
            deps.append(((st.name, st.val, "dma"), False))
        self._do_waits(q, deps)
        ins = self.eng[q].dma_start(out=out, in_=in_, **kw)
        st.val += 16
        ins.then_inc(st.sem, 16)
        tok = (st.name, st.val, "dma")
        self._record("dma:" + st.name, tok, reads, writes)
        self.nops += 1
        return tok

    def finish(self):
        sp = self.eng["sp"]
        for s in self.streams:
            if s.val:
                sp.wait_ge(s.sem, s.val)
        for e in self.ENG:
            if self.cnt[e]:
                sp.wait_ge(self.sem[e], self.cnt[e])

    def barrier(self):
        for e in ("pe", "act", "dve", "pool", "sp"):
            h = self.eng[e]
            w = self.waited[e]
            for p in self.ENG:
                if p == e or self.cnt[p] == 0:
                    continue
                nm = "c_" + p
                if w.get(nm, 0) < self.cnt[p]:
                    w[nm] = self.cnt[p]
                    h.wait_ge(self.sem[p], self.cnt[p])
            for s in self.streams:
                if s.val and w.get(s.name, 0) < s.val:
                    w[s.name] = s.val
                    h.wait_ge(s.sem, s.val)
        self.tab.clear()

    def mm(self, out, lhsT, rhs, start=True, stop=True, ko=None, kl=None, kr=None, **kw):
        return self.op("pe", lambda e: e.matmul(out, lhsT, rhs, start=start, stop=stop, **kw),
                       reads=[(lhsT, kl), (rhs, kr)], writes=[(out, ko)])

    def tr(self, out, in_, ident, ko=None, ki=None):
        return self.op("pe", lambda e: e.transpose(out, in_, ident),
                       reads=[(in_, ki), (ident, None)], writes=[(out, ko)])

    def act(self, out, in_, func, bias=None, scale=None, accum_out=None, ko=None, ki=None,
            eng="act", extra_reads=()):
        kw = {}
        reads = [(in_, ki)] + list(extra_reads)
        if bias is not None:
            kw["bias"] = bias
            if not isinstance(bias, (int, float)):
                reads.append((bias, None))
        if scale is not None:
            kw["scale"] = scale
            if not isinstance(scale, (int, float)):
                reads.append((scale, None))
        writes = [(out, ko)]
        if accum_out is not None:
            kw["accum_out"] = accum_out
            writes.append((accum_out, None))
        return self.op("act", lambda e: e.activation(out, in_, func, **kw), reads=reads, writes=writes)

    def tt(self, out, in0, in1, op, eng="dve", ko=None, k0=None, k1=None):
        return self.op(eng, lambda e: e.tensor_tensor(out, in0, in1, op),
                       reads=[(in0, k0), (in1, k1)], writes=[(out, ko)])

    def ts(self, out, in0, s1, s2=None, op0=ALU.mult, op1=None, eng="dve", ko=None, k0=None,
           accum_out=None):
        reads = [(in0, k0)]
        if not isinstance(s1, (int, float)):
            reads.append((s1, None))
        if s2 is not None and not isinstance(s2, (int, float)):
            reads.append((s2, None))
        kw = {}
        if op1 is not None:
            kw["op1"] = op1
        writes = [(out, ko)]
        if accum_out is not None:
            kw["accum_out"] = accum_out
            writes.append((accum_out, None))
        return self.op(eng, lambda e: e.tensor_scalar(out, in0, s1, s2, op0, **kw),
                       reads=reads, writes=writes)

    def stt(self, out, in0, scalar, in1, op0, op1, eng="dve", ko=None, k0=None, k1=None):
        reads = [(in0, k0), (in1, k1)]
        if not isinstance(scalar, (int, float)):
            reads.append((scalar, None))
        return self.op(eng, lambda e: e.scalar_tensor_tensor(out, in0, scalar, in1, op0, op1),
                       reads=reads, writes=[(out, ko)])

    def copy(self, out, in_, eng="dve", ko=None, ki=None):
        if eng == "act":
            return self.act(out, in_, AF.Copy, ko=ko, ki=ki)
        return self.op(eng, lambda e: e.tensor_copy(out, in_), reads=[(in_, ki)], writes=[(out, ko)])

    def memset(self, ap, val, eng="dve", k=None):
        return self.op(eng, lambda e: e.memset(ap, val), writes=[(ap, k)])

    def recip(self, out, in_, ko=None, ki=None):
        return self.op("dve", lambda e: e.reciprocal(out, in_), reads=[(in_, ki)], writes=[(out, ko)])
from concourse.bass_utils import run_bass_kernel_spmd
from contextlib import ExitStack

EPS = 1e-6
D = 1024
KD = 8
TT = 512
HALO = 2
TW = TT + 2 * HALO
QSCALE = 96 ** -0.5


class Cfg:
    def __init__(self, lp_own=4096, nnon=24, ls=2048, ns=4, dff=2816):
        self.lp_own, self.nnon, self.ls, self.ns, self.dff = lp_own, nnon, ls, ns, dff
        self.nfb = dff // 128
        self.ng = dff // 256
        self.lmax = max(lp_own, ls)


def build(cfg):
    nc = bass.Bass("TRN2", target_bir_lowering=False)
    c = cfg
    NFB, NG = c.nfb, c.ng

    def din(name, shape, dt=F32):
        return nc.dram_tensor(name, list(shape), dt, kind="ExternalInput").ap()

    xo = din("xo", [D, max(c.lp_own, TT) + 4])
    xn = din("xn", [max(c.nnon, 1), D, TW])
    mk = din("mk", [128, max(c.nnon, 1) * 2])
    rpo = din("rpo", [2, 32, max(c.lp_own, TT)])
    rpn = din("rpn", [max(c.nnon, 1), 2, 32, TT])
    rs = din("rs", [2, 32, c.ls])
    xs = din("xs", [max(c.ns, 1), D, c.ls + 4])
    w_in = din("w_in", [D, 1968])
    w_q_b = din("w_q_b", [256, 768])
    w_kv_b = din("w_kv_b", [128, 1024])
    w_out = din("w_out", [D, D])
    w_gate = din("w_gate", [D, c.dff])
    w_up = din("w_up", [D, c.dff])
    w_down = din("w_down", [c.dff, D])
    gv = din("gv", [128, 88])
    rowv = din("rowv", [1, 40])
    cm = din("cm", [128, 6, 128])
    yo = nc.dram_tensor("yo", [D, max(c.lp_own, TT)], F32, kind="ExternalOutput").ap()
    ys = nc.dram_tensor("ys", [max(c.ns, 1), D, c.ls], F32, kind="ExternalOutput").ap()
    wg_s = nc.dram_tensor("wg_s", [NG, 128, KD, 256], BF16, kind="ExternalOutput").ap()
    wu_s = nc.dram_tensor("wu_s", [NG, 128, KD, 256], BF16, kind="ExternalOutput").ap()
    wd_s = nc.dram_tensor("wd_s", [8, 128, NFB, 128], BF16, kind="ExternalOutput").ap()

    with ExitStack() as es:
        P = Prog(nc, es)

        uid = {"n": 0}

        def sbuf(stack, name, shape, dt=F32):
            uid["n"] += 1
            return stack.enter_context(nc.sbuf_tensor("%s_%d" % (name, uid["n"]), list(shape), dt))

        pbd = [es.enter_context(nc.psum_tensor(f"pbd{i}", [128, 1024], F32)) for i in range(4)]

        class Bank:
            def __init__(self, t, half):
                self.t, self.h = t, half

            def __getitem__(self, idx):
                if not isinstance(idx, tuple):
                    idx = (idx,)
                cs = idx[1] if len(idx) > 1 else slice(None)
                a = (cs.start or 0) + self.h * 512
                b = (cs.stop if cs.stop is not None else 512) + self.h * 512
                return self.t[idx[0], a:b]

        pb = [Bank(pbd[i // 2], i % 2) for i in range(8)]
        psrr = {"i": 0}

        def nps(lo=0, hi=8):
            k = "%d_%d" % (lo, hi)
            i = psrr.get(k, lo)
            psrr[k] = lo + ((i - lo + 1) % (hi - lo))
            return pb[i]

        G = {}
        gvt = sbuf(es, "gvt", [128, 88])
        rowt = sbuf(es, "rowt", [128, 40])
        cmt = sbuf(es, "cmt", [128, 6, 128])
        mneg = sbuf(es, "mneg", [128, 2, 128], BF16)
        identf = sbuf(es, "identf", [128, 128])
        identb = sbuf(es, "identb", [128, 128], BF16)
        onesb = sbuf(es, "onesb", [128, 128], BF16)
        onesf = sbuf(es, "onesf", [128, 128])
        abt = sbuf(es, "abt", [128, 16])
        sq2 = [sbuf(es, f"sq{i}", [128, TT], BF16) for i in range(3)]
        lnb = sbuf(es, "lnb", [128, TT])
        rr = {"sq": 0}

        st_c = P.stream("const")
        P.dma(gvt[:], gv[:, :], st_c)
        P.dma(rowt[:], rowv[0:1, :].partition_broadcast(128), st_c)
        P.dma(cmt[:], cm[:, :, :], st_c)
        P.copy(mneg[:], cmt[:, 4:6, :], eng="dve")
        P.memset(identf[:], 1.0, eng="pool")
        P.op("pool", lambda e: e.affine_select(identf[:], identf[:], pattern=[[-1, 128]],
                                               compare_op=ALU.is_equal, fill=0.0, base=0,
                                               channel_multiplier=1),
             reads=[identf[:]], writes=[identf[:]])
        P.copy(identb[:], identf[:], eng="dve")
        P.memset(onesb[:], 1.0, eng="dve")
        P.memset(onesf[:], 1.0, eng="dve")
        P.act(abt[:], rowt[:, 16:32], AF.Exp)
        P.ts(abt[:], abt[:], -1.0, None, op0=ALU.mult)
        G_ATTN, G_FFN, G_FIN, G_QA, G_KVA, G_AO, G_SSD, G_CB, G_CW = 0, 8, 16, 24, 26, 27, 31, 35, 43
        dtb = rowt[:, 0:16]
        dsk = rowt[:, 32:40]

        def rstd_fm(srcs, Dn, N, out_ap):
            ps = nps(0, 6)
            for i, s in enumerate(srcs):
                sq = sq2[rr["sq"] % 3]
                rr["sq"] += 1
                P.act(sq[:, :N], s, AF.Square)
                P.mm(ps[:, :N], onesb[:], sq[:, :N], start=(i == 0), stop=(i == len(srcs) - 1))
            P.act(lnb[:, :N], ps[:, :N], AF.Ln, bias=EPS, scale=1.0 / Dn)
            P.act(out_ap, lnb[:, :N], AF.Exp, scale=-0.5)

        stg = [sbuf(es, f"stg{i}", [128, KD, 128]) for i in range(2)]
        st_w = [P.stream("w0"), P.stream("w1")]
        wrr = {"i": 0}

        def prep_w(dst_fn, src, k_chunks, c0, ncols, gcol, scale=1.0):
            for a in range(0, ncols, 128):
                b = min(a + 128, ncols)
                i = wrr["i"] % 2
                wrr["i"] += 1
                P.dma(stg[i][:, 0:k_chunks, 0:b - a],
                      src[0:k_chunks * 128, c0 + a:c0 + b].rearrange("(k p) n -> p k n", p=128), st_w[i])
                for kc in range(k_chunks):
                    o = dst_fn(kc, a, b)
                    P.ts(o, stg[i][:, kc, 0:b - a], gvt[:, gcol + kc:gcol + kc + 1], None, op0=ALU.mult)

        with ExitStack() as ph:
            wtmp = [sbuf(ph, f"wtmp{i}", [128, KD, 256], BF16) for i in range(2)]
            wdt = [sbuf(ph, f"wdt{i}", [128, NFB, 128], BF16) for i in range(2)]
            wdf = [sbuf(ph, f"wdf{i}", [128, NFB, 128]) for i in range(2)]
            st_o = [P.stream("wo0"), P.stream("wo1")]
            n = 0
            for (src, dst) in ((w_gate, wg_s), (w_up, wu_s)):
                for g in range(NG):
                    t = wtmp[n % 2]
                    prep_w(lambda kc, a, b, t=t: t[:, kc, a:b], src, KD, g * 256, 256, G_FFN)
                    P.dma(dst[g, :, :, :], t[:], st_o[n % 2])
                    n += 1
            for ob in range(8):
                i = ob % 2
                P.dma(wdf[i][:], w_down[:, ob * 128:(ob + 1) * 128].rearrange("(k p) n -> p k n", p=128),
                      st_w[i])
                P.copy(wdt[i][:], wdf[i][:], eng="act" if ob % 2 else "dve")
                P.dma(wd_s[ob, :, :, :], wdt[i][:], st_o[i])
            P.barrier()

        def load_x(dst, src_cols, stream):
            P.dma(dst, src_cols.rearrange("(k p) n -> p k n", p=128), stream)

        def norm_tile(xt, hn, rst, N):
            pieces = [(0, min(N, TT))] + ([(TT, N)] if N > TT else [])
            for (a, b) in pieces:
                rstd_fm([xt[:, kc, a:b] for kc in range(KD)], D, b - a, rst[:, a:b])
            for kc in range(KD):
                P.tt(hn[:, kc, 0:N], xt[:, kc, 0:N], rst[:, 0:N], ALU.mult, ko=kc)

        def mla_phase(own_src, lown, non_src, nnon, rope_own, rope_non):
            nto = lown // TT
            ltot = lown + nnon * TT
            ntt = ltot // TT
            with ExitStack() as ph:
                wqb = sbuf(ph, "wqb", [128, 2, 768], BF16)
                wqr = sbuf(ph, "wqr", [128, 2, 8, 96], BF16)
                wkvb = sbuf(ph, "wkvb", [128, 1024], BF16)
                ckvn = sbuf(ph, "ckvn", [128, ltot], BF16)
                KT = sbuf(ph, "KT", [96, ltot], BF16)
                qlatn = sbuf(ph, "qlatn", [128, 2, lown], BF16)
                rst2 = sbuf(ph, "mrst2", [128, TT])
                rtab = [sbuf(ph, f"rtab{i}", [96, 2, TT]) for i in range(2)]
                t1 = sbuf(ph, "mt1", [96, TT])
                t2 = sbuf(ph, "mt2", [96, TT])
                ph1 = ExitStack()
                w_mla = sbuf(ph1, "w_mla", [128, KD, 576], BF16)
                xt2 = [sbuf(ph1, f"mxt{i}", [128, KD, TT]) for i in range(2)]
                hn = sbuf(ph1, "mhn", [128, KD, TT], BF16)
                rst = sbuf(ph1, "mrst", [128, TT])
                st_x = [P.stream("mx0"), P.stream("mx1")]
                st_r = [P.stream("mr0"), P.stream("mr1")]

                prep_w(lambda kc, a, b: w_mla[:, kc, a:b], w_in, KD, 0, 384, G_ATTN)
                P.memset(w_mla[:, :, 384:576], 0.0, eng="pool")
                prep_w(lambda kc, a, b: w_mla[:, kc, 448 + a:448 + b], w_in, KD, 384, 32, G_ATTN)
                for kc in range(KD):
                    P.ts(w_mla[:, kc, 544:560], w_mla[:, kc, 464:480], -1.0, None, op0=ALU.mult)
                    P.copy(w_mla[:, kc, 560:576], w_mla[:, kc, 448:464], eng="pool")
                prep_w(lambda kc, a, b: wqb[:, kc, a:b], w_q_b, 2, 0, 768, G_QA)
                P.memset(wqr[:], 0.0, eng="pool")
                for kc in range(2):
                    for h in range(8):
                        P.ts(wqr[:, kc, h, 64:80], wqb[:, kc, h * 96 + 80:h * 96 + 96], -1.0, None,
                             op0=ALU.mult)
                        P.copy(wqr[:, kc, h, 80:96], wqb[:, kc, h * 96 + 64:h * 96 + 80], eng="pool")
                prep_w(lambda kc, a, b: wkvb[:, a:b], w_kv_b, 1, 0, 1024, G_KVA)

                def tile_src(t):
                    if t < nto:
                        return own_src[:, HALO + t * TT:HALO + (t + 1) * TT], \
                            rope_own[:, :, t * TT:(t + 1) * TT]
                    s = t - nto
                    return non_src[s, :, HALO:HALO + TT], rope_non[s, :, :, :]

                def issue_load(t):
                    xs_, rp_ = tile_src(t)
                    load_x(xt2[t % 2][:], xs_, st_x[t % 2])
                    P.dma(rtab[t % 2][64:96, :, :], rp_.rearrange("a p n -> p a n"), st_r[t % 2])

                issue_load(0)
                for t in range(ntt):
                    if t + 1 < ntt:
                        issue_load(t + 1)
                    xt = xt2[t % 2]
                    rt = rtab[t % 2]
                    norm_tile(xt, hn, rst, TT)
                    cols = slice(t * TT, (t + 1) * TT)
                    pc = nps(0, 6)
                    for kc in range(KD):
                        P.mm(pc[:], w_mla[:, kc, 256:384], hn[:, kc, :], start=(kc == 0), stop=(kc == KD - 1))
                    rstd_fm([pc[:]], 128, TT, rst2[:])
                    P.tt(ckvn[:, cols], pc[:], rst2[:], ALU.mult, ko=t)
                    pa = nps(0, 6)
                    pr = nps(0, 6)
                    for kc in range(KD):
                        P.mm(pa[0:96, :], w_mla[:, kc, 384:480], hn[:, kc, :], start=(kc == 0), stop=(kc == KD - 1))
                    for kc in range(KD):
                        P.mm(pr[0:96, :], w_mla[:, kc, 480:576], hn[:, kc, :], start=(kc == 0), stop=(kc == KD - 1))
                    P.tt(t1[64:96, :], pa[64:96, :], rt[64:96, 0, :], ALU.mult)
                    P.tt(t2[64:96, :], pr[64:96, :], rt[64:96, 1, :], ALU.mult)
                    P.tt(KT[64:96, cols], t1[64:96, :], t2[64:96, :], ALU.add, ko=("r", t))
                    if t < nto:
                        pq = [nps(0, 6), nps(0, 6)]
                        for cq in range(2):
                            for kc in range(KD):
                                P.mm(pq[cq][:], w_mla[:, kc, cq * 128:(cq + 1) * 128], hn[:, kc, :],
                                     start=(kc == 0), stop=(kc == KD - 1))
                        rstd_fm([pq[0][:], pq[1][:]], 256, TT, rst2[:])
                        for cq in range(2):
                            P.tt(qlatn[:, cq, cols], pq[cq][:], rst2[:], ALU.mult, ko=t)

                P.barrier()
                ph1.close()
                VH = sbuf(ph, "VH", [128, ltot // 128, 65], BF16)
                QH = sbuf(ph, "QH", [96, lown], BF16)
                PT = [sbuf(ph, f"PT{i}", [128, 2 * TT], BF16) for i in range(4)]
                osb2 = [sbuf(ph, f"osb{i}", [64, TT]) for i in range(2)]
                rc2 = [sbuf(ph, f"rc{i}", [65, TT]) for i in range(2)]
                fin = {"f": None, "n": 0}
                pbfin = pb[5]
                P.memset(VH[:, :, 64:65], 1.0, eng="pool")
                nkb = ltot // 128
                for h in range(8):
                    for t in range(ntt):
                        cols = slice(t * TT, (t + 1) * TT)
                        pk = nps(0, 6)
                        P.mm(pk[0:64, :], wkvb[:, h * 128:h * 128 + 64], ckvn[:, cols], kr=t)
                        P.copy(KT[0:64, cols], pk[0:64, :], eng="act" if t % 2 else "dve", ko=("n", t))
                    for g8 in range(0, nkb, 8):
                        pv = nps(0, 6)
                        for j in range(8):
                            kb = g8 + j
                            P.mm(pv[:, j * 64:(j + 1) * 64], ckvn[:, kb * 128:(kb + 1) * 128],
                                 wkvb[:, h * 128 + 64:h * 128 + 128], kl=kb // 4)
                        P.copy(VH[:, g8:g8 + 8, 0:64], pv[:, :].rearrange("p (j d) -> p j d", j=8),
                               eng="dve" if (g8 // 8) % 2 else "act", ko=g8 // 8)
                    for t in range(nto):
                        cols = slice(t * TT, (t + 1) * TT)
                        rt = rtab[t % 2]
                        P.dma(rt[64:96, :, :], rope_own[:, :, cols].rearrange("a p n -> p a n"), st_r[t % 2])
                        pa = nps(0, 6)
                        pr = nps(0, 6)
                        for kc in range(2):
                            P.mm(pa[0:96, :], wqb[:, kc, h * 96:(h + 1) * 96], qlatn[:, kc, cols],
                                 start=(kc == 0), stop=(kc == 1), kr=t)
                        for kc in range(2):
                            P.mm(pr[0:96, :], wqr[:, kc, h, :], qlatn[:, kc, cols],
                                 start=(kc == 0), stop=(kc == 1), kr=t)
                        P.copy(QH[0:64, cols], pa[0:64, :], eng="act", ko=("n", t))
                        P.tt(t1[64:96, :], pa[64:96, :], rt[64:96, 0, :], ALU.mult)
                        P.tt(t2[64:96, :], pr[64:96, :], rt[64:96, 1, :], ALU.mult)
                        P.tt(QH[64:96, cols], t1[64:96, :], t2[64:96, :], ALU.add, ko=("r", t))
                    for t in range(nto):
                        cols = slice(t * TT, (t + 1) * TT)
                        po = pb[6 + (t % 2)]
                        LOOK = 2
                        npair = nkb // 2
                        for idx in range(npair + LOOK):
                            if idx < npair:
                                pd_ = pbd[idx % 3]
                                for j in range(2):
                                    kb = idx * 2 + j
                                    tk = kb // 4
                                    P.op("pe", lambda e, pd_=pd_, kb=kb, j=j, cols=cols: e.matmul(
                                        pd_[:, j * 512:(j + 1) * 512], KT[0:96, kb * 128:(kb + 1) * 128],
                                        QH[0:96, cols], start=True, stop=True),
                                        reads=[(KT[:], ("n", tk)), (KT[:], ("r", tk)), (QH[:], ("n", t)),
                                               (QH[:], ("r", t))],
                                        writes=[(pd_[:, j * 512:(j + 1) * 512], None)])
                                P.act(PT[idx % 4][:], pd_[:, :], AF.Exp, scale=QSCALE)
                            if idx == LOOK and fin["f"] is not None:
                                fin["f"]()
                                fin["f"] = None
                            if idx >= LOOK:
                                pi = idx - LOOK
                                for j in range(2):
                                    kb = pi * 2 + j
                                    P.mm(po[0:65, :], VH[:, kb, :], PT[pi % 4][:, j * 512:(j + 1) * 512],
                                         start=(kb == 0), stop=(kb == nkb - 1), kl=kb // 8)
                        ob_ = osb2[fin["n"] % 2]
                        rc_ = rc2[fin["n"] % 2]
                        fin["n"] += 1
                        P.act(rc_[64:65, :], po[64:65, :], AF.Ln)
                        P.act(rc_[64:65, :], rc_[64:65, :], AF.Exp, scale=-1.0)
                        P.copy(ob_[0:64, :], po[0:64, :], eng="dve")

                        def finalize(ob_=ob_, rc_=rc_, h=h, cols=cols, t=t):
                            pbc = pbfin
                            P.mm(pbc[0:64, :], onesf[64:65, 0:64], rc_[64:65, :])
                            P.tt(G["mlaT"][(h % 2) * 64:(h % 2) * 64 + 64, h // 2, cols], ob_[0:64, :],
                                 pbc[0:64, :], ALU.mult, ko=(h, t))
                        fin["f"] = finalize
                if fin["f"] is not None:
                    fin["f"]()
                    fin["f"] = None
                for t in range(nto):
                    cols = slice(t * TT, (t + 1) * TT)
                    rstd_fm([G["mlaT"][:, cc, cols] for cc in range(4)], 512, TT, rst2[:])
                    for cc in range(4):
                        P.tt(G["mlaT"][:, cc, cols], G["mlaT"][:, cc, cols], rst2[:], ALU.mult,
                             eng="pool" if cc % 2 else "dve")
                P.barrier()

        def ssd_phase(own_src, own_t0, nto, slots, out_t0):
            nch = (nto if nto <= 4 else (nto + 1) // 2) * 4
            with ExitStack() as ph:
                w_ssd = sbuf(ph, "w_ssd", [128, KD, 1552], BF16)
                Sst = [sbuf(ph, f"Sst{d}", [128, 512]) for d in range(2)]
                SBW = sbuf(ph, "SBW", [128, nch, 512], BF16)
                xtb = [sbuf(ph, f"sxt{i}", [128, KD, TW]) for i in range(1)]
                dgw = sbuf(ph, "dgw", [128, 8, 5, 128], BF16)
                preb = [sbuf(ph, f"preb{i}", [128, TW], BF16) for i in range(2)]
                hn = sbuf(ph, "shn", [128, KD, TW], BF16)
                rst = sbuf(ph, "srst", [128, TW])
                xact = sbuf(ph, "xact", [128, 8, TT], BF16)
                x_tok = sbuf(ph, "x_tok", [128, 4, 512], BF16)
                B_tok = sbuf(ph, "B_tok", [128, 4, 256], BF16)
                z_tok = sbuf(ph, "z_tok", [128, 4, 512], BF16)
                dt_tok = sbuf(ph, "dt_tok", [128, 4, 16])
                dtm = sbuf(ph, "dtm", [128, 4, 16])
                sm = [sbuf(ph, f"sm{i}", [128, 16]) for i in range(16)]
                xw4 = sbuf(ph, "xw4", [128, 4, 512], BF16)
                dA4 = sbuf(ph, "dA4", [128, 4, 8])
                w4 = sbuf(ph, "w4", [128, 4, 8])
                ex4 = sbuf(ph, "ex4", [128, 64])
                Sfin = sbuf(ph, "Sfin", [128, 4, 512], BF16)
                Ssnap = sbuf(ph, "Ssnap", [128, 512])
                xdt = [sbuf(ph, f"xdt{d}", [128, 512], BF16) for d in range(2)]
                GT = sbuf(ph, "GT", [128, 2, 128], BF16)
                Dm = [sbuf(ph, f"Dm{i}", [128, 512], BF16) for i in range(2)]
                Mm = [sbuf(ph, f"Mm{i}", [128, 512], BF16) for i in range(2)]
                ya = sbuf(ph, "ya", [128, 512])
                yb = sbuf(ph, "yb", [128, 512])
                stok = sbuf(ph, "stok", [128, 512], BF16)
                st_x = P.stream("sx")
                smr = {"i": 0}

                def smn():
                    smr["i"] += 1
                    return sm[smr["i"] % 16]

                prep_w(lambda kc, a, b: w_ssd[:, kc, a:b], w_in, KD, 416, 1552, G_ATTN)
                P.memset(Sst[0][:], 0.0)
                P.memset(Sst[1][:], 0.0)
                for cc in range(8):
                    for k in range(5):
                        P.ts(dgw[:, cc, k, :], identf[:], gvt[:, G_CW + cc * 5 + k:G_CW + cc * 5 + k + 1], None,
                             op0=ALU.mult)
                tl = {"i": 0, "q": []}

                def prefetch(src):
                    i = tl["i"]
                    tl["i"] += 1
                    load_x(xtb[0][:], src, st_x)
                    tl["q"].append(xtb[0])

                def tile_front(src, full):
                    xt = tl["q"].pop(0)
                    import os
                    dbg = os.environ.get("SSD_DBG", "z")
                    if dbg == "0":
                        return
                    norm_tile(xt, hn, rst, TW)
                    if tl["next"] is not None:
                        prefetch(tl["next"])
                        tl["next"] = None
                    if dbg == "a":
                        return
                    ncc = 8 if full else 6
                    for cc in range(ncc):
                        pm = nps(0, 6)
                        ph_ = nps(0, 6)
                        col = 512 + cc * 128
                        for kc in range(KD):
                            P.mm(pm[:], w_ssd[:, kc, col:col + 128], hn[:, kc, 0:TT],
                                 start=(kc == 0), stop=(kc == KD - 1))
                        for kc in range(KD):
                            P.mm(ph_[:, 0:4], w_ssd[:, kc, col:col + 128], hn[:, kc, TT:TW],
                                 start=(kc == 0), stop=(kc == KD - 1))
                        pr_ = preb[cc % 2]
                        P.copy(pr_[:, 0:TT], pm[:], eng="act")
                        P.copy(pr_[:, TT:TW], ph_[:, 0:4], eng="dve")
                        pcv = nps(0, 6)
                        for k in range(5):
                            P.mm(pcv[:], dgw[:, cc, k, :], pr_[:, k:k + TT], start=(k == 0), stop=(k == 4))
                        P.act(xact[:, cc, :], pcv[:], AF.Silu, bias=gvt[:, G_CB + cc:G_CB + cc + 1])
                    if dbg == "b":
                        return
                    for tb in range(4):
                        pd = nps(0, 6)
                        for kc in range(KD):
                            P.mm(pd[:, 0:16], hn[:, kc, HALO + tb * 128:HALO + (tb + 1) * 128],
                                 w_ssd[:, kc, 1536:1552], start=(kc == 0), stop=(kc == KD - 1))
                        v = smn()
                        P.tt(v[:], pd[:, 0:16], dtb, ALU.add)
                        a_ = smn()
                        P.act(a_[:], v[:], AF.Abs)
                        e_ = smn()
                        P.act(e_[:], a_[:], AF.Exp, scale=-1.0)
                        l_ = smn()
                        P.act(l_[:], e_[:], AF.Ln, bias=1.0)
                        P.ts(v[:], v[:], 0.0, None, op0=ALU.max)
                        P.tt(dt_tok[:, tb, :], v[:], l_[:], ALU.add, ko=tb)
                    if full:
                        for tb in range(4):
                            pz = nps(0, 6)
                            for kc in range(KD):
                                P.mm(pz[:], hn[:, kc, HALO + tb * 128:HALO + (tb + 1) * 128],
                                     w_ssd[:, kc, 0:512], start=(kc == 0), stop=(kc == KD - 1))
                            P.act(z_tok[:, tb, :], pz[:], AF.Silu, ko=tb)
                    if dbg == "c":
                        return
                    for tb in range(4):
                        pt_ = nps(0, 6)
                        pt2 = nps(0, 6)
                        for cc in range(4):
                            P.mm(pt_[:, cc * 128:(cc + 1) * 128], xact[:, cc, tb * 128:(tb + 1) * 128], identb[:])
                        for cc in range(2):
                            P.mm(pt2[:, cc * 128:(cc + 1) * 128], xact[:, 4 + cc, tb * 128:(tb + 1) * 128],
                                 identb[:])
                        P.copy(x_tok[:, tb, :], pt_[:, 0:512], eng="dve", ko=tb)
                        P.copy(B_tok[:, tb, :], pt2[:, 0:256], eng="act", ko=tb)

                def bc8(ap8):
                    return ap8.unsqueeze(2).to_broadcast([128, 8, 64])

                def v8(ap512):
                    return ap512.rearrange("p (h d) -> p h d", h=8)

                def su_batch(d, dsrc, order, save=None):
                    dv = dsrc[:, :, d * 8:(d + 1) * 8]
                    P.tt(dA4[:], dv, abt[:, d * 8:(d + 1) * 8].unsqueeze(1).to_broadcast([128, 4, 8]), ALU.mult)
                    pp = nps(0, 6)
                    dflat = dA4[:, :, :].rearrange("p c h -> p (c h)")
                    P.mm(pp[:, 0:32], cmt[:, 2 + d, :], dflat)
                    P.mm(pp[:, 32:64], onesf[:], dflat)
                    P.act(ex4[:], pp[:, 0:64], AF.Exp)
                    P.tt(w4[:], dv, ex4[:, 0:32].rearrange("p (c h) -> p c h", c=4), ALU.mult)
                    P.tt(xw4[:, :, :].rearrange("p c (h e) -> p c h e", h=8),
                         x_tok[:, :, :].rearrange("p c (h e) -> p c h e", h=8),
                         w4[:, :, :].unsqueeze(3).to_broadcast([128, 4, 8, 64]), ALU.mult)
                    S = Sst[d]
                    for tb in order:
                        pc = nps(0, 6)
                        for g in range(2):
                            P.mm(pc[:, g * 256:(g + 1) * 256], B_tok[:, tb, g * 128:(g + 1) * 128],
                                 xw4[:, tb, g * 256:(g + 1) * 256])
                        if save is not None:
                            P.copy(save(tb), S[:], eng="act")
                        P.tt(v8(S[:]), v8(S[:]), bc8(ex4[:, 32 + tb * 8:32 + tb * 8 + 8]), ALU.mult)
                        P.tt(S[:], S[:], pc[:], ALU.add)

                def full_chunk(tb, ci, out_cols):
                    pg = nps(0, 6)
                    for g in range(2):
                        P.mm(pg[:, g * 128:(g + 1) * 128], xact[:, 4 + g, tb * 128:(tb + 1) * 128],
                             xact[:, 6 + g, tb * 128:(tb + 1) * 128])
                    P.copy(GT[:], pg[:, 0:256].rearrange("p (g l) -> p g l", g=2), eng="act")
                    prel = []
                    for d in range(2):
                        dt8 = dt_tok[:, tb, d * 8:(d + 1) * 8]
                        dtA = smn()
                        P.tt(dtA[:, 0:8], dt8, abt[:, d * 8:(d + 1) * 8], ALU.mult, k0=tb)
                        pcs = nps(0, 6)
                        P.mm(pcs[:, 0:8], cmt[:, d, :], dtA[:, 0:8])
                        et = smn()
                        P.act(et[:, 0:8], pcs[:, 0:8], AF.Exp)
                        ncs = smn()
                        P.ts(ncs[:, 0:8], pcs[:, 0:8], -1.0, None, op0=ALU.mult)
                        P.tt(v8(xdt[d][:]), v8(x_tok[:, tb, :]), bc8(dt8), ALU.mult, k0=tb, k1=tb)
                        poff = nps(0, 6)
                        for g in range(2):
                            rhs = Sfin[:, tb, g * 256:(g + 1) * 256] if d == 0 else SBW[:, ci, g * 256:(g + 1) * 256]
                            P.mm(poff[:, g * 256:(g + 1) * 256], xact[:, 6 + g, tb * 128:(tb + 1) * 128], rhs,
                                 kr=(None if d == 0 else ci))
                        tgt = ya if d == 0 else yb
                        P.tt(v8(tgt[:]), v8(poff[:]), bc8(et[:, 0:8]), ALU.mult)
                        prel.append((dtA, ncs))
                    pdgs = [pb[6], pb[7]]
                    groups = [(d, g) for d in range(2) for g in range(2)]

                    def stageA(k):
                        d, g = groups[k]
                        dtA, ncs = prel[d]
                        pcb = nps(0, 6)
                        for j in range(4):
                            h = g * 4 + j
                            P.mm(pcb[:, j * 128:(j + 1) * 128], dtA[:, h:h + 1].to_broadcast([128, 128]),
                                 cmt[:, d, :], start=True, stop=False)
                            P.mm(pcb[:, j * 128:(j + 1) * 128], identb[:], mneg[:, d, :], start=False, stop=True)
                        Dq = Dm[k % 2]
                        for j in range(4):
                            h = g * 4 + j
                            P.act(Dq[:, j * 128:(j + 1) * 128], pcb[:, j * 128:(j + 1) * 128], AF.Exp,
                                  bias=ncs[:, h:h + 1])
                        P.tt(Mm[k % 2][:, :].rearrange("p (j l) -> p j l", j=4),
                             Dq[:, :].rearrange("p (j l) -> p j l", j=4),
                             GT[:, g, :].unsqueeze(1).to_broadcast([128, 4, 128]), ALU.mult)

                    def stageB(k):
                        d, g = groups[k]
                        for j in range(4):
                            h = g * 4 + j
                            P.mm(pdgs[d][:, h * 64:(h + 1) * 64], Mm[k % 2][:, j * 128:(j + 1) * 128],
                                 xdt[d][:, h * 64:(h + 1) * 64])

                    stageA(0)
                    stageA(1)
                    stageB(0)
                    stageA(2)
                    stageB(1)
                    stageA(3)
                    stageB(2)
                    stageB(3)
                    P.tt(ya[:], ya[:], pdgs[0][:], ALU.add)
                    P.tt(yb[:], yb[:], pdgs[1][:], ALU.add)
                    P.tt(ya[:], ya[:], yb[:], ALU.add)
                    P.tt(v8(yb[:]), v8(x_tok[:, tb, :]), bc8(dsk), ALU.mult, k0=tb)
                    P.tt(ya[:], ya[:], yb[:], ALU.add)
                    yg = yb
                    P.tt(yg[:], ya[:], z_tok[:, tb, :], ALU.mult, k1=tb)
                    ss = smn()
                    P.memset(ss[:, 0:1], 0.0)
                    P.act(stok[:], yg[:], AF.Square, accum_out=ss[:, 0:1])
                    l2 = smn()
                    P.act(l2[:, 0:1], ss[:, 0:1], AF.Ln, bias=EPS, scale=1.0 / 512)
                    r2 = smn()
                    P.act(r2[:, 0:1], l2[:, 0:1], AF.Exp, scale=-0.5)
                    P.ts(stok[:], yg[:], r2[:, 0:1], None, op0=ALU.mult)
                    pt_ = nps(0, 6)
                    for cc in range(4):
                        P.mm(pt_[:, cc * 128:(cc + 1) * 128], stok[:, cc * 128:(cc + 1) * 128], identb[:])
                    P.copy(G["ssdT"][:, :, out_cols], pt_[:, 0:512].rearrange("p (c n) -> p c n", c=4), eng="dve",
                           ko=ci)

                sstop = getattr(c, "stop", 0)
                def own_v(kind, t):
                    g0 = (own_t0 + t) * TT
                    return (kind, own_src[:, g0:g0 + TW], t)
                visits = [("slot", src, midx) for (src, midx, fon, bon) in slots]
                if nto <= 4:
                    parts = [(0, nto)]
                else:
                    parts = [(0, nto // 2), (nto // 2, nto)]
                for pi_, (ta, tb_) in enumerate(parts):
                    if len(parts) == 2 and pi_ == 0:
                        visits.append(("snap", None, None))
                        for t in range(nto - 1, tb_ - 1, -1):
                            visits.append(own_v("bwdns", t))
                    if len(parts) == 2 and pi_ == 1:
                        visits.append(("restore", None, None))
                    for t in range(tb_ - 1, ta - 1, -1):
                        visits.append(own_v("bwd", t))
                    for t in range(ta, tb_):
                        visits.append(own_v("full", t))
                tiles = [v for v in visits if v[1] is not None]
                prefetch(tiles[0][1])
                ti = 0
                for v in visits:
                    if v[0] == "snap":
                        P.copy(Ssnap[:], Sst[1][:], eng="dve")
                        continue
                    if v[0] == "restore":
                        P.copy(Sst[1][:], Ssnap[:], eng="dve")
                        continue
                    ti += 1
                    tl["next"] = tiles[ti][1] if ti < len(tiles) else None
                    if v[0] == "slot":
                        midx = v[2]
                        tile_front(v[1], False)
                        P.tt(dtm[:, :, :].rearrange("p c (d h) -> p c d h", d=2),
                             dt_tok[:, :, :].rearrange("p c (d h) -> p c d h", d=2),
                             mkt[:, midx * 2:midx * 2 + 2].unsqueeze(1).unsqueeze(3).to_broadcast([128, 4, 2, 8]),
                             ALU.mult)
                        su_batch(0, dtm, (0, 1, 2, 3))
                        su_batch(1, dtm, (3, 2, 1, 0))
                    elif v[0] == "bwdns":
                        tile_front(v[1], False)
                        su_batch(1, dt_tok, (3, 2, 1, 0))
                    elif v[0] == "bwd":
                        t = v[2]
                        tile_front(v[1], False)
                        lt = t - (parts[-1][0] if t >= parts[-1][0] and len(parts) == 2 else 0)
                        su_batch(1, dt_tok, (3, 2, 1, 0), save=lambda tb, lt=lt: SBW[:, lt * 4 + tb, :])
                    else:
                        t = v[2]
                        tile_front(v[1], True)
                        lt = t - (parts[-1][0] if t >= parts[-1][0] and len(parts) == 2 else 0)
                        su_batch(0, dt_tok, (0, 1, 2, 3), save=lambda tb: Sfin[:, tb, :])
                        for tb in range(4):
                            oc = (out_t0 + t) * TT + tb * 128
                            full_chunk(tb, lt * 4 + tb, slice(oc, oc + 128))
                P.barrier()

        def ffn_phase(own_src, lown, out_dst):
            nto = lown // TT
            with ExitStack() as ph:
                wout = sbuf(ph, "wout", [128, KD, D], BF16)
                xt2 = [sbuf(ph, f"fxt{i}", [128, KD, TT]) for i in range(2)]
                h2 = sbuf(ph, "h2", [128, KD, TT], BF16)
                rst = sbuf(ph, "frst", [128, TT])
                actT = sbuf(ph, "actT", [128, NFB, TT], BF16)
                sg = [sbuf(ph, f"sg{i}", [128, TT]) for i in range(2)]
                wg = [sbuf(ph, f"wg{i}", [128, KD, 256], BF16) for i in range(2)]
                wu = [sbuf(ph, f"wu{i}", [128, KD, 256], BF16) for i in range(2)]
                wd = [sbuf(ph, f"wd{i}", [128, NFB, 128], BF16) for i in range(2)]
                st_x = [P.stream("fx0"), P.stream("fx1")]
                st_g = [P.stream("fg0"), P.stream("fg1")]
                st_d = [P.stream("fd0"), P.stream("fd1")]
                st_y = [P.stream("fy0"), P.stream("fy1")]
                prep_w(lambda kc, a, b: wout[:, kc, a:b], w_out, 4, 0, D, G_AO)
                prep_w(lambda kc, a, b: wout[:, 4 + kc, a:b], w_out[512:1024, :], 4, 0, D, G_SSD)
                load_x(xt2[0][:], own_src[:, HALO:HALO + TT], st_x[0])

                def issue_g(g):
                    P.dma(wg[g % 2][:], wg_s[g, :, :, :], st_g[g % 2])
                    P.dma(wu[g % 2][:], wu_s[g, :, :, :], st_g[g % 2])

                def issue_d(ob):
                    P.dma(wd[ob % 2][:], wd_s[ob, :, :, :], st_d[ob % 2])

                issue_g(0)
                for t in range(nto):
                    if t + 1 < nto:
                        load_x(xt2[(t + 1) % 2][:], own_src[:, HALO + (t + 1) * TT:HALO + (t + 2) * TT],
                               st_x[(t + 1) % 2])
                    xt = xt2[t % 2]
                    cols = slice(t * TT, (t + 1) * TT)
                    for ob in range(8):
                        p_ = nps(0, 8)
                        for kc in range(8):
                            rhs = G["mlaT"][:, kc, cols] if kc < 4 else G["ssdT"][:, kc - 4, cols]
                            P.mm(p_[:], wout[:, kc, ob * 128:(ob + 1) * 128], rhs, start=(kc == 0), stop=(kc == 7))
                        P.tt(xt[:, ob, :], p_[:], xt[:, ob, :], ALU.add)
                    norm_tile(xt, h2, rst, TT)
                    for g in range(NG):
                        i = g % 2
                        if g + 1 < NG:
                            issue_g(g + 1)
                        if g == NG - 1:
                            issue_d(0)
                        for j in range(2):
                            fb = g * 2 + j
                            pgt = nps(0, 8)
                            put = nps(0, 8)
                            for kc in range(KD):
                                P.mm(pgt[:], wg[i][:, kc, j * 128:(j + 1) * 128], h2[:, kc, :],
                                     start=(kc == 0), stop=(kc == KD - 1))
                            for kc in range(KD):
                                P.mm(put[:], wu[i][:, kc, j * 128:(j + 1) * 128], h2[:, kc, :],
                                     start=(kc == 0), stop=(kc == KD - 1))
                            s_ = sg[fb % 2]
                            P.act(s_[:], pgt[:], AF.Silu)
                            P.tt(actT[:, fb, :], s_[:], put[:], ALU.mult, ko=fb)
                    if t + 1 < nto:
                        issue_g(0)
                    for ob in range(8):
                        i = ob % 2
                        if ob + 1 < 8:
                            issue_d(ob + 1)
                        p_ = nps(0, 8)
                        for kc in range(NFB):
                            P.mm(p_[:], wd[i][:, kc, :], actT[:, kc, :], start=(kc == 0), stop=(kc == NFB - 1),
                                 kr=kc)
                        P.tt(xt[:, ob, :], p_[:], xt[:, ob, :], ALU.add)
                    rstd_fm([xt[:, kc, :] for kc in range(KD)], D, TT, rst[:])
                    for ob in range(8):
                        P.stt(xt[:, ob, :], xt[:, ob, :], gvt[:, G_FIN + ob:G_FIN + ob + 1], rst[:],
                              ALU.mult, ALU.mult)
                    P.dma(out_dst[:, cols].rearrange("(k p) n -> p k n", p=128), xt[:], st_y[t % 2])
                P.barrier()

        mkt = sbuf(es, "mkt", [128, max(c.nnon, 1) * 2])
        P.dma(mkt[:], mk[:, :], st_c)
        jobs = []
        if c.lp_own:
            jobs.append(dict(src=xo, lown=c.lp_own, non=xn, nnon=c.nnon, rope=rpo, rnon=rpn, out=yo))
        for s in range(c.ns):
            jobs.append(dict(src=xs[s], lown=c.ls, non=None, nnon=0, rope=rs, rnon=None, out=ys[s]))
        stop = getattr(c, "stop", 0)
        for jb in jobs:
          with ExitStack() as js:
            if stop == 1:
                break
            G["mlaT"] = sbuf(js, "mlaT", [128, 4, jb["lown"]], BF16)
            mla_phase(jb["src"], jb["lown"], jb["non"], jb["nnon"], jb["rope"], jb["rnon"])
            if stop == 2:
                break
            G["ssdT"] = sbuf(js, "ssdT", [128, 4, jb["lown"]], BF16)
            nto = jb["lown"] // TT
            xslots = [(jb["non"][s, :, :], s, True, True) for s in range(jb["nnon"])]
            ssd_phase(jb["src"], 0, nto, xslots, 0)
            if stop >= 3:
                break
            ffn_phase(jb["src"], jb["lown"], jb["out"])
        P.finish()
    return nc
def _rope_tab(pos):
    inv = (10000.0 ** (-np.arange(0, 32, 2, dtype=np.float32) / 32)).astype(np.float32)
    ang = pos.astype(np.float32)[:, None] * inv[None, :]
    co = np.cos(ang).astype(np.float32).T
    si = np.sin(ang).astype(np.float32).T
    return np.stack([np.concatenate([co, co], 0), np.concatenate([si, si], 0)], 0)


def _consts():
    j = np.arange(128)[:, None]
    l = np.arange(128)[None, :]
    cm = np.zeros((128, 6, 128), np.float32)
    cm[:, 0] = (j <= l)
    cm[:, 1] = (j >= l)
    cm[:, 2] = (j > l)
    cm[:, 3] = (j < l)
    cm[:, 4] = np.where(j <= l, 0.0, -30000.0)
    cm[:, 5] = np.where(j >= l, 0.0, -30000.0)
    return cm


def make_in_maps(inp, cfg, ncores, cores_per_seq):
    f = lambda k: np.asarray(inp[k], np.float32)
    xp = f("x_prompt")
    xsm = f("x_sample")
    gv = np.zeros((128, 88), np.float32)
    def put(col, vec):
        v = vec.reshape(-1, 128).T
        gv[:, col:col + v.shape[1]] = v
    put(0, f("attn_norm_g")[0]); put(8, f("ffn_norm_g")[0]); put(16, f("final_norm_g"))
    put(24, f("q_a_norm_g")[0]); put(26, f("kv_a_norm_g")[0]); put(27, f("attn_out_norm_g")[0])
    put(31, f("ssd_norm_g")[0]); put(35, f("conv_b")[0])
    cw = f("conv_w")[0]
    for cc in range(8):
        for k in range(5):
            gv[:, 43 + cc * 5 + k] = cw[k, cc * 128:(cc + 1) * 128]
    rowv = np.concatenate([f("dt_bias")[0].reshape(-1), f("a_log")[0].reshape(-1), f("d_skip")[0].reshape(-1)])[None, :]
    cm = _consts()
    LP = xp.shape[1]
    own = cfg.lp_own
    maps = []
    rs = _rope_tab(np.arange(cfg.ls))
    for c in range(ncores):
        b, q = divmod(c, cores_per_seq)
        xT = np.zeros((D, LP + 4), np.float32)
        xT[:, 2:LP + 2] = xp[b].T
        o0 = q * own
        xo = np.ascontiguousarray(xT[:, o0:o0 + own + 4])
        ntl = LP // TT
        ot0, ot1 = o0 // TT, (o0 + own) // TT
        order = list(range(0, ot0)) + list(range(ntl - 1, ot1 - 1, -1))
        nn = max(len(order), 1)
        xn = np.zeros((nn, D, TW), np.float32)
        mk = np.zeros((128, nn * 2), np.float32)
        rpn = np.zeros((nn, 2, 32, TT), np.float32)
        for s, t in enumerate(order):
            xn[s] = xT[:, t * TT:t * TT + TW]
            mk[:, 2 * s] = 1.0 if t < ot0 else 0.0
            mk[:, 2 * s + 1] = 1.0 if t >= ot1 else 0.0
            rpn[s] = _rope_tab(np.arange(t * TT, (t + 1) * TT))
        rpo = _rope_tab(np.arange(o0, o0 + own))
        xsp = np.zeros((cfg.ns, D, cfg.ls + 4), np.float32)
        for i in range(cfg.ns):
            xsp[i, :, 2:cfg.ls + 2] = xsm[c * cfg.ns + i].T
        maps.append(dict(xo=xo, xn=xn, mk=mk, rpo=rpo, rpn=rpn, rs=rs, xs=xsp,
                         w_in=f("w_in")[0], w_q_b=f("w_q_b")[0], w_kv_b=f("w_kv_b")[0], w_out=f("w_out")[0],
                         w_gate=f("w_gate")[0], w_up=f("w_up")[0], w_down=f("w_down")[0],
                         gv=gv, rowv=rowv.astype(np.float32), cm=cm))
    return maps


def run(inp, cfg, ncores, cores_per_seq):
    nc = build(cfg)
    maps = make_in_maps(inp, cfg, ncores, cores_per_seq)
    res = run_bass_kernel_spmd(nc, maps, core_ids=list(range(ncores)))
    xp = np.asarray(inp["x_prompt"]); xsm = np.asarray(inp["x_sample"])
    yp = np.zeros(xp.shape, np.float32)
    ysm = np.zeros(xsm.shape, np.float32)
    for c in range(ncores):
        r = res.results[c]
        b, q = divmod(c, cores_per_seq)
        yp[b, q * cfg.lp_own:(q + 1) * cfg.lp_own, :] = r["yo"].T
        for i in range(cfg.ns):
            ysm[c * cfg.ns + i] = r["ys"][i].T
    return yp, ysm


def kernel(**inputs):
    cfg = Cfg()
    return run(inputs, cfg, 8, 4)
```

```python
import numpy as np
import concourse.bass as bass
import concourse.mybir as mybir

F32 = mybir.dt.float32
BF16 = mybir.dt.bfloat16
AF = mybir.ActivationFunctionType
ALU = mybir.AluOpType
AX = mybir.AxisListType


class Stream:
    def __init__(self, sem):
        self.sem = sem
        self.val = 0


class Prog:
    ENG = ("pe", "act", "dve", "pool")

    def __init__(self, nc, es):
        self.nc = nc
        self.es = es
        self.eng = {"pe": nc.tensor, "act": nc.scalar, "dve": nc.vector,
                    "pool": nc.gpsimd, "sp": nc.sync}
        self.sem = {e: es.enter_context(nc.semaphore("c_" + e)) for e in self.ENG}
        self.cnt = {e: 0 for e in self.ENG}
        self.waited = {e: {} for e in ("pe", "act", "dve", "pool", "sp")}
        self.semobj = {"c_" + e: self.sem[e] for e in self.ENG}
        self.tab = {}
        self.streams = []
        self._snames = {}
        self.nops = 0

    NDS = 24

    def stream(self, name):
        return None

    def _dma_sem(self):
        if not self.streams:
            for i in range(self.NDS):
                st = Stream(self.es.enter_context(self.nc.semaphore("d_%d" % i)))
                st.name = "d_%d" % i
                self.semobj[st.name] = st.sem
                self.streams.append(st)
            self._dsi = 0
        st = self.streams[self._dsi % self.NDS]
        self._dsi += 1
        return st

    def _entries(self, buf, key):
        t = self.tab.setdefault(buf, {})
        if key is None:
            return list(t.values())
        out = []
        if key in t:
            out.append(t[key])
        if None in t:
            out.append(t[None])
        return out

    def _deps(self, eng, reads, writes):
        deps = []
        for ap, key in reads:
            for ent in self._entries(ap.tensor.name, key):
                if ent[0] is not None:
                    deps.append((ent[0], True))
        for ap, key in writes:
            for ent in self._entries(ap.tensor.name, key):
                if ent[0] is not None:
                    deps.append((ent[0], False))
                for tok in ent[1].values():
                    deps.append((tok, False))
        return deps

    def _record(self, eng, tok, reads, writes):
        for ap, key in reads:
            t = self.tab.setdefault(ap.tensor.name, {})
            ent = t.setdefault(key, [None, {}])
            ent[1][eng] = tok
        for ap, key in writes:
            t = self.tab.setdefault(ap.tensor.name, {})
            if key is None:
                t.clear()
                t[None] = [tok, {}]
            else:
                t[key] = [tok, {}]

    @staticmethod
    def _autokey(ap):
        if not ap.tensor.name.startswith("pbd"):
            return None
        a = ap.ap
        c0 = ap.offset % a[0][0]
        c1 = c0 + sum((cnt - 1) * st for st, cnt in a[1:]) + 1
        if c1 <= 512:
            return 0
        if c0 >= 512:
            return 1
        return None

    def _norm(self, lst):
        out = []
        for x in lst:
            if not isinstance(x, tuple):
                x = (x, None)
            if x[1] is None:
                x = (x[0], self._autokey(x[0]))
            out.append(x)
        return out

    def _do_waits(self, eng, deps):
        need = {}
        for (semname, val, peng), raw in deps:
            if peng == eng:
                if eng not in ("act", "dve", "pool"):
                    continue
            if need.get(semname, 0) < val:
                need[semname] = val
        w = self.waited[eng]
        e = self.eng[eng]
        for semname, val in need.items():
            if w.get(semname, 0) >= val:
                continue
            w[semname] = val
            e.wait_ge(self.semobj[semname], val)

    def op(self, eng, fn, reads=(), writes=()):
        reads = self._norm(reads)
        writes = self._norm(writes)
        deps = self._deps(eng, reads, writes)
        self._do_waits(eng, deps)
        ins = fn(self.eng[eng])
        self.cnt[eng] += 1
        ins.then_inc(self.sem[eng], 1)
        tok = ("c_" + eng, self.cnt[eng], eng)
        self._record(eng, tok, reads, writes)
        self.nops += 1
        return tok

    def dma(self, out, in_, stream=None, rk=None, wk=None, q="sp", **kw):
        reads = [(in_, rk)]
        writes = [(out, wk)]
        st = self._dma_sem()
        deps = self._deps(q, reads, writes)
        if st.val:
            deps.append(((st.name, st.val, "dma"), False))
        self._do_waits(q, deps)
        ins = self.eng[q].dma_start(out=out, in_=in_, **kw)
        st.val += 16
        ins.then_inc(st.sem, 16)
        tok = (st.name, st.val, "dma")
        self._record("dma:" + st.name, tok, reads, writes)
        self.nops += 1
        return tok

    def finish(self):
        sp = self.eng["sp"]
        for s in self.streams:
            if s.val:
                sp.wait_ge(s.sem, s.val)
        for e in self.ENG:
            if self.cnt[e]:
                sp.wait_ge(self.sem[e], self.cnt[e])

    def barrier(self):
        for e in ("pe", "act", "dve", "pool", "sp"):
            h = self.eng[e]
            w = self.waited[e]
            for p in self.ENG:
                if p == e or self.cnt[p] == 0:
                    continue
                nm = "c_" + p
                if w.get(nm, 0) < self.cnt[p]:
                    w[nm] = self.cnt[p]
                    h.wait_ge(self.sem[p], self.cnt[p])
            for s in self.streams:
                if s.val and w.get(s.name, 0) < s.val:
                    w[s.name] = s.val
                    h.wait_ge(s.sem, s.val)
        self.tab.clear()

    def mm(self, out, lhsT, rhs, start=True, stop=True, ko=None, kl=None, kr=None, **kw):
        return self.op("pe", lambda e: e.matmul(out, lhsT, rhs, start=start, stop=stop, **kw),
                       reads=[(lhsT, kl), (rhs, kr)], writes=[(out, ko)])

    def tr(self, out, in_, ident, ko=None, ki=None):
        return self.op("pe", lambda e: e.transpose(out, in_, ident),
                       reads=[(in_, ki), (ident, None)], writes=[(out, ko)])

    def act(self, out, in_, func, bias=None, scale=None, accum_out=None, ko=None, ki=None,
            eng="act", extra_reads=()):
        kw = {}
        reads = [(in_, ki)] + list(extra_reads)
        if bias is not None:
            kw["bias"] = bias
            if not isinstance(bias, (int, float)):
                reads.append((bias, None))
        if scale is not None:
            kw["scale"] = scale
            if not isinstance(scale, (int, float)):
                reads.append((scale, None))
        writes = [(out, ko)]
        if accum_out is not None:
            kw["accum_out"] = accum_out
            writes.append((accum_out, None))
        return self.op("act", lambda e: e.activation(out, in_, func, **kw), reads=reads, writes=writes)

    def tt(self, out, in0, in1, op, eng="dve", ko=None, k0=None, k1=None):
        return self.op(eng, lambda e: e.tensor_tensor(out, in0, in1, op),
                       reads=[(in0, k0), (in1, k1)], writes=[(out, ko)])

    def ts(self, out, in0, s1, s2=None, op0=ALU.mult, op1=None, eng="dve", ko=None, k0=None,
           accum_out=None):
        reads = [(in0, k0)]
        if not isinstance(s1, (int, float)):
            reads.append((s1, None))
        if s2 is not None and not isinstance(s2, (int, float)):
            reads.append((s2, None))
        kw = {}
        if op1 is not None:
            kw["op1"] = op1
        writes = [(out, ko)]
        if accum_out is not None:
            kw["accum_out"] = accum_out
            writes.append((accum_out, None))
        return self.op(eng, lambda e: e.tensor_scalar(out, in0, s1, s2, op0, **kw),
                       reads=reads, writes=writes)

    def stt(self, out, in0, scalar, in1, op0, op1, eng="dve", ko=None, k0=None, k1=None):
        reads = [(in0, k0), (in1, k1)]
        if not isinstance(scalar, (int, float)):
            reads.append((scalar, None))
        return self.op(eng, lambda e: e.scalar_tensor_tensor(out, in0, scalar, in1, op0, op1),
                       reads=reads, writes=[(out, ko)])

    def copy(self, out, in_, eng="dve", ko=None, ki=None):
        if eng == "act":
            return self.act(out, in_, AF.Copy, ko=ko, ki=ki)
        return self.op(eng, lambda e: e.tensor_copy(out, in_), reads=[(in_, ki)], writes=[(out, ko)])

    def memset(self, ap, val, eng="dve", k=None):
        return self.op(eng, lambda e: e.memset(ap, val), writes=[(ap, k)])

    def recip(self, out, in_, ko=None, ki=None):
        return self.op("dve", lambda e: e.reciprocal(out, in_), reads=[(in_, ki)], writes=[(out, ko)])
from concourse.bass_utils import run_bass_kernel_spmd
from contextlib import ExitStack

EPS = 1e-6
D = 1024
KD = 8
TT = 512
HALO = 2
TW = TT + 2 * HALO
QSCALE = 96 ** -0.5


class Cfg:
    def __init__(self, lp_own=4096, nnon=24, ls=2048, ns=4, dff=2816):
        self.lp_own, self.nnon, self.ls, self.ns, self.dff = lp_own, nnon, ls, ns, dff
        self.nfb = dff // 128
        self.ng = dff // 256
        self.lmax = max(lp_own, ls)


def build(cfg):
    nc = bass.Bass("TRN2", target_bir_lowering=False)
    c = cfg
    NFB, NG = c.nfb, c.ng

    def din(name, shape, dt=F32):
        return nc.dram_tensor(name, list(shape), dt, kind="ExternalInput").ap()

    xo = din("xo", [D, max(c.lp_own, TT) + 4])
    xn = din("xn", [max(c.nnon, 1), D, TW])
    mk = din("mk", [128, max(c.nnon, 1) * 2])
    rpo = din("rpo", [2, 32, max(c.lp_own, TT)])
    rpn = din("rpn", [max(c.nnon, 1), 2, 32, TT])
    rs = din("rs", [2, 32, c.ls])
    xs = din("xs", [max(c.ns, 1), D, c.ls + 4])
    w_in = din("w_in", [D, 1968])
    w_q_b = din("w_q_b", [256, 768])
    w_kv_b = din("w_kv_b", [128, 1024])
    w_out = din("w_out", [D, D])
    w_gate = din("w_gate", [D, c.dff])
    w_up = din("w_up", [D, c.dff])
    w_down = din("w_down", [c.dff, D])
    gv = din("gv", [128, 88])
    rowv = din("rowv", [1, 40])
    cm = din("cm", [128, 6, 128])
    yo = nc.dram_tensor("yo", [D, max(c.lp_own, TT)], F32, kind="ExternalOutput").ap()
    ys = nc.dram_tensor("ys", [max(c.ns, 1), D, c.ls], F32, kind="ExternalOutput").ap()
    wg_s = nc.dram_tensor("wg_s", [NG, 128, KD, 256], BF16, kind="ExternalOutput").ap()
    wu_s = nc.dram_tensor("wu_s", [NG, 128, KD, 256], BF16, kind="ExternalOutput").ap()
    wd_s = nc.dram_tensor("wd_s", [8, 128, NFB, 128], BF16, kind="ExternalOutput").ap()

    with ExitStack() as es:
        P = Prog(nc, es)

        uid = {"n": 0}

        def sbuf(stack, name, shape, dt=F32):
            uid["n"] += 1
            return stack.enter_context(nc.sbuf_tensor("%s_%d" % (name, uid["n"]), list(shape), dt))

        pbd = [es.enter_context(nc.psum_tensor(f"pbd{i}", [128, 1024], F32)) for i in range(4)]

        class Bank:
            def __init__(self, t, half):
                self.t, self.h = t, half

            def __getitem__(self, idx):
                if not isinstance(idx, tuple):
                    idx = (idx,)
                cs = idx[1] if len(idx) > 1 else slice(None)
                a = (cs.start or 0) + self.h * 512
                b = (cs.stop if cs.stop is not None else 512) + self.h * 512
                return self.t[idx[0], a:b]

        pb = [Bank(pbd[i // 2], i % 2) for i in range(8)]
        psrr = {"i": 0}

        def nps(lo=0, hi=8):
            k = "%d_%d" % (lo, hi)
            i = psrr.get(k, lo)
            psrr[k] = lo + ((i - lo + 1) % (hi - lo))
            return pb[i]

        G = {}
        gvt = sbuf(es, "gvt", [128, 88])
        rowt = sbuf(es, "rowt", [128, 40])
        cmt = sbuf(es, "cmt", [128, 6, 128])
        mneg = sbuf(es, "mneg", [128, 2, 128], BF16)
        identf = sbuf(es, "identf", [128, 128])
        identb = sbuf(es, "identb", [128, 128], BF16)
        onesb = sbuf(es, "onesb", [128, 128], BF16)
        onesf = sbuf(es, "onesf", [128, 128])
        abt = sbuf(es, "abt", [128, 16])
        sq2 = [sbuf(es, f"sq{i}", [128, TT], BF16) for i in range(3)]
        lnb = sbuf(es, "lnb", [128, TT])
        rr = {"sq": 0}

        st_c = P.stream("const")
        P.dma(gvt[:], gv[:, :], st_c)
        P.dma(rowt[:], rowv[0:1, :].partition_broadcast(128), st_c)
        P.dma(cmt[:], cm[:, :, :], st_c)
        P.copy(mneg[:], cmt[:, 4:6, :], eng="dve")
        P.memset(identf[:], 1.0, eng="pool")
        P.op("pool", lambda e: e.affine_select(identf[:], identf[:], pattern=[[-1, 128]],
                                               compare_op=ALU.is_equal, fill=0.0, base=0,
                                               channel_multiplier=1),
             reads=[identf[:]], writes=[identf[:]])
        P.copy(identb[:], identf[:], eng="dve")
        P.memset(onesb[:], 1.0, eng="dve")
        P.memset(onesf[:], 1.0, eng="dve")
        P.act(abt[:], rowt[:, 16:32], AF.Exp)
        P.ts(abt[:], abt[:], -1.0, None, op0=ALU.mult)
        G_ATTN, G_FFN, G_FIN, G_QA, G_KVA, G_AO, G_SSD, G_CB, G_CW = 0, 8, 16, 24, 26, 27, 31, 35, 43
        dtb = rowt[:, 0:16]
        dsk = rowt[:, 32:40]

        def rstd_fm(srcs, Dn, N, out_ap):
            ps = nps(0, 6)
            for i, s in enumerate(srcs):
                sq = sq2[rr["sq"] % 3]
                rr["sq"] += 1
                P.act(sq[:, :N], s, AF.Square)
                P.mm(ps[:, :N], onesb[:], sq[:, :N], start=(i == 0), stop=(i == len(srcs) - 1))
            P.act(lnb[:, :N], ps[:, :N], AF.Ln, bias=EPS, scale=1.0 / Dn)
            P.act(out_ap, lnb[:, :N], AF.Exp, scale=-0.5)

        stg = [sbuf(es, f"stg{i}", [128, KD, 128]) for i in range(2)]
        st_w = [P.stream("w0"), P.stream("w1")]
        wrr = {"i": 0}

        def prep_w(dst_fn, src, k_chunks, c0, ncols, gcol, scale=1.0):
            for a in range(0, ncols, 128):
                b = min(a + 128, ncols)
                i = wrr["i"] % 2
                wrr["i"] += 1
                P.dma(stg[i][:, 0:k_chunks, 0:b - a],
                      src[0:k_chunks * 128, c0 + a:c0 + b].rearrange("(k p) n -> p k n", p=128), st_w[i])
                P.tt(dst_fn(a, b), stg[i][:, 0:k_chunks, 0:b - a],
                     gvt[:, gcol:gcol + k_chunks].unsqueeze(2).to_broadcast([128, k_chunks, b - a]), ALU.mult)

        with ExitStack() as ph:
            wtmp = [sbuf(ph, f"wtmp{i}", [128, KD, 256], BF16) for i in range(2)]
            wdt = [sbuf(ph, f"wdt{i}", [128, NFB, 128], BF16) for i in range(2)]
            wdf = [sbuf(ph, f"wdf{i}", [128, NFB, 128]) for i in range(2)]
            st_o = [P.stream("wo0"), P.stream("wo1")]
            n = 0
            for (src, dst) in ((w_gate, wg_s), (w_up, wu_s)):
                for g in range(NG):
                    t = wtmp[n % 2]
                    prep_w(lambda a, b, t=t: t[:, :, a:b], src, KD, g * 256, 256, G_FFN)
                    P.dma(dst[g, :, :, :], t[:], st_o[n % 2])
                    n += 1
            for ob in range(8):
                i = ob % 2
                P.dma(wdf[i][:], w_down[:, ob * 128:(ob + 1) * 128].rearrange("(k p) n -> p k n", p=128),
                      st_w[i])
                P.copy(wdt[i][:], wdf[i][:], eng="act" if ob % 2 else "dve")
                P.dma(wd_s[ob, :, :, :], wdt[i][:], st_o[i])
            P.barrier()

        def load_x(dst, src_cols, stream):
            P.dma(dst, src_cols.rearrange("(k p) n -> p k n", p=128), stream)

        def norm_tile(xt, hn, rst, N):
            pieces = [(0, min(N, TT))] + ([(TT, N)] if N > TT else [])
            for (a, b) in pieces:
                rstd_fm([xt[:, kc, a:b] for kc in range(KD)], D, b - a, rst[:, a:b])
            for kc in range(KD):
                P.tt(hn[:, kc, 0:N], xt[:, kc, 0:N], rst[:, 0:N], ALU.mult, ko=kc)

        def mla_phase(own_src, lown, non_src, nnon, rope_own, rope_non):
            nto = lown // TT
            ltot = lown + nnon * TT
            ntt = ltot // TT
            with ExitStack() as ph:
                wqb = sbuf(ph, "wqb", [128, 2, 768], BF16)
                wqr = sbuf(ph, "wqr", [128, 2, 8, 96], BF16)
                wkvb = sbuf(ph, "wkvb", [128, 1024], BF16)
                ckvn = sbuf(ph, "ckvn", [128, ltot], BF16)
                KT = sbuf(ph, "KT", [96, ltot], BF16)
                qlatn = sbuf(ph, "qlatn", [128, 2, lown], BF16)
                rst2 = sbuf(ph, "mrst2", [128, TT])
                rtab = [sbuf(ph, f"rtab{i}", [96, 2, TT]) for i in range(2)]
                t1 = sbuf(ph, "mt1", [96, TT])
                t2 = sbuf(ph, "mt2", [96, TT])
                ph1 = ExitStack()
                w_mla = sbuf(ph1, "w_mla", [128, KD, 576], BF16)
                xt2 = [sbuf(ph1, f"mxt{i}", [128, KD, TT]) for i in range(2)]
                hn = sbuf(ph1, "mhn", [128, KD, TT], BF16)
                rst = sbuf(ph1, "mrst", [128, TT])
                st_x = [P.stream("mx0"), P.stream("mx1")]
                st_r = [P.stream("mr0"), P.stream("mr1")]

                prep_w(lambda a, b: w_mla[:, :, a:b], w_in, KD, 0, 384, G_ATTN)
                P.memset(w_mla[:, :, 384:576], 0.0, eng="pool")
                prep_w(lambda a, b: w_mla[:, :, 448 + a:448 + b], w_in, KD, 384, 32, G_ATTN)
                for kc in range(KD):
                    P.ts(w_mla[:, kc, 544:560], w_mla[:, kc, 464:480], -1.0, None, op0=ALU.mult)
                    P.copy(w_mla[:, kc, 560:576], w_mla[:, kc, 448:464], eng="pool")
                prep_w(lambda a, b: wqb[:, :, a:b], w_q_b, 2, 0, 768, G_QA)
                P.memset(wqr[:], 0.0, eng="pool")
                for kc in range(2):
                    for h in range(8):
                        P.ts(wqr[:, kc, h, 64:80], wqb[:, kc, h * 96 + 80:h * 96 + 96], -1.0, None,
                             op0=ALU.mult)
                        P.copy(wqr[:, kc, h, 80:96], wqb[:, kc, h * 96 + 64:h * 96 + 80], eng="pool")
                prep_w(lambda a, b: wkvb[:, a:b].unsqueeze(1), w_kv_b, 1, 0, 1024, G_KVA)

                def tile_src(t):
                    if t < nto:
                        return own_src[:, HALO + t * TT:HALO + (t + 1) * TT], \
                            rope_own[:, :, t * TT:(t + 1) * TT]
                    s = t - nto
                    return non_src[s, :, HALO:HALO + TT], rope_non[s, :, :, :]

                def issue_load(t):
                    xs_, rp_ = tile_src(t)
                    load_x(xt2[t % 2][:], xs_, st_x[t % 2])
                    P.dma(rtab[t % 2][64:96, :, :], rp_.rearrange("a p n -> p a n"), st_r[t % 2])

                issue_load(0)
                for t in range(ntt):
                    if t + 1 < ntt:
                        issue_load(t + 1)
                    xt = xt2[t % 2]
                    rt = rtab[t % 2]
                    norm_tile(xt, hn, rst, TT)
                    cols = slice(t * TT, (t + 1) * TT)
                    pc = nps(0, 6)
                    for kc in range(KD):
                        P.mm(pc[:], w_mla[:, kc, 256:384], hn[:, kc, :], start=(kc == 0), stop=(kc == KD - 1))
                    rstd_fm([pc[:]], 128, TT, rst2[:])
                    P.tt(ckvn[:, cols], pc[:], rst2[:], ALU.mult, ko=t)
                    pa = nps(0, 6)
                    pr = nps(0, 6)
                    for kc in range(KD):
                        P.mm(pa[0:96, :], w_mla[:, kc, 384:480], hn[:, kc, :], start=(kc == 0), stop=(kc == KD - 1))
                    for kc in range(KD):
                        P.mm(pr[0:96, :], w_mla[:, kc, 480:576], hn[:, kc, :], start=(kc == 0), stop=(kc == KD - 1))
                    P.tt(t1[64:96, :], pa[64:96, :], rt[64:96, 0, :], ALU.mult)
                    P.tt(t2[64:96, :], pr[64:96, :], rt[64:96, 1, :], ALU.mult)
                    P.tt(KT[64:96, cols], t1[64:96, :], t2[64:96, :], ALU.add, ko=("r", t))
                    if t < nto:
                        pq = [nps(0, 6), nps(0, 6)]
                        for cq in range(2):
                            for kc in range(KD):
                                P.mm(pq[cq][:], w_mla[:, kc, cq * 128:(cq + 1) * 128], hn[:, kc, :],
                                     start=(kc == 0), stop=(kc == KD - 1))
                        rstd_fm([pq[0][:], pq[1][:]], 256, TT, rst2[:])
                        for cq in range(2):
                            P.tt(qlatn[:, cq, cols], pq[cq][:], rst2[:], ALU.mult, ko=t)

                P.barrier()
                ph1.close()
                VH = sbuf(ph, "VH", [128, ltot // 128, 65], BF16)
                QH = sbuf(ph, "QH", [96, lown], BF16)
                PT = [sbuf(ph, f"PT{i}", [128, 2 * TT], BF16) for i in range(4)]
                osb2 = [sbuf(ph, f"osb{i}", [64, TT]) for i in range(2)]
                rc2 = [sbuf(ph, f"rc{i}", [65, TT]) for i in range(2)]
                fin = {"f": None, "n": 0}
                pbfin = pb[5]
                P.memset(VH[:, :, 64:65], 1.0, eng="pool")
                nkb = ltot // 128
                for h in range(8):
                    for t in range(ntt):
                        cols = slice(t * TT, (t + 1) * TT)
                        pk = nps(0, 6)
                        P.mm(pk[0:64, :], wkvb[:, h * 128:h * 128 + 64], ckvn[:, cols], kr=t)
                        P.copy(KT[0:64, cols], pk[0:64, :], eng="act" if t % 2 else "dve", ko=("n", t))
                    for g8 in range(0, nkb, 8):
                        pv = nps(0, 6)
                        for j in range(8):
                            kb = g8 + j
                            P.mm(pv[:, j * 64:(j + 1) * 64], ckvn[:, kb * 128:(kb + 1) * 128],
                                 wkvb[:, h * 128 + 64:h * 128 + 128], kl=kb // 4)
                        P.copy(VH[:, g8:g8 + 8, 0:64], pv[:, :].rearrange("p (j d) -> p j d", j=8),
                               eng="dve" if (g8 // 8) % 2 else "act", ko=g8 // 8)
                    for t in range(nto):
                        cols = slice(t * TT, (t + 1) * TT)
                        rt = rtab[t % 2]
                        P.dma(rt[64:96, :, :], rope_own[:, :, cols].rearrange("a p n -> p a n"), st_r[t % 2])
                        pa = nps(0, 6)
                        pr = nps(0, 6)
                        for kc in range(2):
                            P.mm(pa[0:96, :], wqb[:, kc, h * 96:(h + 1) * 96], qlatn[:, kc, cols],
                                 start=(kc == 0), stop=(kc == 1), kr=t)
                        for kc in range(2):
                            P.mm(pr[0:96, :], wqr[:, kc, h, :], qlatn[:, kc, cols],
                                 start=(kc == 0), stop=(kc == 1), kr=t)
                        P.copy(QH[0:64, cols], pa[0:64, :], eng="act", ko=("n", t))
                        P.tt(t1[64:96, :], pa[64:96, :], rt[64:96, 0, :], ALU.mult)
                        P.tt(t2[64:96, :], pr[64:96, :], rt[64:96, 1, :], ALU.mult)
                        P.tt(QH[64:96, cols], t1[64:96, :], t2[64:96, :], ALU.add, ko=("r", t))
                    for t in range(nto):
                        cols = slice(t * TT, (t + 1) * TT)
                        po = pb[6 + (t % 2)]
                        LOOK = 2
                        npair = nkb // 2
                        for idx in range(npair + LOOK):
                            if idx < npair:
                                pd_ = pbd[idx % 3]
                                for j in range(2):
                                    kb = idx * 2 + j
                                    tk = kb // 4
                                    P.op("pe", lambda e, pd_=pd_, kb=kb, j=j, cols=cols: e.matmul(
                                        pd_[:, j * 512:(j + 1) * 512], KT[0:96, kb * 128:(kb + 1) * 128],
                                        QH[0:96, cols], start=True, stop=True),
                                        reads=[(KT[:], ("n", tk)), (KT[:], ("r", tk)), (QH[:], ("n", t)),
                                               (QH[:], ("r", t))],
                                        writes=[(pd_[:, j * 512:(j + 1) * 512], None)])
                                P.act(PT[idx % 4][:], pd_[:, :], AF.Exp, scale=QSCALE)
                            if idx == LOOK and fin["f"] is not None:
                                fin["f"]()
                                fin["f"] = None
                            if idx >= LOOK:
                                pi = idx - LOOK
                                for j in range(2):
                                    kb = pi * 2 + j
                                    P.mm(po[0:65, :], VH[:, kb, :], PT[pi % 4][:, j * 512:(j + 1) * 512],
                                         start=(kb == 0), stop=(kb == nkb - 1), kl=kb // 8)
                        ob_ = osb2[fin["n"] % 2]
                        rc_ = rc2[fin["n"] % 2]
                        fin["n"] += 1
                        P.act(rc_[64:65, :], po[64:65, :], AF.Ln)
                        P.act(rc_[64:65, :], rc_[64:65, :], AF.Exp, scale=-1.0)
                        P.copy(ob_[0:64, :], po[0:64, :], eng="dve")

                        def finalize(ob_=ob_, rc_=rc_, h=h, cols=cols, t=t):
                            pbc = pbfin
                            P.mm(pbc[0:64, :], onesf[64:65, 0:64], rc_[64:65, :])
                            P.tt(G["mlaT"][(h % 2) * 64:(h % 2) * 64 + 64, h // 2, cols], ob_[0:64, :],
                                 pbc[0:64, :], ALU.mult, ko=(h, t))
                        fin["f"] = finalize
                if fin["f"] is not None:
                    fin["f"]()
                    fin["f"] = None
                for t in range(nto):
                    cols = slice(t * TT, (t + 1) * TT)
                    rstd_fm([G["mlaT"][:, cc, cols] for cc in range(4)], 512, TT, rst2[:])
                    for cc in range(4):
                        P.tt(G["mlaT"][:, cc, cols], G["mlaT"][:, cc, cols], rst2[:], ALU.mult,
                             eng="pool" if cc % 2 else "dve")
                P.barrier()

        def ssd_phase(own_src, own_t0, nto, slots, out_t0):
            nch = (nto if nto <= 4 else (nto + 1) // 2) * 4
            with ExitStack() as ph:
                w_ssd = sbuf(ph, "w_ssd", [128, KD, 1552], BF16)
                Sst = [sbuf(ph, f"Sst{d}", [128, 512]) for d in range(2)]
                SBW = sbuf(ph, "SBW", [128, nch, 512], BF16)
                xtb = [sbuf(ph, f"sxt{i}", [128, KD, TW]) for i in range(1)]
                dgw = sbuf(ph, "dgw", [128, 8, 5, 128], BF16)
                preb = [sbuf(ph, f"preb{i}", [128, TW], BF16) for i in range(2)]
                hn = sbuf(ph, "shn", [128, KD, TW], BF16)
                rst = sbuf(ph, "srst", [128, TW])
                xact = sbuf(ph, "xact", [128, 8, TT], BF16)
                x_tok = sbuf(ph, "x_tok", [128, 4, 512], BF16)
                B_tok = sbuf(ph, "B_tok", [128, 4, 256], BF16)
                z_tok = sbuf(ph, "z_tok", [128, 4, 512], BF16)
                dt_tok = sbuf(ph, "dt_tok", [128, 4, 16])
                dtm = sbuf(ph, "dtm", [128, 4, 16])
                sm = [sbuf(ph, f"sm{i}", [128, 16]) for i in range(16)]
                xw4 = sbuf(ph, "xw4", [128, 4, 512], BF16)
                dA4 = sbuf(ph, "dA4", [128, 4, 8])
                w4 = sbuf(ph, "w4", [128, 4, 8])
                ex4 = sbuf(ph, "ex4", [128, 64])
                Sfin = sbuf(ph, "Sfin", [128, 4, 512], BF16)
                Ssnap = sbuf(ph, "Ssnap", [128, 512])
                xdt = [sbuf(ph, f"xdt{d}", [128, 512], BF16) for d in range(2)]
                GT = sbuf(ph, "GT", [128, 2, 128], BF16)
                Dm = [sbuf(ph, f"Dm{i}", [128, 512], BF16) for i in range(2)]
                Mm = [sbuf(ph, f"Mm{i}", [128, 512], BF16) for i in range(2)]
                ya = sbuf(ph, "ya", [128, 512])
                yb = sbuf(ph, "yb", [128, 512])
                stok = sbuf(ph, "stok", [128, 512], BF16)
                st_x = P.stream("sx")
                smr = {"i": 0}

                def smn():
                    smr["i"] += 1
                    return sm[smr["i"] % 16]

                prep_w(lambda a, b: w_ssd[:, :, a:b], w_in, KD, 416, 1552, G_ATTN)
                P.memset(Sst[0][:], 0.0)
                P.memset(Sst[1][:], 0.0)
                for cc in range(8):
                    for k in range(5):
                        P.ts(dgw[:, cc, k, :], identf[:], gvt[:, G_CW + cc * 5 + k:G_CW + cc * 5 + k + 1], None,
                             op0=ALU.mult)
                tl = {"i": 0, "q": []}

                def prefetch(src):
                    i = tl["i"]
                    tl["i"] += 1
                    load_x(xtb[0][:], src, st_x)
                    tl["q"].append(xtb[0])

                def tile_front(src, full):
                    xt = tl["q"].pop(0)
                    import os
                    dbg = os.environ.get("SSD_DBG", "z")
                    if dbg == "0":
                        return
                    norm_tile(xt, hn, rst, TW)
                    if tl["next"] is not None:
                        prefetch(tl["next"])
                        tl["next"] = None
                    if dbg == "a":
                        return
                    ncc = 8 if full else 6
                    for cc in range(ncc):
                        pm = nps(0, 6)
                        ph_ = nps(0, 6)
                        col = 512 + cc * 128
                        for kc in range(KD):
                            P.mm(pm[:], w_ssd[:, kc, col:col + 128], hn[:, kc, 0:TT],
                                 start=(kc == 0), stop=(kc == KD - 1))
                        for kc in range(KD):
                            P.mm(ph_[:, 0:4], w_ssd[:, kc, col:col + 128], hn[:, kc, TT:TW],
                                 start=(kc == 0), stop=(kc == KD - 1))
                        pr_ = preb[cc % 2]
                        P.copy(pr_[:, 0:TT], pm[:], eng="act")
                        P.copy(pr_[:, TT:TW], ph_[:, 0:4], eng="dve")
                        pcv = nps(0, 6)
                        for k in range(5):
                            P.mm(pcv[:], dgw[:, cc, k, :], pr_[:, k:k + TT], start=(k == 0), stop=(k == 4))
                        P.act(xact[:, cc, :], pcv[:], AF.Silu, bias=gvt[:, G_CB + cc:G_CB + cc + 1])
                    if dbg == "b":
                        return
                    for tb in range(4):
                        pd = nps(0, 6)
                        for kc in range(KD):
                            P.mm(pd[:, 0:16], hn[:, kc, HALO + tb * 128:HALO + (tb + 1) * 128],
                                 w_ssd[:, kc, 1536:1552], start=(kc == 0), stop=(kc == KD - 1))
                        v = smn()
                        P.tt(v[:], pd[:, 0:16], dtb, ALU.add)
                        a_ = smn()
                        P.act(a_[:], v[:], AF.Abs)
                        e_ = smn()
                        P.act(e_[:], a_[:], AF.Exp, scale=-1.0)
                        l_ = smn()
                        P.act(l_[:], e_[:], AF.Ln, bias=1.0)
                        P.ts(v[:], v[:], 0.0, None, op0=ALU.max)
                        P.tt(dt_tok[:, tb, :], v[:], l_[:], ALU.add, ko=tb)
                    if full:
                        for tb in range(4):
                            pz = nps(0, 6)
                            for kc in range(KD):
                                P.mm(pz[:], hn[:, kc, HALO + tb * 128:HALO + (tb + 1) * 128],
                                     w_ssd[:, kc, 0:512], start=(kc == 0), stop=(kc == KD - 1))
                            P.act(z_tok[:, tb, :], pz[:], AF.Silu, ko=tb)
                    if dbg == "c":
                        return
                    for tb in range(4):
                        pt_ = nps(0, 6)
                        pt2 = nps(0, 6)
                        for cc in range(4):
                            P.mm(pt_[:, cc * 128:(cc + 1) * 128], xact[:, cc, tb * 128:(tb + 1) * 128], identb[:])
                        for cc in range(2):
                            P.mm(pt2[:, cc * 128:(cc + 1) * 128], xact[:, 4 + cc, tb * 128:(tb + 1) * 128],
                                 identb[:])
                        P.copy(x_tok[:, tb, :], pt_[:, 0:512], eng="dve", ko=tb)
                        P.copy(B_tok[:, tb, :], pt2[:, 0:256], eng="act", ko=tb)

                def bc8(ap8):
                    return ap8.unsqueeze(2).to_broadcast([128, 8, 64])

                def v8(ap512):
                    return ap512.rearrange("p (h d) -> p h d", h=8)

                def su_batch(d, dsrc, order, save=None):
                    dv = dsrc[:, :, d * 8:(d + 1) * 8]
                    P.tt(dA4[:], dv, abt[:, d * 8:(d + 1) * 8].unsqueeze(1).to_broadcast([128, 4, 8]), ALU.mult)
                    pp = nps(0, 6)
                    dflat = dA4[:, :, :].rearrange("p c h -> p (c h)")
                    P.mm(pp[:, 0:32], cmt[:, 2 + d, :], dflat)
                    P.mm(pp[:, 32:64], onesf[:], dflat)
                    P.act(ex4[:], pp[:, 0:64], AF.Exp)
                    P.tt(w4[:], dv, ex4[:, 0:32].rearrange("p (c h) -> p c h", c=4), ALU.mult)
                    P.tt(xw4[:, :, :].rearrange("p c (h e) -> p c h e", h=8),
                         x_tok[:, :, :].rearrange("p c (h e) -> p c h e", h=8),
                         w4[:, :, :].unsqueeze(3).to_broadcast([128, 4, 8, 64]), ALU.mult)
                    S = Sst[d]
                    for tb in order:
                        pc = nps(0, 6)
                        for g in range(2):
                            P.mm(pc[:, g * 256:(g + 1) * 256], B_tok[:, tb, g * 128:(g + 1) * 128],
                                 xw4[:, tb, g * 256:(g + 1) * 256])
                        if save is not None:
                            P.copy(save(tb), S[:], eng="act")
                        P.tt(v8(S[:]), v8(S[:]), bc8(ex4[:, 32 + tb * 8:32 + tb * 8 + 8]), ALU.mult)
                        P.tt(S[:], S[:], pc[:], ALU.add)

                def full_chunk(tb, ci, out_cols):
                    pg = nps(0, 6)
                    for g in range(2):
                        P.mm(pg[:, g * 128:(g + 1) * 128], xact[:, 4 + g, tb * 128:(tb + 1) * 128],
                             xact[:, 6 + g, tb * 128:(tb + 1) * 128])
                    P.copy(GT[:], pg[:, 0:256].rearrange("p (g l) -> p g l", g=2), eng="act")
                    prel = []
                    for d in range(2):
                        dt8 = dt_tok[:, tb, d * 8:(d + 1) * 8]
                        dtA = smn()
                        P.tt(dtA[:, 0:8], dt8, abt[:, d * 8:(d + 1) * 8], ALU.mult, k0=tb)
                        pcs = nps(0, 6)
                        P.mm(pcs[:, 0:8], cmt[:, d, :], dtA[:, 0:8])
                        et = smn()
                        P.act(et[:, 0:8], pcs[:, 0:8], AF.Exp)
                        ncs = smn()
                        P.ts(ncs[:, 0:8], pcs[:, 0:8], -1.0, None, op0=ALU.mult)
                        P.tt(v8(xdt[d][:]), v8(x_tok[:, tb, :]), bc8(dt8), ALU.mult, k0=tb, k1=tb)
                        poff = nps(0, 6)
                        for g in range(2):
                            rhs = Sfin[:, tb, g * 256:(g + 1) * 256] if d == 0 else SBW[:, ci, g * 256:(g + 1) * 256]
                            P.mm(poff[:, g * 256:(g + 1) * 256], xact[:, 6 + g, tb * 128:(tb + 1) * 128], rhs,
                                 kr=(None if d == 0 else ci))
                        tgt = ya if d == 0 else yb
                        P.tt(v8(tgt[:]), v8(poff[:]), bc8(et[:, 0:8]), ALU.mult)
                        prel.append((dtA, ncs))
                    pdgs = [pb[6], pb[7]]
                    groups = [(d, g) for d in range(2) for g in range(2)]

                    def stageA(k):
                        d, g = groups[k]
                        dtA, ncs = prel[d]
                        pcb = nps(0, 6)
                        for j in range(4):
                            h = g * 4 + j
                            P.mm(pcb[:, j * 128:(j + 1) * 128], dtA[:, h:h + 1].to_broadcast([128, 128]),
                                 cmt[:, d, :], start=True, stop=False)
                            P.mm(pcb[:, j * 128:(j + 1) * 128], identb[:], mneg[:, d, :], start=False, stop=True)
                        Dq = Dm[k % 2]
                        for j in range(4):
                            h = g * 4 + j
                            P.act(Dq[:, j * 128:(j + 1) * 128], pcb[:, j * 128:(j + 1) * 128], AF.Exp,
                                  bias=ncs[:, h:h + 1])
                        P.tt(Mm[k % 2][:, :].rearrange("p (j l) -> p j l", j=4),
                             Dq[:, :].rearrange("p (j l) -> p j l", j=4),
                             GT[:, g, :].unsqueeze(1).to_broadcast([128, 4, 128]), ALU.mult)

                    def stageB(k):
                        d, g = groups[k]
                        for j in range(4):
                            h = g * 4 + j
                            P.mm(pdgs[d][:, h * 64:(h + 1) * 64], Mm[k % 2][:, j * 128:(j + 1) * 128],
                                 xdt[d][:, h * 64:(h + 1) * 64])

                    stageA(0)
                    stageA(1)
                    stageB(0)
                    stageA(2)
                    stageB(1)
                    stageA(3)
                    stageB(2)
                    stageB(3)
                    P.tt(ya[:], ya[:], pdgs[0][:], ALU.add)
                    P.tt(yb[:], yb[:], pdgs[1][:], ALU.add)
                    P.tt(ya[:], ya[:], yb[:], ALU.add)
                    P.tt(v8(yb[:]), v8(x_tok[:, tb, :]), bc8(dsk), ALU.mult, k0=tb)
                    P.tt(ya[:], ya[:], yb[:], ALU.add)
                    yg = yb
                    P.tt(yg[:], ya[:], z_tok[:, tb, :], ALU.mult, k1=tb)
                    ss = smn()
                    P.memset(ss[:, 0:1], 0.0)
                    P.act(stok[:], yg[:], AF.Square, accum_out=ss[:, 0:1])
                    l2 = smn()
                    P.act(l2[:, 0:1], ss[:, 0:1], AF.Ln, bias=EPS, scale=1.0 / 512)
                    r2 = smn()
                    P.act(r2[:, 0:1], l2[:, 0:1], AF.Exp, scale=-0.5)
                    P.ts(stok[:], yg[:], r2[:, 0:1], None, op0=ALU.mult)
                    pt_ = nps(0, 6)
                    for cc in range(4):
                        P.mm(pt_[:, cc * 128:(cc + 1) * 128], stok[:, cc * 128:(cc + 1) * 128], identb[:])
                    P.copy(G["ssdT"][:, :, out_cols], pt_[:, 0:512].rearrange("p (c n) -> p c n", c=4), eng="dve",
                           ko=ci)

                sstop = getattr(c, "stop", 0)
                def own_v(kind, t):
                    g0 = (own_t0 + t) * TT
                    return (kind, own_src[:, g0:g0 + TW], t)
                visits = [("slot", src, midx) for (src, midx, fon, bon) in slots]
                if nto <= 4:
                    parts = [(0, nto)]
                else:
                    parts = [(0, nto // 2), (nto // 2, nto)]
                for pi_, (ta, tb_) in enumerate(parts):
                    if len(parts) == 2 and pi_ == 0:
                        visits.append(("snap", None, None))
                        for t in range(nto - 1, tb_ - 1, -1):
                            visits.append(own_v("bwdns", t))
                    if len(parts) == 2 and pi_ == 1:
                        visits.append(("restore", None, None))
                    for t in range(tb_ - 1, ta - 1, -1):
                        visits.append(own_v("bwd", t))
                    for t in range(ta, tb_):
                        visits.append(own_v("full", t))
                tiles = [v for v in visits if v[1] is not None]
                prefetch(tiles[0][1])
                ti = 0
                for v in visits:
                    if v[0] == "snap":
                        P.copy(Ssnap[:], Sst[1][:], eng="dve")
                        continue
                    if v[0] == "restore":
                        P.copy(Sst[1][:], Ssnap[:], eng="dve")
                        continue
                    ti += 1
                    tl["next"] = tiles[ti][1] if ti < len(tiles) else None
                    if v[0] == "slot":
                        midx = v[2]
                        tile_front(v[1], False)
                        P.tt(dtm[:, :, :].rearrange("p c (d h) -> p c d h", d=2),
                             dt_tok[:, :, :].rearrange("p c (d h) -> p c d h", d=2),
                             mkt[:, midx * 2:midx * 2 + 2].unsqueeze(1).unsqueeze(3).to_broadcast([128, 4, 2, 8]),
                             ALU.mult)
                        su_batch(0, dtm, (0, 1, 2, 3))
                        su_batch(1, dtm, (3, 2, 1, 0))
                    elif v[0] == "bwdns":
                        tile_front(v[1], False)
                        su_batch(1, dt_tok, (3, 2, 1, 0))
                    elif v[0] == "bwd":
                        t = v[2]
                        tile_front(v[1], False)
                        lt = t - (parts[-1][0] if t >= parts[-1][0] and len(parts) == 2 else 0)
                        su_batch(1, dt_tok, (3, 2, 1, 0), save=lambda tb, lt=lt: SBW[:, lt * 4 + tb, :])
                    else:
                        t = v[2]
                        tile_front(v[1], True)
                        lt = t - (parts[-1][0] if t >= parts[-1][0] and len(parts) == 2 else 0)
                        su_batch(0, dt_tok, (0, 1, 2, 3), save=lambda tb: Sfin[:, tb, :])
                        for tb in range(4):
                            oc = (out_t0 + t) * TT + tb * 128
                            full_chunk(tb, lt * 4 + tb, slice(oc, oc + 128))
                P.barrier()

        def ffn_phase(own_src, lown, out_dst):
            nto = lown // TT
            with ExitStack() as ph:
                wout = sbuf(ph, "wout", [128, KD, D], BF16)
                xt2 = [sbuf(ph, f"fxt{i}", [128, KD, TT]) for i in range(2)]
                h2 = sbuf(ph, "h2", [128, KD, TT], BF16)
                rst = sbuf(ph, "frst", [128, TT])
                actT = sbuf(ph, "actT", [128, NFB, TT], BF16)
                sg = [sbuf(ph, f"sg{i}", [128, TT]) for i in range(2)]
                wg = [sbuf(ph, f"wg{i}", [128, KD, 256], BF16) for i in range(2)]
                wu = [sbuf(ph, f"wu{i}", [128, KD, 256], BF16) for i in range(2)]
                wd = [sbuf(ph, f"wd{i}", [128, NFB, 128], BF16) for i in range(2)]
                st_x = [P.stream("fx0"), P.stream("fx1")]
                st_g = [P.stream("fg0"), P.stream("fg1")]
                st_d = [P.stream("fd0"), P.stream("fd1")]
                st_y = [P.stream("fy0"), P.stream("fy1")]
                prep_w(lambda a, b: wout[:, 0:4, a:b], w_out, 4, 0, D, G_AO)
                prep_w(lambda a, b: wout[:, 4:8, a:b], w_out[512:1024, :], 4, 0, D, G_SSD)
                load_x(xt2[0][:], own_src[:, HALO:HALO + TT], st_x[0])

                def issue_g(g):
                    P.dma(wg[g % 2][:], wg_s[g, :, :, :], st_g[g % 2])
                    P.dma(wu[g % 2][:], wu_s[g, :, :, :], st_g[g % 2])

                def issue_d(ob):
                    P.dma(wd[ob % 2][:], wd_s[ob, :, :, :], st_d[ob % 2])

                issue_g(0)
                for t in range(nto):
                    if t + 1 < nto:
                        load_x(xt2[(t + 1) % 2][:], own_src[:, HALO + (t + 1) * TT:HALO + (t + 2) * TT],
                               st_x[(t + 1) % 2])
                    xt = xt2[t % 2]
                    cols = slice(t * TT, (t + 1) * TT)
                    for ob in range(8):
                        p_ = nps(0, 8)
                        for kc in range(8):
                            rhs = G["mlaT"][:, kc, cols] if kc < 4 else G["ssdT"][:, kc - 4, cols]
                            P.mm(p_[:], wout[:, kc, ob * 128:(ob + 1) * 128], rhs, start=(kc == 0), stop=(kc == 7))
                        P.tt(xt[:, ob, :], p_[:], xt[:, ob, :], ALU.add)
                    norm_tile(xt, h2, rst, TT)
                    for g in range(NG):
                        i = g % 2
                        if g + 1 < NG:
                            issue_g(g + 1)
                        if g == NG - 1:
                            issue_d(0)
                        for j in range(2):
                            fb = g * 2 + j
                            pgt = nps(0, 8)
                            put = nps(0, 8)
                            for kc in range(KD):
                                P.mm(pgt[:], wg[i][:, kc, j * 128:(j + 1) * 128], h2[:, kc, :],
                                     start=(kc == 0), stop=(kc == KD - 1))
                            for kc in range(KD):
                                P.mm(put[:], wu[i][:, kc, j * 128:(j + 1) * 128], h2[:, kc, :],
                                     start=(kc == 0), stop=(kc == KD - 1))
                            s_ = sg[fb % 2]
                            P.act(s_[:], pgt[:], AF.Silu)
                            P.tt(actT[:, fb, :], s_[:], put[:], ALU.mult, ko=fb)
                    if t + 1 < nto:
                        issue_g(0)
                    for ob in range(8):
                        i = ob % 2
                        if ob + 1 < 8:
                            issue_d(ob + 1)
                        p_ = nps(0, 8)
                        for kc in range(NFB):
                            P.mm(p_[:], wd[i][:, kc, :], actT[:, kc, :], start=(kc == 0), stop=(kc == NFB - 1),
                                 kr=kc)
                        P.tt(xt[:, ob, :], p_[:], xt[:, ob, :], ALU.add)
                    rstd_fm([xt[:, kc, :] for kc in range(KD)], D, TT, rst[:])
                    for ob in range(8):
                        P.stt(xt[:, ob, :], xt[:, ob, :], gvt[:, G_FIN + ob:G_FIN + ob + 1], rst[:],
                              ALU.mult, ALU.mult)
                    P.dma(out_dst[:, cols].rearrange("(k p) n -> p k n", p=128), xt[:], st_y[t % 2])
                P.barrier()

        mkt = sbuf(es, "mkt", [128, max(c.nnon, 1) * 2])
        P.dma(mkt[:], mk[:, :], st_c)
        jobs = []
        if c.lp_own:
            jobs.append(dict(src=xo, lown=c.lp_own, non=xn, nnon=c.nnon, rope=rpo, rnon=rpn, out=yo))
        for s in range(c.ns):
            jobs.append(dict(src=xs[s], lown=c.ls, non=None, nnon=0, rope=rs, rnon=None, out=ys[s]))
        stop = getattr(c, "stop", 0)
        for jb in jobs:
          with ExitStack() as js:
            if stop == 1:
                break
            G["mlaT"] = sbuf(js, "mlaT", [128, 4, jb["lown"]], BF16)
            mla_phase(jb["src"], jb["lown"], jb["non"], jb["nnon"], jb["rope"], jb["rnon"])
            if stop == 2:
                break
            G["ssdT"] = sbuf(js, "ssdT", [128, 4, jb["lown"]], BF16)
            nto = jb["lown"] // TT
            xslots = [(jb["non"][s, :, :], s, True, True) for s in range(jb["nnon"])]
            ssd_phase(jb["src"], 0, nto, xslots, 0)
            if stop >= 3:
                break
            ffn_phase(jb["src"], jb["lown"], jb["out"])
        P.finish()
    return nc
def _rope_tab(pos):
    inv = (10000.0 ** (-np.arange(0, 32, 2, dtype=np.float32) / 32)).astype(np.float32)
    ang = pos.astype(np.float32)[:, None] * inv[None, :]
    co = np.cos(ang).astype(np.float32).T
    si = np.sin(ang).astype(np.float32).T
    return np.stack([np.concatenate([co, co], 0), np.concatenate([si, si], 0)], 0)


def _consts():
    j = np.arange(128)[:, None]
    l = np.arange(128)[None, :]
    cm = np.zeros((128, 6, 128), np.float32)
    cm[:, 0] = (j <= l)
    cm[:, 1] = (j >= l)
    cm[:, 2] = (j > l)
    cm[:, 3] = (j < l)
    cm[:, 4] = np.where(j <= l, 0.0, -30000.0)
    cm[:, 5] = np.where(j >= l, 0.0, -30000.0)
    return cm


def make_in_maps(inp, cfg, ncores, cores_per_seq):
    f = lambda k: np.asarray(inp[k], np.float32)
    xp = f("x_prompt")
    xsm = f("x_sample")
    gv = np.zeros((128, 88), np.float32)
    def put(col, vec):
        v = vec.reshape(-1, 128).T
        gv[:, col:col + v.shape[1]] = v
    put(0, f("attn_norm_g")[0]); put(8, f("ffn_norm_g")[0]); put(16, f("final_norm_g"))
    put(24, f("q_a_norm_g")[0]); put(26, f("kv_a_norm_g")[0]); put(27, f("attn_out_norm_g")[0])
    put(31, f("ssd_norm_g")[0]); put(35, f("conv_b")[0])
    cw = f("conv_w")[0]
    for cc in range(8):
        for k in range(5):
            gv[:, 43 + cc * 5 + k] = cw[k, cc * 128:(cc + 1) * 128]
    rowv = np.concatenate([f("dt_bias")[0].reshape(-1), f("a_log")[0].reshape(-1), f("d_skip")[0].reshape(-1)])[None, :]
    cm = _consts()
    LP = xp.shape[1]
    own = cfg.lp_own
    maps = []
    rs = _rope_tab(np.arange(cfg.ls))
    for c in range(ncores):
        b, q = divmod(c, cores_per_seq)
        xT = np.zeros((D, LP + 4), np.float32)
        xT[:, 2:LP + 2] = xp[b].T
        o0 = q * own
        xo = np.ascontiguousarray(xT[:, o0:o0 + own + 4])
        ntl = LP // TT
        ot0, ot1 = o0 // TT, (o0 + own) // TT
        order = list(range(0, ot0)) + list(range(ntl - 1, ot1 - 1, -1))
        nn = max(len(order), 1)
        xn = np.zeros((nn, D, TW), np.float32)
        mk = np.zeros((128, nn * 2), np.float32)
        rpn = np.zeros((nn, 2, 32, TT), np.float32)
        for s, t in enumerate(order):
            xn[s] = xT[:, t * TT:t * TT + TW]
            mk[:, 2 * s] = 1.0 if t < ot0 else 0.0
            mk[:, 2 * s + 1] = 1.0 if t >= ot1 else 0.0
            rpn[s] = _rope_tab(np.arange(t * TT, (t + 1) * TT))
        rpo = _rope_tab(np.arange(o0, o0 + own))
        xsp = np.zeros((cfg.ns, D, cfg.ls + 4), np.float32)
        for i in range(cfg.ns):
            xsp[i, :, 2:cfg.ls + 2] = xsm[c * cfg.ns + i].T
        maps.append(dict(xo=xo, xn=xn, mk=mk, rpo=rpo, rpn=rpn, rs=rs, xs=xsp,
                         w_in=f("w_in")[0], w_q_b=f("w_q_b")[0], w_kv_b=f("w_kv_b")[0], w_out=f("w_out")[0],
                         w_gate=f("w_gate")[0], w_up=f("w_up")[0], w_down=f("w_down")[0],
                         gv=gv, rowv=rowv.astype(np.float32), cm=cm))
    return maps


def run(inp, cfg, ncores, cores_per_seq):
    nc = build(cfg)
    maps = make_in_maps(inp, cfg, ncores, cores_per_seq)
    res = run_bass_kernel_spmd(nc, maps, core_ids=list(range(ncores)))
    xp = np.asarray(inp["x_prompt"]); xsm = np.asarray(inp["x_sample"])
    yp = np.zeros(xp.shape, np.float32)
    ysm = np.zeros(xsm.shape, np.float32)
    for c in range(ncores):
        r = res.results[c]
        b, q = divmod(c, cores_per_seq)
        yp[b, q * cfg.lp_own:(q + 1) * cfg.lp_own, :] = r["yo"].T
        for i in range(cfg.ns):
            ysm[c * cfg.ns + i] = r["ys"][i].T
    return yp, ysm


def kernel(**inputs):
    cfg = Cfg()
    return run(inputs, cfg, 8, 4)
```

```python
import numpy as np
import concourse.bass as bass
import concourse.mybir as mybir

F32 = mybir.dt.float32
BF16 = mybir.dt.bfloat16
AF = mybir.ActivationFunctionType
ALU = mybir.AluOpType
AX = mybir.AxisListType


class Stream:
    def __init__(self, sem):
        self.sem = sem
        self.val = 0


class Prog:
    ENG = ("pe", "act", "dve", "pool")

    def __init__(self, nc, es):
        self.nc = nc
        self.es = es
        self.eng = {"pe": nc.tensor, "act": nc.scalar, "dve": nc.vector,
                    "pool": nc.gpsimd, "sp": nc.sync}
        self.sem = {e: es.enter_context(nc.semaphore("c_" + e)) for e in self.ENG}
        self.cnt = {e: 0 for e in self.ENG}
        self.waited = {e: {} for e in ("pe", "act", "dve", "pool", "sp")}
        self.semobj = {"c_" + e: self.sem[e] for e in self.ENG}
        self.tab = {}
        self.streams = []
        self._snames = {}
        self.nops = 0

    NDS = 24

    def stream(self, name):
        return None

    def _dma_sem(self):
        if not self.streams:
            for i in range(self.NDS):
                st = Stream(self.es.enter_context(self.nc.semaphore("d_%d" % i)))
                st.name = "d_%d" % i
                self.semobj[st.name] = st.sem
                self.streams.append(st)
            self._dsi = 0
        st = self.streams[self._dsi % self.NDS]
        self._dsi += 1
        return st

    def _entries(self, buf, key):
        t = self.tab.setdefault(buf, {})
        if key is None:
            return list(t.values())
        out = []
        if key in t:
            out.append(t[key])
        if None in t:
            out.append(t[None])
        return out

    def _deps(self, eng, reads, writes):
        deps = []
        for ap, key in reads:
            for ent in self._entries(ap.tensor.name, key):
                if ent[0] is not None:
                    deps.append((ent[0], True))
        for ap, key in writes:
            for ent in self._entries(ap.tensor.name, key):
                if ent[0] is not None:
                    deps.append((ent[0], False))
                for tok in ent[1].values():
                    deps.append((tok, False))
        return deps

    def _record(self, eng, tok, reads, writes):
        for ap, key in reads:
            t = self.tab.setdefault(ap.tensor.name, {})
            ent = t.setdefault(key, [None, {}])
            ent[1][eng] = tok
        for ap, key in writes:
            t = self.tab.setdefault(ap.tensor.name, {})
            if key is None:
                t.clear()
                t[None] = [tok, {}]
            else:
                t[key] = [tok, {}]

    @staticmethod
    def _autokey(ap):
        if not ap.tensor.name.startswith("pbd"):
            return None
        a = ap.ap
        c0 = ap.offset % a[0][0]
        c1 = c0 + sum((cnt - 1) * st for st, cnt in a[1:]) + 1
        if c1 <= 512:
            return 0
        if c0 >= 512:
            return 1
        return None

    def _norm(self, lst):
        out = []
        for x in lst:
            if not isinstance(x, tuple):
                x = (x, None)
            if x[1] is None:
                x = (x[0], self._autokey(x[0]))
            out.append(x)
        return out

    def _do_waits(self, eng, deps):
        need = {}
        for (semname, val, peng), raw in deps:
            if peng == eng:
                if eng not in ("act", "dve", "pool"):
                    continue
            if need.get(semname, 0) < val:
                need[semname] = val
        w = self.waited[eng]
        e = self.eng[eng]
        for semname, val in need.items():
            if w.get(semname, 0) >= val:
                continue
            w[semname] = val
            e.wait_ge(self.semobj[semname], val)

    def op(self, eng, fn, reads=(), writes=()):
        reads = self._norm(reads)
        writes = self._norm(writes)
        deps = self._deps(eng, reads, writes)
        self._do_waits(eng, deps)
        ins = fn(self.eng[eng])
        self.cnt[eng] += 1
        ins.then_inc(self.sem[eng], 1)
        tok = ("c_" + eng, self.cnt[eng], eng)
        self._record(eng, tok, reads, writes)
        self.nops += 1
        return tok

    def dma(self, out, in_, stream=None, rk=None, wk=None, q="sp", **kw):
        reads = [(in_, rk)]
        writes = [(out, wk)]
        st = self._dma_sem()
        deps = self._deps(q, reads, writes)
        if st.val:
            deps.append(((st.name, st.val, "dma"), False))
        self._do_waits(q, deps)
        ins = self.eng[q].dma_start(out=out, in_=in_, **kw)
        st.val += 16
        ins.then_inc(st.sem, 16)
        tok = (st.name, st.val, "dma")
        self._record("dma:" + st.name, tok, reads, writes)
        self.nops += 1
        return tok

    def finish(self):
        sp = self.eng["sp"]
        for s in self.streams:
            if s.val:
                sp.wait_ge(s.sem, s.val)
        for e in self.ENG:
            if self.cnt[e]:
                sp.wait_ge(self.sem[e], self.cnt[e])

    def barrier(self):
        for e in ("pe", "act", "dve", "pool", "sp"):
            h = self.eng[e]
            w = self.waited[e]
            for p in self.ENG:
                if p == e or self.cnt[p] == 0:
                    continue
                nm = "c_" + p
                if w.get(nm, 0) < self.cnt[p]:
                    w[nm] = self.cnt[p]
                    h.wait_ge(self.sem[p], self.cnt[p])
            for s in self.streams:
                if s.val and w.get(s.name, 0) < s.val:
                    w[s.name] = s.val
                    h.wait_ge(s.sem, s.val)
        self.tab.clear()

    def mm(self, out, lhsT, rhs, start=True, stop=True, ko=None, kl=None, kr=None, **kw):
        return self.op("pe", lambda e: e.matmul(out, lhsT, rhs, start=start, stop=stop, **kw),
                       reads=[(lhsT, kl), (rhs, kr)], writes=[(out, ko)])

    def tr(self, out, in_, ident, ko=None, ki=None):
        return self.op("pe", lambda e: e.transpose(out, in_, ident),
                       reads=[(in_, ki), (ident, None)], writes=[(out, ko)])

    def act(self, out, in_, func, bias=None, scale=None, accum_out=None, ko=None, ki=None,
            eng="act", extra_reads=()):
        kw = {}
        reads = [(in_, ki)] + list(extra_reads)
        if bias is not None:
            kw["bias"] = bias
            if not isinstance(bias, (int, float)):
                reads.append((bias, None))
        if scale is not None:
            kw["scale"] = scale
            if not isinstance(scale, (int, float)):
                reads.append((scale, None))
        writes = [(out, ko)]
        if accum_out is not None:
            kw["accum_out"] = accum_out
            writes.append((accum_out, None))
        return self.op("act", lambda e: e.activation(out, in_, func, **kw), reads=reads, writes=writes)

    def tt(self, out, in0, in1, op, eng="dve", ko=None, k0=None, k1=None):
        return self.op(eng, lambda e: e.tensor_tensor(out, in0, in1, op),
                       reads=[(in0, k0), (in1, k1)], writes=[(out, ko)])

    def ts(self, out, in0, s1, s2=None, op0=ALU.mult, op1=None, eng="dve", ko=None, k0=None,
           accum_out=None):
        reads = [(in0, k0)]
        if not isinstance(s1, (int, float)):
            reads.append((s1, None))
        if s2 is not None and not isinstance(s2, (int, float)):
            reads.append((s2, None))
        kw = {}
        if op1 is not None:
            kw["op1"] = op1
        writes = [(out, ko)]
        if accum_out is not None:
            kw["accum_out"] = accum_out
            writes.append((accum_out, None))
        return self.op(eng, lambda e: e.tensor_scalar(out, in0, s1, s2, op0, **kw),
                       reads=reads, writes=writes)

    def stt(self, out, in0, scalar, in1, op0, op1, eng="dve", ko=None, k0=None, k1=None):
        reads = [(in0, k0), (in1, k1)]
        if not isinstance(scalar, (int, float)):
            reads.append((scalar, None))
        return self.op(eng, lambda e: e.scalar_tensor_tensor(out, in0, scalar, in1, op0, op1),
                       reads=reads, writes=[(out, ko)])

    def copy(self, out, in_, eng="dve", ko=None, ki=None):
        if eng == "act":
            return self.act(out, in_, AF.Copy, ko=ko, ki=ki)
        return self.op(eng, lambda e: e.tensor_copy(out, in_), reads=[(in_, ki)], writes=[(out, ko)])

    def memset(self, ap, val, eng="dve", k=None):
        return self.op(eng, lambda e: e.memset(ap, val), writes=[(ap, k)])

    def recip(self, out, in_, ko=None, ki=None):
        return self.op("dve", lambda e: e.reciprocal(out, in_), reads=[(in_, ki)], writes=[(out, ko)])
from concourse.bass_utils import run_bass_kernel_spmd
from contextlib import ExitStack

EPS = 1e-6
D = 1024
KD = 8
TT = 512
HALO = 2
TW = TT + 2 * HALO
QSCALE = 96 ** -0.5


class Cfg:
    def __init__(self, lp_own=4096, nnon=24, ls=2048, ns=4, dff=2816):
        self.lp_own, self.nnon, self.ls, self.ns, self.dff = lp_own, nnon, ls, ns, dff
        self.nfb = dff // 128
        self.ng = dff // 256
        self.lmax = max(lp_own, ls)


def build(cfg):
    nc = bass.Bass("TRN2", target_bir_lowering=False)
    c = cfg
    NFB, NG = c.nfb, c.ng

    def din(name, shape, dt=F32):
        return nc.dram_tensor(name, list(shape), dt, kind="ExternalInput").ap()

    xo = din("xo", [D, max(c.lp_own, TT) + 4])
    xn = din("xn", [max(c.nnon, 1), D, TW])
    mk = din("mk", [128, max(c.nnon, 1) * 2])
    rpo = din("rpo", [2, 32, max(c.lp_own, TT)])
    rpn = din("rpn", [max(c.nnon, 1), 2, 32, TT])
    rs = din("rs", [2, 32, c.ls])
    xs = din("xs", [max(c.ns, 1), D, c.ls + 4])
    w_in = din("w_in", [D, 1968])
    w_q_b = din("w_q_b", [256, 768])
    w_kv_b = din("w_kv_b", [128, 1024])
    w_out = din("w_out", [D, D])
    w_gate = din("w_gate", [D, c.dff])
    w_up = din("w_up", [D, c.dff])
    w_down = din("w_down", [c.dff, D])
    gv = din("gv", [128, 88])
    rowv = din("rowv", [1, 40])
    cm = din("cm", [128, 6, 128])
    yo = nc.dram_tensor("yo", [D, max(c.lp_own, TT)], F32, kind="ExternalOutput").ap()
    ys = nc.dram_tensor("ys", [max(c.ns, 1), D, c.ls], F32, kind="ExternalOutput").ap()
    wg_s = nc.dram_tensor("wg_s", [NG, 128, KD, 256], BF16, kind="ExternalOutput").ap()
    wu_s = nc.dram_tensor("wu_s", [NG, 128, KD, 256], BF16, kind="ExternalOutput").ap()
    wd_s = nc.dram_tensor("wd_s", [8, 128, NFB, 128], BF16, kind="ExternalOutput").ap()

    with ExitStack() as es:
        P = Prog(nc, es)

        uid = {"n": 0}

        def sbuf(stack, name, shape, dt=F32):
            uid["n"] += 1
            return stack.enter_context(nc.sbuf_tensor("%s_%d" % (name, uid["n"]), list(shape), dt))

        pbd = [es.enter_context(nc.psum_tensor(f"pbd{i}", [128, 1024], F32)) for i in range(4)]

        class Bank:
            def __init__(self, t, half):
                self.t, self.h = t, half

            def __getitem__(self, idx):
                if not isinstance(idx, tuple):
                    idx = (idx,)
                cs = idx[1] if len(idx) > 1 else slice(None)
                a = (cs.start or 0) + self.h * 512
                b = (cs.stop if cs.stop is not None else 512) + self.h * 512
                return self.t[idx[0], a:b]

        pb = [Bank(pbd[i // 2], i % 2) for i in range(8)]
        psrr = {"i": 0}

        def nps(lo=0, hi=8):
            k = "%d_%d" % (lo, hi)
            i = psrr.get(k, lo)
            psrr[k] = lo + ((i - lo + 1) % (hi - lo))
            return pb[i]

        G = {}
        gvt = sbuf(es, "gvt", [128, 88])
        rowt = sbuf(es, "rowt", [128, 40])
        cmt = sbuf(es, "cmt", [128, 6, 128])
        mneg = sbuf(es, "mneg", [128, 2, 128], BF16)
        identf = sbuf(es, "identf", [128, 128])
        identb = sbuf(es, "identb", [128, 128], BF16)
        onesb = sbuf(es, "onesb", [128, 128], BF16)
        onesf = sbuf(es, "onesf", [128, 128])
        abt = sbuf(es, "abt", [128, 16])
        sq2 = [sbuf(es, f"sq{i}", [128, TT], BF16) for i in range(3)]
        lnb = sbuf(es, "lnb", [128, TT])
        rr = {"sq": 0}

        st_c = P.stream("const")
        P.dma(gvt[:], gv[:, :], st_c)
        P.dma(rowt[:], rowv[0:1, :].partition_broadcast(128), st_c)
        P.dma(cmt[:], cm[:, :, :], st_c)
        P.copy(mneg[:], cmt[:, 4:6, :], eng="dve")
        P.memset(identf[:], 1.0, eng="pool")
        P.op("pool", lambda e: e.affine_select(identf[:], identf[:], pattern=[[-1, 128]],
                                               compare_op=ALU.is_equal, fill=0.0, base=0,
                                               channel_multiplier=1),
             reads=[identf[:]], writes=[identf[:]])
        P.copy(identb[:], identf[:], eng="dve")
        P.memset(onesb[:], 1.0, eng="dve")
        P.memset(onesf[:], 1.0, eng="dve")
        P.act(abt[:], rowt[:, 16:32], AF.Exp)
        P.ts(abt[:], abt[:], -1.0, None, op0=ALU.mult)
        G_ATTN, G_FFN, G_FIN, G_QA, G_KVA, G_AO, G_SSD, G_CB, G_CW = 0, 8, 16, 24, 26, 27, 31, 35, 43
        dtb = rowt[:, 0:16]
        dsk = rowt[:, 32:40]

        def rstd_fm(srcs, Dn, N, out_ap):
            ps = nps(0, 6)
            for i, s in enumerate(srcs):
                sq = sq2[rr["sq"] % 3]
                rr["sq"] += 1
                P.act(sq[:, :N], s, AF.Square)
                P.mm(ps[:, :N], onesb[:], sq[:, :N], start=(i == 0), stop=(i == len(srcs) - 1))
            P.act(lnb[:, :N], ps[:, :N], AF.Ln, bias=EPS, scale=1.0 / Dn)
            P.act(out_ap, lnb[:, :N], AF.Exp, scale=-0.5)

        stg = [sbuf(es, f"stg{i}", [128, KD, 128]) for i in range(2)]
        st_w = [P.stream("w0"), P.stream("w1")]
        wrr = {"i": 0}

        def prep_w(dst_fn, src, k_chunks, c0, ncols, gcol, scale=1.0):
            for a in range(0, ncols, 128):
                b = min(a + 128, ncols)
                i = wrr["i"] % 2
                wrr["i"] += 1
                P.dma(stg[i][:, 0:k_chunks, 0:b - a],
                      src[0:k_chunks * 128, c0 + a:c0 + b].rearrange("(k p) n -> p k n", p=128), st_w[i])
                P.tt(dst_fn(a, b), stg[i][:, 0:k_chunks, 0:b - a],
                     gvt[:, gcol:gcol + k_chunks].unsqueeze(2).to_broadcast([128, k_chunks, b - a]), ALU.mult)

        with ExitStack() as ph:
            wtmp = [sbuf(ph, f"wtmp{i}", [128, KD, 256], BF16) for i in range(2)]
            wdt = [sbuf(ph, f"wdt{i}", [128, NFB, 128], BF16) for i in range(2)]
            wdf = [sbuf(ph, f"wdf{i}", [128, NFB, 128]) for i in range(2)]
            st_o = [P.stream("wo0"), P.stream("wo1")]
            n = 0
            for (src, dst) in ((w_gate, wg_s), (w_up, wu_s)):
                for g in range(NG):
                    t = wtmp[n % 2]
                    prep_w(lambda a, b, t=t: t[:, :, a:b], src, KD, g * 256, 256, G_FFN)
                    P.dma(dst[g, :, :, :], t[:], st_o[n % 2])
                    n += 1
            for ob in range(8):
                i = ob % 2
                P.dma(wdf[i][:], w_down[:, ob * 128:(ob + 1) * 128].rearrange("(k p) n -> p k n", p=128),
                      st_w[i])
                P.copy(wdt[i][:], wdf[i][:], eng="act" if ob % 2 else "dve")
                P.dma(wd_s[ob, :, :, :], wdt[i][:], st_o[i])
            P.barrier()

        def load_x(dst, src_cols, stream):
            P.dma(dst, src_cols.rearrange("(k p) n -> p k n", p=128), stream)

        def norm_tile(xt, hn, rst, N):
            pieces = [(0, min(N, TT))] + ([(TT, N)] if N > TT else [])
            for (a, b) in pieces:
                rstd_fm([xt[:, kc, a:b] for kc in range(KD)], D, b - a, rst[:, a:b])
            for kc in range(KD):
                P.tt(hn[:, kc, 0:N], xt[:, kc, 0:N], rst[:, 0:N], ALU.mult, ko=kc)

        def mla_phase(own_src, lown, non_src, nnon, rope_own, rope_non):
            nto = lown // TT
            ltot = lown + nnon * TT
            ntt = ltot // TT
            with ExitStack() as ph:
                wqb = sbuf(ph, "wqb", [128, 2, 768], BF16)
                wqr = sbuf(ph, "wqr", [128, 2, 8, 96], BF16)
                wkvb = sbuf(ph, "wkvb", [128, 1024], BF16)
                ckvn = sbuf(ph, "ckvn", [128, ltot], BF16)
                KT = sbuf(ph, "KT", [96, ltot], BF16)
                qlatn = sbuf(ph, "qlatn", [128, 2, lown], BF16)
                rst2 = sbuf(ph, "mrst2", [128, TT])
                rtab = [sbuf(ph, f"rtab{i}", [96, 2, TT]) for i in range(2)]
                t1 = sbuf(ph, "mt1", [96, TT])
                t2 = sbuf(ph, "mt2", [96, TT])
                ph1 = ExitStack()
                w_mla = sbuf(ph1, "w_mla", [128, KD, 576], BF16)
                xt2 = [sbuf(ph1, f"mxt{i}", [128, KD, TT]) for i in range(2)]
                hn = sbuf(ph1, "mhn", [128, KD, TT], BF16)
                rst = sbuf(ph1, "mrst", [128, TT])
                st_x = [P.stream("mx0"), P.stream("mx1")]
                st_r = [P.stream("mr0"), P.stream("mr1")]

                prep_w(lambda a, b: w_mla[:, :, a:b], w_in, KD, 0, 384, G_ATTN)
                P.memset(w_mla[:, :, 384:576], 0.0, eng="pool")
                prep_w(lambda a, b: w_mla[:, :, 448 + a:448 + b], w_in, KD, 384, 32, G_ATTN)
                for kc in range(KD):
                    P.ts(w_mla[:, kc, 544:560], w_mla[:, kc, 464:480], -1.0, None, op0=ALU.mult)
                    P.copy(w_mla[:, kc, 560:576], w_mla[:, kc, 448:464], eng="pool")
                prep_w(lambda a, b: wqb[:, :, a:b], w_q_b, 2, 0, 768, G_QA)
                P.memset(wqr[:], 0.0, eng="pool")
                for kc in range(2):
                    for h in range(8):
                        P.ts(wqr[:, kc, h, 64:80], wqb[:, kc, h * 96 + 80:h * 96 + 96], -1.0, None,
                             op0=ALU.mult)
                        P.copy(wqr[:, kc, h, 80:96], wqb[:, kc, h * 96 + 64:h * 96 + 80], eng="pool")
                prep_w(lambda a, b: wkvb[:, a:b].unsqueeze(1), w_kv_b, 1, 0, 1024, G_KVA)

                def tile_src(t):
                    if t < nto:
                        return own_src[:, HALO + t * TT:HALO + (t + 1) * TT], \
                            rope_own[:, :, t * TT:(t + 1) * TT]
                    s = t - nto
                    return non_src[s, :, HALO:HALO + TT], rope_non[s, :, :, :]

                def issue_load(t):
                    xs_, rp_ = tile_src(t)
                    load_x(xt2[t % 2][:], xs_, st_x[t % 2])
                    P.dma(rtab[t % 2][64:96, :, :], rp_.rearrange("a p n -> p a n"), st_r[t % 2])

                issue_load(0)
                for t in range(ntt):
                    if t + 1 < ntt:
                        issue_load(t + 1)
                    xt = xt2[t % 2]
                    rt = rtab[t % 2]
                    norm_tile(xt, hn, rst, TT)
                    cols = slice(t * TT, (t + 1) * TT)
                    pc = nps(0, 6)
                    for kc in range(KD):
                        P.mm(pc[:], w_mla[:, kc, 256:384], hn[:, kc, :], start=(kc == 0), stop=(kc == KD - 1))
                    rstd_fm([pc[:]], 128, TT, rst2[:])
                    P.tt(ckvn[:, cols], pc[:], rst2[:], ALU.mult, ko=t)
                    pa = nps(0, 6)
                    pr = nps(0, 6)
                    for kc in range(KD):
                        P.mm(pa[0:96, :], w_mla[:, kc, 384:480], hn[:, kc, :], start=(kc == 0), stop=(kc == KD - 1))
                    for kc in range(KD):
                        P.mm(pr[0:96, :], w_mla[:, kc, 480:576], hn[:, kc, :], start=(kc == 0), stop=(kc == KD - 1))
                    P.tt(t1[64:96, :], pa[64:96, :], rt[64:96, 0, :], ALU.mult)
                    P.tt(t2[64:96, :], pr[64:96, :], rt[64:96, 1, :], ALU.mult)
                    P.tt(KT[64:96, cols], t1[64:96, :], t2[64:96, :], ALU.add, ko=("r", t))
                    if t < nto:
                        pq = [nps(0, 6), nps(0, 6)]
                        for cq in range(2):
                            for kc in range(KD):
                                P.mm(pq[cq][:], w_mla[:, kc, cq * 128:(cq + 1) * 128], hn[:, kc, :],
                                     start=(kc == 0), stop=(kc == KD - 1))
                        rstd_fm([pq[0][:], pq[1][:]], 256, TT, rst2[:])
                        for cq in range(2):
                            P.tt(qlatn[:, cq, cols], pq[cq][:], rst2[:], ALU.mult, ko=t)

                P.barrier()
                ph1.close()
                VH = sbuf(ph, "VH", [128, ltot // 128, 65], BF16)
                QH = sbuf(ph, "QH", [96, lown], BF16)
                PT = [sbuf(ph, f"PT{i}", [128, 2 * TT], BF16) for i in range(4)]
                osb2 = [sbuf(ph, f"osb{i}", [64, TT]) for i in range(2)]
                rc2 = [sbuf(ph, f"rc{i}", [65, TT]) for i in range(2)]
                fin = {"f": None, "n": 0}
                pbfin = pb[5]
                P.memset(VH[:, :, 64:65], 1.0, eng="pool")
                nkb = ltot // 128
                for h in range(8):
                    for t in range(ntt):
                        cols = slice(t * TT, (t + 1) * TT)
                        pk = nps(0, 6)
                        P.mm(pk[0:64, :], wkvb[:, h * 128:h * 128 + 64], ckvn[:, cols], kr=t)
                        P.copy(KT[0:64, cols], pk[0:64, :], eng="act" if t % 2 else "dve", ko=("n", t))
                    for g8 in range(0, nkb, 8):
                        pv = nps(0, 6)
                        for j in range(8):
                            kb = g8 + j
                            P.mm(pv[:, j * 64:(j + 1) * 64], ckvn[:, kb * 128:(kb + 1) * 128],
                                 wkvb[:, h * 128 + 64:h * 128 + 128], kl=kb // 4)
                        P.copy(VH[:, g8:g8 + 8, 0:64], pv[:, :].rearrange("p (j d) -> p j d", j=8),
                               eng="dve" if (g8 // 8) % 2 else "act", ko=g8 // 8)
                    for t in range(nto):
                        cols = slice(t * TT, (t + 1) * TT)
                        rt = rtab[t % 2]
                        P.dma(rt[64:96, :, :], rope_own[:, :, cols].rearrange("a p n -> p a n"), st_r[t % 2])
                        pa = nps(0, 6)
                        pr = nps(0, 6)
                        for kc in range(2):
                            P.mm(pa[0:96, :], wqb[:, kc, h * 96:(h + 1) * 96], qlatn[:, kc, cols],
                                 start=(kc == 0), stop=(kc == 1), kr=t)
                        for kc in range(2):
                            P.mm(pr[0:96, :], wqr[:, kc, h, :], qlatn[:, kc, cols],
                                 start=(kc == 0), stop=(kc == 1), kr=t)
                        P.copy(QH[0:64, cols], pa[0:64, :], eng="act", ko=("n", t))
                        P.tt(t1[64:96, :], pa[64:96, :], rt[64:96, 0, :], ALU.mult)
                        P.tt(t2[64:96, :], pr[64:96, :], rt[64:96, 1, :], ALU.mult)
                        P.tt(QH[64:96, cols], t1[64:96, :], t2[64:96, :], ALU.add, ko=("r", t))
                    for t in range(nto):
                        cols = slice(t * TT, (t + 1) * TT)
                        po = pb[6 + (t % 2)]
                        LOOK = 2
                        npair = nkb // 2
                        for idx in range(npair + LOOK):
                            if idx < npair:
                                pd_ = pbd[idx % 3]
                                for j in range(2):
                                    kb = idx * 2 + j
                                    tk = kb // 4
                                    P.op("pe", lambda e, pd_=pd_, kb=kb, j=j, cols=cols: e.matmul(
                                        pd_[:, j * 512:(j + 1) * 512], KT[0:96, kb * 128:(kb + 1) * 128],
                                        QH[0:96, cols], start=True, stop=True),
                                        reads=[(KT[:], ("n", tk)), (KT[:], ("r", tk)), (QH[:], ("n", t)),
                                               (QH[:], ("r", t))],
                                        writes=[(pd_[:, j * 512:(j + 1) * 512], None)])
                                P.act(PT[idx % 4][:], pd_[:, :], AF.Exp, scale=QSCALE)
                            if idx == LOOK and fin["f"] is not None:
                                fin["f"]()
                                fin["f"] = None
                            if idx >= LOOK:
                                pi = idx - LOOK
                                for j in range(2):
                                    kb = pi * 2 + j
                                    P.mm(po[0:65, :], VH[:, kb, :], PT[pi % 4][:, j * 512:(j + 1) * 512],
                                         start=(kb == 0), stop=(kb == nkb - 1), kl=kb // 8)
                        ob_ = osb2[fin["n"] % 2]
                        rc_ = rc2[fin["n"] % 2]
                        fin["n"] += 1
                        P.act(rc_[64:65, :], po[64:65, :], AF.Ln)
                        P.act(rc_[64:65, :], rc_[64:65, :], AF.Exp, scale=-1.0)
                        P.copy(ob_[0:64, :], po[0:64, :], eng="dve")

                        def finalize(ob_=ob_, rc_=rc_, h=h, cols=cols, t=t):
                            pbc = pbfin
                            P.mm(pbc[0:64, :], onesf[64:65, 0:64], rc_[64:65, :])
                            P.tt(G["mlaT"][(h % 2) * 64:(h % 2) * 64 + 64, h // 2, cols], ob_[0:64, :],
                                 pbc[0:64, :], ALU.mult, ko=(h, t))
                        fin["f"] = finalize
                if fin["f"] is not None:
                    fin["f"]()
                    fin["f"] = None
                for t in range(nto):
                    cols = slice(t * TT, (t + 1) * TT)
                    rstd_fm([G["mlaT"][:, cc, cols] for cc in range(4)], 512, TT, rst2[:])
                    P.tt(G["mlaT"][:, :, cols], G["mlaT"][:, :, cols],
                         rst2[:, :].unsqueeze(1).to_broadcast([128, 4, TT]), ALU.mult)
                P.barrier()

        def ssd_phase(own_src, own_t0, nto, slots, out_t0):
            nch = (nto if nto <= 4 else (nto + 1) // 2) * 4
            with ExitStack() as ph:
                w_ssd = sbuf(ph, "w_ssd", [128, KD, 1552], BF16)
                Sst = [sbuf(ph, f"Sst{d}", [128, 512]) for d in range(2)]
                SBW = sbuf(ph, "SBW", [128, nch, 512], BF16)
                xtb = [sbuf(ph, f"sxt{i}", [128, KD, TW]) for i in range(1)]
                dgw = sbuf(ph, "dgw", [128, 8, 5, 128], BF16)
                preb = [sbuf(ph, f"preb{i}", [128, TW], BF16) for i in range(2)]
                hn = sbuf(ph, "shn", [128, KD, TW], BF16)
                rst = sbuf(ph, "srst", [128, TW])
                xact = sbuf(ph, "xact", [128, 8, TT], BF16)
                x_tok = sbuf(ph, "x_tok", [128, 4, 512], BF16)
                B_tok = sbuf(ph, "B_tok", [128, 4, 256], BF16)
                z_tok = sbuf(ph, "z_tok", [128, 4, 512], BF16)
                dt_tok = sbuf(ph, "dt_tok", [128, 4, 16])
                dtm = sbuf(ph, "dtm", [128, 4, 16])
                sm = [sbuf(ph, f"sm{i}", [128, 16]) for i in range(16)]
                xw4 = sbuf(ph, "xw4", [128, 4, 512], BF16)
                dA4 = sbuf(ph, "dA4", [128, 4, 8])
                w4 = sbuf(ph, "w4", [128, 4, 8])
                ex4 = sbuf(ph, "ex4", [128, 64])
                Sfin = sbuf(ph, "Sfin", [128, 4, 512], BF16)
                Ssnap = sbuf(ph, "Ssnap", [128, 512])
                xdt = [sbuf(ph, f"xdt{d}", [128, 512], BF16) for d in range(2)]
                GT = sbuf(ph, "GT", [128, 2, 128], BF16)
                Dm = [sbuf(ph, f"Dm{i}", [128, 512], BF16) for i in range(2)]
                Mm = [sbuf(ph, f"Mm{i}", [128, 512], BF16) for i in range(2)]
                ya = sbuf(ph, "ya", [128, 512])
                yb = sbuf(ph, "yb", [128, 512])
                stok = sbuf(ph, "stok", [128, 512], BF16)
                st_x = P.stream("sx")
                smr = {"i": 0}

                def smn():
                    smr["i"] += 1
                    return sm[smr["i"] % 16]

                prep_w(lambda a, b: w_ssd[:, :, a:b], w_in, KD, 416, 1552, G_ATTN)
                P.memset(Sst[0][:], 0.0)
                P.memset(Sst[1][:], 0.0)
                for cc in range(8):
                    for k in range(5):
                        P.ts(dgw[:, cc, k, :], identf[:], gvt[:, G_CW + cc * 5 + k:G_CW + cc * 5 + k + 1], None,
                             op0=ALU.mult)
                tl = {"i": 0, "q": []}

                def prefetch(src):
                    i = tl["i"]
                    tl["i"] += 1
                    load_x(xtb[0][:], src, st_x)
                    tl["q"].append(xtb[0])

                def tile_front(src, full):
                    xt = tl["q"].pop(0)
                    import os
                    dbg = os.environ.get("SSD_DBG", "z")
                    if dbg == "0":
                        return
                    norm_tile(xt, hn, rst, TW)
                    if tl["next"] is not None:
                        prefetch(tl["next"])
                        tl["next"] = None
                    if dbg == "a":
                        return
                    ncc = 8 if full else 6
                    for cc in range(ncc):
                        pm = nps(0, 6)
                        ph_ = nps(0, 6)
                        col = 512 + cc * 128
                        for kc in range(KD):
                            P.mm(pm[:], w_ssd[:, kc, col:col + 128], hn[:, kc, 0:TT],
                                 start=(kc == 0), stop=(kc == KD - 1))
                        for kc in range(KD):
                            P.mm(ph_[:, 0:4], w_ssd[:, kc, col:col + 128], hn[:, kc, TT:TW],
                                 start=(kc == 0), stop=(kc == KD - 1))
                        pr_ = preb[cc % 2]
                        P.copy(pr_[:, 0:TT], pm[:], eng="act", ko="m")
                        P.copy(pr_[:, TT:TW], ph_[:, 0:4], eng="dve", ko="h")
                        pcv = nps(0, 6)
                        for k in range(5):
                            P.mm(pcv[:], dgw[:, cc, k, :], pr_[:, k:k + TT], start=(k == 0), stop=(k == 4))
                        P.act(xact[:, cc, :], pcv[:], AF.Silu, bias=gvt[:, G_CB + cc:G_CB + cc + 1])
                    if dbg == "b":
                        return
                    for tb in range(4):
                        pd = nps(0, 6)
                        for kc in range(KD):
                            P.mm(pd[:, 0:16], hn[:, kc, HALO + tb * 128:HALO + (tb + 1) * 128],
                                 w_ssd[:, kc, 1536:1552], start=(kc == 0), stop=(kc == KD - 1))
                        v = smn()
                        P.tt(v[:], pd[:, 0:16], dtb, ALU.add)
                        a_ = smn()
                        P.act(a_[:], v[:], AF.Abs)
                        e_ = smn()
                        P.act(e_[:], a_[:], AF.Exp, scale=-1.0)
                        l_ = smn()
                        P.act(l_[:], e_[:], AF.Ln, bias=1.0)
                        P.ts(v[:], v[:], 0.0, None, op0=ALU.max)
                        P.tt(dt_tok[:, tb, :], v[:], l_[:], ALU.add, ko=tb)
                    if full:
                        for tb in range(4):
                            pz = nps(0, 6)
                            for kc in range(KD):
                                P.mm(pz[:], hn[:, kc, HALO + tb * 128:HALO + (tb + 1) * 128],
                                     w_ssd[:, kc, 0:512], start=(kc == 0), stop=(kc == KD - 1))
                            P.act(z_tok[:, tb, :], pz[:], AF.Silu, ko=tb)
                    if dbg == "c":
                        return
                    for tb in range(4):
                        pt_ = nps(0, 6)
                        pt2 = nps(0, 6)
                        for cc in range(4):
                            P.mm(pt_[:, cc * 128:(cc + 1) * 128], xact[:, cc, tb * 128:(tb + 1) * 128], identb[:])
                        for cc in range(2):
                            P.mm(pt2[:, cc * 128:(cc + 1) * 128], xact[:, 4 + cc, tb * 128:(tb + 1) * 128],
                                 identb[:])
                        P.copy(x_tok[:, tb, :], pt_[:, 0:512], eng="dve", ko=tb)
                        P.copy(B_tok[:, tb, :], pt2[:, 0:256], eng="act", ko=tb)

                def bc8(ap8):
                    return ap8.unsqueeze(2).to_broadcast([128, 8, 64])

                def v8(ap512):
                    return ap512.rearrange("p (h d) -> p h d", h=8)

                def su_batch(d, dsrc, order, save=None):
                    dv = dsrc[:, :, d * 8:(d + 1) * 8]
                    P.tt(dA4[:], dv, abt[:, d * 8:(d + 1) * 8].unsqueeze(1).to_broadcast([128, 4, 8]), ALU.mult)
                    pp = nps(0, 6)
                    dflat = dA4[:, :, :].rearrange("p c h -> p (c h)")
                    P.mm(pp[:, 0:32], cmt[:, 2 + d, :], dflat)
                    P.mm(pp[:, 32:64], onesf[:], dflat)
                    P.act(ex4[:], pp[:, 0:64], AF.Exp)
                    P.tt(w4[:], dv, ex4[:, 0:32].rearrange("p (c h) -> p c h", c=4), ALU.mult)
                    P.tt(xw4[:, :, :].rearrange("p c (h e) -> p c h e", h=8),
                         x_tok[:, :, :].rearrange("p c (h e) -> p c h e", h=8),
                         w4[:, :, :].unsqueeze(3).to_broadcast([128, 4, 8, 64]), ALU.mult)
                    S = Sst[d]
                    for tb in order:
                        pc = nps(0, 6)
                        for g in range(2):
                            P.mm(pc[:, g * 256:(g + 1) * 256], B_tok[:, tb, g * 128:(g + 1) * 128],
                                 xw4[:, tb, g * 256:(g + 1) * 256])
                        if save is not None:
                            P.copy(save(tb), S[:], eng="act")
                        P.tt(v8(S[:]), v8(S[:]), bc8(ex4[:, 32 + tb * 8:32 + tb * 8 + 8]), ALU.mult)
                        P.tt(S[:], S[:], pc[:], ALU.add)

                def full_chunk(tb, ci, out_cols):
                    pg = nps(0, 6)
                    for g in range(2):
                        P.mm(pg[:, g * 128:(g + 1) * 128], xact[:, 4 + g, tb * 128:(tb + 1) * 128],
                             xact[:, 6 + g, tb * 128:(tb + 1) * 128])
                    P.copy(GT[:], pg[:, 0:256].rearrange("p (g l) -> p g l", g=2), eng="act")
                    prel = []
                    for d in range(2):
                        dt8 = dt_tok[:, tb, d * 8:(d + 1) * 8]
                        dtA = smn()
                        P.tt(dtA[:, 0:8], dt8, abt[:, d * 8:(d + 1) * 8], ALU.mult, k0=tb)
                        pcs = nps(0, 6)
                        P.mm(pcs[:, 0:8], cmt[:, d, :], dtA[:, 0:8])
                        et = smn()
                        P.act(et[:, 0:8], pcs[:, 0:8], AF.Exp)
                        ncs = smn()
                        P.ts(ncs[:, 0:8], pcs[:, 0:8], -1.0, None, op0=ALU.mult)
                        P.tt(v8(xdt[d][:]), v8(x_tok[:, tb, :]), bc8(dt8), ALU.mult, k0=tb, k1=tb)
                        poff = nps(0, 6)
                        for g in range(2):
                            rhs = Sfin[:, tb, g * 256:(g + 1) * 256] if d == 0 else SBW[:, ci, g * 256:(g + 1) * 256]
                            P.mm(poff[:, g * 256:(g + 1) * 256], xact[:, 6 + g, tb * 128:(tb + 1) * 128], rhs,
                                 kr=(None if d == 0 else ci))
                        tgt = ya if d == 0 else yb
                        P.tt(v8(tgt[:]), v8(poff[:]), bc8(et[:, 0:8]), ALU.mult)
                        prel.append((dtA, ncs))
                    pdgs = [pb[6], pb[7]]
                    groups = [(d, g) for d in range(2) for g in range(2)]

                    def stageA(k):
                        d, g = groups[k]
                        dtA, ncs = prel[d]
                        pcb = nps(0, 6)
                        for j in range(4):
                            h = g * 4 + j
                            P.mm(pcb[:, j * 128:(j + 1) * 128], dtA[:, h:h + 1].to_broadcast([128, 128]),
                                 cmt[:, d, :], start=True, stop=False)
                            P.mm(pcb[:, j * 128:(j + 1) * 128], identb[:], mneg[:, d, :], start=False, stop=True)
                        Dq = Dm[k % 2]
                        for j in range(4):
                            h = g * 4 + j
                            P.act(Dq[:, j * 128:(j + 1) * 128], pcb[:, j * 128:(j + 1) * 128], AF.Exp,
                                  bias=ncs[:, h:h + 1])
                        P.tt(Mm[k % 2][:, :].rearrange("p (j l) -> p j l", j=4),
                             Dq[:, :].rearrange("p (j l) -> p j l", j=4),
                             GT[:, g, :].unsqueeze(1).to_broadcast([128, 4, 128]), ALU.mult)

                    def stageB(k):
                        d, g = groups[k]
                        for j in range(4):
                            h = g * 4 + j
                            P.mm(pdgs[d][:, h * 64:(h + 1) * 64], Mm[k % 2][:, j * 128:(j + 1) * 128],
                                 xdt[d][:, h * 64:(h + 1) * 64])

                    stageA(0)
                    stageA(1)
                    stageB(0)
                    stageA(2)
                    stageB(1)
                    stageA(3)
                    stageB(2)
                    stageB(3)
                    P.tt(ya[:], ya[:], pdgs[0][:], ALU.add)
                    P.tt(yb[:], yb[:], pdgs[1][:], ALU.add)
                    P.tt(ya[:], ya[:], yb[:], ALU.add)
                    P.tt(v8(yb[:]), v8(x_tok[:, tb, :]), bc8(dsk), ALU.mult, k0=tb)
                    P.tt(ya[:], ya[:], yb[:], ALU.add)
                    yg = yb
                    P.tt(yg[:], ya[:], z_tok[:, tb, :], ALU.mult, k1=tb)
                    ss = smn()
                    P.memset(ss[:, 0:1], 0.0)
                    P.act(stok[:], yg[:], AF.Square, accum_out=ss[:, 0:1])
                    l2 = smn()
                    P.act(l2[:, 0:1], ss[:, 0:1], AF.Ln, bias=EPS, scale=1.0 / 512)
                    r2 = smn()
                    P.act(r2[:, 0:1], l2[:, 0:1], AF.Exp, scale=-0.5)
                    P.ts(stok[:], yg[:], r2[:, 0:1], None, op0=ALU.mult)
                    pt_ = nps(0, 6)
                    for cc in range(4):
                        P.mm(pt_[:, cc * 128:(cc + 1) * 128], stok[:, cc * 128:(cc + 1) * 128], identb[:])
                    P.copy(G["ssdT"][:, :, out_cols], pt_[:, 0:512].rearrange("p (c n) -> p c n", c=4), eng="dve",
                           ko=ci)

                sstop = getattr(c, "stop", 0)
                def own_v(kind, t):
                    g0 = (own_t0 + t) * TT
                    return (kind, own_src[:, g0:g0 + TW], t)
                visits = [("slot", src, midx) for (src, midx, fon, bon) in slots]
                if nto <= 4:
                    parts = [(0, nto)]
                else:
                    parts = [(0, nto // 2), (nto // 2, nto)]
                for pi_, (ta, tb_) in enumerate(parts):
                    if len(parts) == 2 and pi_ == 0:
                        visits.append(("snap", None, None))
                        for t in range(nto - 1, tb_ - 1, -1):
                            visits.append(own_v("bwdns", t))
                    if len(parts) == 2 and pi_ == 1:
                        visits.append(("restore", None, None))
                    for t in range(tb_ - 1, ta - 1, -1):
                        visits.append(own_v("bwd", t))
                    for t in range(ta, tb_):
                        visits.append(own_v("full", t))
                tiles = [v for v in visits if v[1] is not None]
                prefetch(tiles[0][1])
                ti = 0
                for v in visits:
                    if v[0] == "snap":
                        P.copy(Ssnap[:], Sst[1][:], eng="dve")
                        continue
                    if v[0] == "restore":
                        P.copy(Sst[1][:], Ssnap[:], eng="dve")
                        continue
                    ti += 1
                    tl["next"] = tiles[ti][1] if ti < len(tiles) else None
                    if v[0] == "slot":
                        midx = v[2]
                        tile_front(v[1], False)
                        P.tt(dtm[:, :, :].rearrange("p c (d h) -> p c d h", d=2),
                             dt_tok[:, :, :].rearrange("p c (d h) -> p c d h", d=2),
                             mkt[:, midx * 2:midx * 2 + 2].unsqueeze(1).unsqueeze(3).to_broadcast([128, 4, 2, 8]),
                             ALU.mult)
                        su_batch(0, dtm, (0, 1, 2, 3))
                        su_batch(1, dtm, (3, 2, 1, 0))
                    elif v[0] == "bwdns":
                        tile_front(v[1], False)
                        su_batch(1, dt_tok, (3, 2, 1, 0))
                    elif v[0] == "bwd":
                        t = v[2]
                        tile_front(v[1], False)
                        lt = t - (parts[-1][0] if t >= parts[-1][0] and len(parts) == 2 else 0)
                        su_batch(1, dt_tok, (3, 2, 1, 0), save=lambda tb, lt=lt: SBW[:, lt * 4 + tb, :])
                    else:
                        t = v[2]
                        tile_front(v[1], True)
                        lt = t - (parts[-1][0] if t >= parts[-1][0] and len(parts) == 2 else 0)
                        su_batch(0, dt_tok, (0, 1, 2, 3), save=lambda tb: Sfin[:, tb, :])
                        for tb in range(4):
                            oc = (out_t0 + t) * TT + tb * 128
                            full_chunk(tb, lt * 4 + tb, slice(oc, oc + 128))
                P.barrier()

        def ffn_phase(own_src, lown, out_dst):
            nto = lown // TT
            with ExitStack() as ph:
                wout = sbuf(ph, "wout", [128, KD, D], BF16)
                xt2 = [sbuf(ph, f"fxt{i}", [128, KD, TT]) for i in range(2)]
                h2 = sbuf(ph, "h2", [128, KD, TT], BF16)
                rst = sbuf(ph, "frst", [128, TT])
                actT = sbuf(ph, "actT", [128, NFB, TT], BF16)
                sg = [sbuf(ph, f"sg{i}", [128, TT]) for i in range(2)]
                wg = [sbuf(ph, f"wg{i}", [128, KD, 256], BF16) for i in range(2)]
                wu = [sbuf(ph, f"wu{i}", [128, KD, 256], BF16) for i in range(2)]
                wd = [sbuf(ph, f"wd{i}", [128, NFB, 128], BF16) for i in range(2)]
                st_x = [P.stream("fx0"), P.stream("fx1")]
                st_g = [P.stream("fg0"), P.stream("fg1")]
                st_d = [P.stream("fd0"), P.stream("fd1")]
                st_y = [P.stream("fy0"), P.stream("fy1")]
                prep_w(lambda a, b: wout[:, 0:4, a:b], w_out, 4, 0, D, G_AO)
                prep_w(lambda a, b: wout[:, 4:8, a:b], w_out[512:1024, :], 4, 0, D, G_SSD)
                load_x(xt2[0][:], own_src[:, HALO:HALO + TT], st_x[0])

                def issue_g(g):
                    P.dma(wg[g % 2][:], wg_s[g, :, :, :], st_g[g % 2])
                    P.dma(wu[g % 2][:], wu_s[g, :, :, :], st_g[g % 2])

                def issue_d(ob):
                    P.dma(wd[ob % 2][:], wd_s[ob, :, :, :], st_d[ob % 2])

                issue_g(0)
                for t in range(nto):
                    if t + 1 < nto:
                        load_x(xt2[(t + 1) % 2][:], own_src[:, HALO + (t + 1) * TT:HALO + (t + 2) * TT],
                               st_x[(t + 1) % 2])
                    xt = xt2[t % 2]
                    cols = slice(t * TT, (t + 1) * TT)
                    for ob in range(8):
                        p_ = nps(0, 8)
                        for kc in range(8):
                            rhs = G["mlaT"][:, kc, cols] if kc < 4 else G["ssdT"][:, kc - 4, cols]
                            P.mm(p_[:], wout[:, kc, ob * 128:(ob + 1) * 128], rhs, start=(kc == 0), stop=(kc == 7))
                        P.tt(xt[:, ob, :], p_[:], xt[:, ob, :], ALU.add)
                    norm_tile(xt, h2, rst, TT)
                    for g in range(NG):
                        i = g % 2
                        if g + 1 < NG:
                            issue_g(g + 1)
                        if g == NG - 1:
                            issue_d(0)
                        for j in range(2):
                            fb = g * 2 + j
                            pgt = nps(0, 8)
                            put = nps(0, 8)
                            for kc in range(KD):
                                P.mm(pgt[:], wg[i][:, kc, j * 128:(j + 1) * 128], h2[:, kc, :],
                                     start=(kc == 0), stop=(kc == KD - 1))
                            for kc in range(KD):
                                P.mm(put[:], wu[i][:, kc, j * 128:(j + 1) * 128], h2[:, kc, :],
                                     start=(kc == 0), stop=(kc == KD - 1))
                            s_ = sg[fb % 2]
                            P.act(s_[:], pgt[:], AF.Silu)
                            P.tt(actT[:, fb, :], s_[:], put[:], ALU.mult, ko=fb)
                    if t + 1 < nto:
                        issue_g(0)
                    for ob in range(8):
                        i = ob % 2
                        if ob + 1 < 8:
                            issue_d(ob + 1)
                        p_ = nps(0, 8)
                        for kc in range(NFB):
                            P.mm(p_[:], wd[i][:, kc, :], actT[:, kc, :], start=(kc == 0), stop=(kc == NFB - 1),
                                 kr=kc)
                        P.tt(xt[:, ob, :], p_[:], xt[:, ob, :], ALU.add)
                    rstd_fm([xt[:, kc, :] for kc in range(KD)], D, TT, rst[:])
                    for ob in range(8):
                        P.stt(xt[:, ob, :], xt[:, ob, :], gvt[:, G_FIN + ob:G_FIN + ob + 1], rst[:],
                              ALU.mult, ALU.mult)
                    P.dma(out_dst[:, cols].rearrange("(k p) n -> p k n", p=128), xt[:], st_y[t % 2])
                P.barrier()

        mkt = sbuf(es, "mkt", [128, max(c.nnon, 1) * 2])
        P.dma(mkt[:], mk[:, :], st_c)
        jobs = []
        if c.lp_own:
            jobs.append(dict(src=xo, lown=c.lp_own, non=xn, nnon=c.nnon, rope=rpo, rnon=rpn, out=yo))
        for s in range(c.ns):
            jobs.append(dict(src=xs[s], lown=c.ls, non=None, nnon=0, rope=rs, rnon=None, out=ys[s]))
        stop = getattr(c, "stop", 0)
        for jb in jobs:
          with ExitStack() as js:
            if stop == 1:
                break
            G["mlaT"] = sbuf(js, "mlaT", [128, 4, jb["lown"]], BF16)
            mla_phase(jb["src"], jb["lown"], jb["non"], jb["nnon"], jb["rope"], jb["rnon"])
            if stop == 2:
                break
            G["ssdT"] = sbuf(js, "ssdT", [128, 4, jb["lown"]], BF16)
            nto = jb["lown"] // TT
            xslots = [(jb["non"][s, :, :], s, True, True) for s in range(jb["nnon"])]
            ssd_phase(jb["src"], 0, nto, xslots, 0)
            if stop >= 3:
                break
            ffn_phase(jb["src"], jb["lown"], jb["out"])
        P.finish()
    return nc
def _rope_tab(pos):
    inv = (10000.0 ** (-np.arange(0, 32, 2, dtype=np.float32) / 32)).astype(np.float32)
    ang = pos.astype(np.float32)[:, None] * inv[None, :]
    co = np.cos(ang).astype(np.float32).T
    si = np.sin(ang).astype(np.float32).T
    return np.stack([np.concatenate([co, co], 0), np.concatenate([si, si], 0)], 0)


def _consts():
    j = np.arange(128)[:, None]
    l = np.arange(128)[None, :]
    cm = np.zeros((128, 6, 128), np.float32)
    cm[:, 0] = (j <= l)
    cm[:, 1] = (j >= l)
    cm[:, 2] = (j > l)
    cm[:, 3] = (j < l)
    cm[:, 4] = np.where(j <= l, 0.0, -30000.0)
    cm[:, 5] = np.where(j >= l, 0.0, -30000.0)
    return cm


def make_in_maps(inp, cfg, ncores, cores_per_seq):
    f = lambda k: np.asarray(inp[k], np.float32)
    xp = f("x_prompt")
    xsm = f("x_sample")
    gv = np.zeros((128, 88), np.float32)
    def put(col, vec):
        v = vec.reshape(-1, 128).T
        gv[:, col:col + v.shape[1]] = v
    put(0, f("attn_norm_g")[0]); put(8, f("ffn_norm_g")[0]); put(16, f("final_norm_g"))
    put(24, f("q_a_norm_g")[0]); put(26, f("kv_a_norm_g")[0]); put(27, f("attn_out_norm_g")[0])
    put(31, f("ssd_norm_g")[0]); put(35, f("conv_b")[0])
    cw = f("conv_w")[0]
    for cc in range(8):
        for k in range(5):
            gv[:, 43 + cc * 5 + k] = cw[k, cc * 128:(cc + 1) * 128]
    rowv = np.concatenate([f("dt_bias")[0].reshape(-1), f("a_log")[0].reshape(-1), f("d_skip")[0].reshape(-1)])[None, :]
    cm = _consts()
    LP = xp.shape[1]
    own = cfg.lp_own
    maps = []
    rs = _rope_tab(np.arange(cfg.ls))
    for c in range(ncores):
        b, q = divmod(c, cores_per_seq)
        xT = np.zeros((D, LP + 4), np.float32)
        xT[:, 2:LP + 2] = xp[b].T
        o0 = q * own
        xo = np.ascontiguousarray(xT[:, o0:o0 + own + 4])
        ntl = LP // TT
        ot0, ot1 = o0 // TT, (o0 + own) // TT
        order = list(range(0, ot0)) + list(range(ntl - 1, ot1 - 1, -1))
        nn = max(len(order), 1)
        xn = np.zeros((nn, D, TW), np.float32)
        mk = np.zeros((128, nn * 2), np.float32)
        rpn = np.zeros((nn, 2, 32, TT), np.float32)
        for s, t in enumerate(order):
            xn[s] = xT[:, t * TT:t * TT + TW]
            mk[:, 2 * s] = 1.0 if t < ot0 else 0.0
            mk[:, 2 * s + 1] = 1.0 if t >= ot1 else 0.0
            rpn[s] = _rope_tab(np.arange(t * TT, (t + 1) * TT))
        rpo = _rope_tab(np.arange(o0, o0 + own))
        xsp = np.zeros((cfg.ns, D, cfg.ls + 4), np.float32)
        for i in range(cfg.ns):
            xsp[i, :, 2:cfg.ls + 2] = xsm[c * cfg.ns + i].T
        maps.append(dict(xo=xo, xn=xn, mk=mk, rpo=rpo, rpn=rpn, rs=rs, xs=xsp,
                         w_in=f("w_in")[0], w_q_b=f("w_q_b")[0], w_kv_b=f("w_kv_b")[0], w_out=f("w_out")[0],
                         w_gate=f("w_gate")[0], w_up=f("w_up")[0], w_down=f("w_down")[0],
                         gv=gv, rowv=rowv.astype(np.float32), cm=cm))
    return maps


def run(inp, cfg, ncores, cores_per_seq):
    nc = build(cfg)
    maps = make_in_maps(inp, cfg, ncores, cores_per_seq)
    res = run_bass_kernel_spmd(nc, maps, core_ids=list(range(ncores)))
    xp = np.asarray(inp["x_prompt"]); xsm = np.asarray(inp["x_sample"])
    yp = np.zeros(xp.shape, np.float32)
    ysm = np.zeros(xsm.shape, np.float32)
    for c in range(ncores):
        r = res.results[c]
        b, q = divmod(c, cores_per_seq)
        yp[b, q * cfg.lp_own:(q + 1) * cfg.lp_own, :] = r["yo"].T
        for i in range(cfg.ns):
            ysm[c * cfg.ns + i] = r["ys"][i].T
    return yp, ysm


def kernel(**inputs):
    cfg = Cfg()
    return run(inputs, cfg, 8, 4)
```
